# Optimizing a Trainium2 kernel written in Bass

```python
import math
import jax, jax.numpy as jnp
from jax import lax
import numpy as np

D_MODEL = 2048
BATCH = 4
SEQ = 2048
DEPTH = 4
DEC_BATCH = 32
DEC_SEQ = 8
PAST_LEN = 16384
PAGE_SIZE = 128

N_META = 16
D_MIX = D_MODEL
D_ATTN = D_MIX // 2
D_REC = D_MIX - D_ATTN
HEAD_DIM = 64
N_HEADS = D_ATTN // HEAD_DIM
N_KV = 4
GROUP = N_HEADS // N_KV
KV_DIM = N_KV * HEAD_DIM
WINDOW = 128
ATT_BLOCK = 128
N_BUCKETS = 32
MAX_DISTANCE = 128
REC_HEAD_DIM = 128
N_REC_HEADS = D_REC // REC_HEAD_DIM
REC_CHUNK = 64
EPS = 1e-6
D_IN = 2 * D_ATTN + 2 * KV_DIM + 4 * D_REC
SPLITS = (D_ATTN, D_ATTN + KV_DIM, D_ATTN + 2 * KV_DIM, 2 * D_ATTN + 2 * KV_DIM,
          2 * D_ATTN + 2 * KV_DIM + D_REC, 2 * D_ATTN + 2 * KV_DIM + 2 * D_REC,
          2 * D_ATTN + 2 * KV_DIM + 3 * D_REC)

kernel_name = 'hymba_swa_sink_hgrn2_decoder_step'


def rmsnorm(x, w):
    xf = x.astype(jnp.float32)
    y = xf * lax.rsqrt(jnp.mean(xf * xf, axis=-1, keepdims=True) + EPS)
    return y.astype(x.dtype) * w


def t5_bucket(dist):
    max_exact = N_BUCKETS // 2
    d = jnp.maximum(dist, 0)
    df = jnp.maximum(d, 1).astype(jnp.float32)
    large = max_exact + (jnp.log(df / max_exact) / math.log(MAX_DISTANCE / max_exact)
                         * (N_BUCKETS - max_exact)).astype(jnp.int32)
    large = jnp.minimum(large, N_BUCKETS - 1)
    return jnp.where(d < max_exact, d, large)


def sink_softmax(s, sink):
    sk = sink.astype(jnp.float32)[:, :, None, None]
    m = jnp.maximum(jnp.max(s, axis=-1, keepdims=True), sk)
    e = jnp.exp(s - m)
    return e / (jnp.sum(e, axis=-1, keepdims=True) + jnp.exp(sk - m))


def project(h, norm_w, w_in, lb):
    B, T = h.shape[0], h.shape[1]
    z = rmsnorm(h, norm_w) @ w_in
    qa, ka, va, ga, qr, fr, ir, gr = jnp.split(z, SPLITS, axis=-1)
    qa = qa.reshape(B, T, N_HEADS, HEAD_DIM)
    ka = ka.reshape(B, T, N_KV, HEAD_DIM)
    va = va.reshape(B, T, N_KV, HEAD_DIM)
    fr32 = fr.astype(jnp.float32)
    lb32 = lb.astype(jnp.float32)
    logf = jnp.log(lb32 + (1.0 - lb32) * jax.nn.sigmoid(fr32)).reshape(B, T, N_REC_HEADS, REC_HEAD_DIM)
    kr = ((1.0 - lb32) * jax.nn.sigmoid(-fr32)).reshape(B, T, N_REC_HEADS, REC_HEAD_DIM)
    qr = jax.nn.silu(qr).reshape(B, T, N_REC_HEADS, REC_HEAD_DIM)
    ir = ir.reshape(B, T, N_REC_HEADS, REC_HEAD_DIM)
    return qa, ka, va, ga, qr, logf, kr, ir, gr


def swa_prompt(q, k, v, sinks, bias_d):
    B, L = q.shape[0], q.shape[1]
    pad = (-L) % ATT_BLOCK
    padf = lambda a: jnp.pad(a, ((0, 0), (pad, 0), (0, 0), (0, 0)))
    nb = (L + pad) // ATT_BLOCK
    qb = padf(q).reshape(B, nb, ATT_BLOCK, N_KV, GROUP, HEAD_DIM)
    kb = padf(k).reshape(B, nb, ATT_BLOCK, N_KV, HEAD_DIM)
    vb = padf(v).reshape(B, nb, ATT_BLOCK, N_KV, HEAD_DIM)
    prev = lambda a: jnp.concatenate([jnp.zeros_like(a[:, :1]), a[:, :-1]], axis=1)
    kk = jnp.concatenate([prev(kb), kb], axis=2)
    vv = jnp.concatenate([prev(vb), vb], axis=2)
    s = jnp.einsum('bnqkgd,bnskd->bnkgqs', qb, kk,
                   preferred_element_type=jnp.float32) * (HEAD_DIM ** -0.5)
    qi = jnp.arange(ATT_BLOCK)
    sj = jnp.arange(2 * ATT_BLOCK)
    dist = ATT_BLOCK + qi[:, None] - sj[None, :]
    kpos = (jnp.arange(nb)[:, None] - 1) * ATT_BLOCK + sj[None, :] - pad
    valid = ((dist >= 0) & (dist < WINDOW))[None] & (kpos >= 0)[:, None, :]
    bias = bias_d[:, jnp.clip(dist, 0, WINDOW - 1)].reshape(N_KV, GROUP, ATT_BLOCK, 2 * ATT_BLOCK)
    s = jnp.where(valid[None, :, None, None], s + bias, -jnp.inf)
    p = sink_softmax(s, sinks.reshape(N_KV, GROUP))
    o = jnp.einsum('bnkgqs,bnskd->bnqkgd', p.astype(v.dtype), vv)
    return o.reshape(B, nb * ATT_BLOCK, N_HEADS, HEAD_DIM)[:, pad:]


def swa_sample(q, k_new, v_new, k_buf, v_buf, sinks, bias_d):
    Bd, T = q.shape[0], q.shape[1]
    W = k_buf.shape[1]
    kk = jnp.concatenate([k_buf.astype(k_new.dtype), k_new], axis=1)
    vv = jnp.concatenate([v_buf.astype(v_new.dtype), v_new], axis=1)
    qg = q.reshape(Bd, T, N_KV, GROUP, HEAD_DIM)
    s = jnp.einsum('btkgd,bskd->bkgts', qg, kk,
                   preferred_element_type=jnp.float32) * (HEAD_DIM ** -0.5)
    kpos = jnp.arange(W + T) - W
    dist = jnp.arange(T)[:, None] - kpos[None, :]
    valid = (dist >= 0) & (dist < WINDOW)
    bias = bias_d[:, jnp.clip(dist, 0, WINDOW - 1)].reshape(N_KV, GROUP, T, W + T)
    s = jnp.where(valid, s + bias, -jnp.inf)
    p = sink_softmax(s, sinks.reshape(N_KV, GROUP))
    o = jnp.einsum('bkgts,bskd->btkgd', p.astype(v_new.dtype), vv).reshape(Bd, T, N_HEADS, HEAD_DIM)
    return o, kk[:, -W:], vv[:, -W:]


def hgrn2_chunked(q, logf, k, v, s0):
    B, T = q.shape[0], q.shape[1]
    C = min(REC_CHUNK, T)
    pad = (-T) % C
    padf = lambda a: jnp.pad(a.astype(jnp.float32), ((0, 0), (pad, 0), (0, 0), (0, 0)))
    nc = (T + pad) // C
    to_chunks = lambda a: padf(a).reshape(B, nc, C, N_REC_HEADS, a.shape[-1]).transpose(1, 0, 3, 2, 4)
    qs, gs, ks, vs = to_chunks(q), to_chunks(logf), to_chunks(k), to_chunks(v)
    tril = jnp.tril(jnp.ones((C, C), dtype=bool))

    def step(S, inp):
        qc, gc, kc, vc = inp
        A = jnp.cumsum(gc, axis=2)
        A_last = A[:, :, -1:]
        diff = A[:, :, :, None, :] - A[:, :, None, :, :]
        decay = jnp.exp(jnp.where(tril[:, :, None], diff, -jnp.inf))
        attn = jnp.einsum('bhtd,bhtsd,bhsd->bhts', qc, decay, kc)
        o = jnp.einsum('bhts,bhsv->bhtv', attn, vc) + jnp.einsum('bhtd,bhdv->bhtv', qc * jnp.exp(A), S)
        S_new = jnp.exp(A_last)[:, :, 0, :, None] * S + jnp.einsum('bhsd,bhsv->bhdv', kc * jnp.exp(A_last - A), vc)
        return S_new, o

    s_fin, o = lax.scan(step, s0.astype(jnp.float32), (qs, gs, ks, vs))
    o = o.transpose(1, 0, 3, 2, 4).reshape(B, nc * C, N_REC_HEADS, REC_HEAD_DIM)[:, pad:]
    return o.astype(v.dtype), s_fin


def mix_out(h, oa, ga, orec, gr, rec_norm_w, w_out):
    B, T = h.shape[0], h.shape[1]
    ya = oa.reshape(B, T, D_ATTN) * jax.nn.silu(ga)
    of = orec.astype(jnp.float32)
    on = (of * lax.rsqrt(jnp.mean(of * of, axis=-1, keepdims=True) + EPS)).astype(h.dtype)
    yr = (on * rec_norm_w.reshape(N_REC_HEADS, REC_HEAD_DIM)).reshape(B, T, D_REC) * jax.nn.silu(gr)
    return h + jnp.concatenate([ya, yr], axis=-1) @ w_out


def setup_inputs(seed: int = 0) -> dict:
    key = jax.random.key(seed)
    ks = jax.random.split(key, 14)
    nrm = jax.random.normal
    win = min(WINDOW, PAST_LEN)
    return {
        'x_prompt': nrm(ks[0], (BATCH, SEQ, D_MODEL), jnp.float32),
        'x_sample': nrm(ks[1], (DEC_BATCH, DEC_SEQ, D_MODEL), jnp.float32),
        'cache_k': nrm(ks[2], (DEPTH, DEC_BATCH, win, N_KV, HEAD_DIM), jnp.float32),
        'cache_v': nrm(ks[3], (DEPTH, DEC_BATCH, win, N_KV, HEAD_DIM), jnp.float32),
        'state_h': 0.5 * nrm(ks[4], (DEPTH, DEC_BATCH, N_REC_HEADS, REC_HEAD_DIM, REC_HEAD_DIM), jnp.float32),
        'meta_tokens': nrm(ks[5], (N_META, D_MODEL), jnp.float32),
        'w_in': nrm(ks[6], (DEPTH, D_MODEL, D_IN), jnp.float32) * D_MODEL ** -0.5,
        'w_out': nrm(ks[7], (DEPTH, D_MIX, D_MODEL), jnp.float32) * D_MIX ** -0.5,
        'norm_w': 1.0 + 0.01 * nrm(ks[8], (DEPTH, D_MODEL), jnp.float32),
        'final_norm_w': 1.0 + 0.01 * nrm(ks[9], (D_MODEL,), jnp.float32),
        'attn_sinks': 0.5 * nrm(ks[10], (DEPTH, N_HEADS), jnp.float32),
        'rel_bias_table': 0.5 * nrm(ks[11], (N_BUCKETS, N_HEADS), jnp.float32),
        'hgrn_lb_logits': 0.5 * nrm(ks[12], (DEPTH, D_REC), jnp.float32),
        'hgrn_norm_w': 1.0 + 0.01 * nrm(ks[13], (DEPTH, D_REC), jnp.float32),
    }


def reference(x_prompt, x_sample, cache_k, cache_v, state_h, meta_tokens, w_in, w_out, norm_w,
              final_norm_w, attn_sinks, rel_bias_table, hgrn_lb_logits, hgrn_norm_w):
    B = x_prompt.shape[0]
    meta = jnp.broadcast_to(meta_tokens[None].astype(x_prompt.dtype), (B, N_META, D_MODEL))
    hp = jnp.concatenate([meta, x_prompt], axis=1)
    hs = x_sample
    bias_d = rel_bias_table[t5_bucket(jnp.arange(WINDOW))].T.astype(jnp.float32)
    pl = jax.nn.softmax(hgrn_lb_logits.astype(jnp.float32), axis=0)
    lb_all = jnp.cumsum(pl, axis=0) - pl[0:1]
    win = cache_k.shape[2]
    pk, pv, ps, sk, sv, ss = [], [], [], [], [], []
    for l in range(DEPTH):
        qa, ka, va, ga, qr, logf, kr, ir, gr = project(hp, norm_w[l], w_in[l], lb_all[l])
        oa = swa_prompt(qa, ka, va, attn_sinks[l], bias_d)
        s0 = jnp.zeros((B, N_REC_HEADS, REC_HEAD_DIM, REC_HEAD_DIM), jnp.float32)
        orec, sp = hgrn2_chunked(qr, logf, kr, ir, s0)
        hp = mix_out(hp, oa, ga, orec, gr, hgrn_norm_w[l], w_out[l])
        pk.append(ka[:, -win:])
        pv.append(va[:, -win:])
        ps.append(sp.astype(state_h.dtype))
        qa, ka, va, ga, qr, logf, kr, ir, gr = project(hs, norm_w[l], w_in[l], lb_all[l])
        oa, kbuf, vbuf = swa_sample(qa, ka, va, cache_k[l], cache_v[l], attn_sinks[l], bias_d)
        orec, s_s = hgrn2_chunked(qr, logf, kr, ir, state_h[l])
        hs = mix_out(hs, oa, ga, orec, gr, hgrn_norm_w[l], w_out[l])
        sk.append(kbuf)
        sv.append(vbuf)
        ss.append(s_s.astype(state_h.dtype))
    y_prompt = rmsnorm(hp[:, N_META:], final_norm_w)
    y_sample = rmsnorm(hs, final_norm_w)
    return (y_prompt, y_sample, jnp.stack(pk), jnp.stack(pv), jnp.stack(ps),
            jnp.stack(sk), jnp.stack(sv), jnp.stack(ss))
```

```python
import contextlib
import numpy as np
import concourse.bass as bass
import concourse.mybir as mybir
from concourse.bass_utils import run_bass_kernel_spmd

F32 = mybir.dt.float32
BF16 = mybir.dt.bfloat16
AF = mybir.ActivationFunctionType
ALU = mybir.AluOpType

DEPTH = 4
D = 2048
KC = 16
TP = 1032
NSEQ = 4
TS = 32
T = TP + TS
DIN = 6656
TCH = [(0, 320), (320, 320), (640, 320), (960, 104)]
EPS = 1e-6
NS = 3
C_QA, C_KA, C_VA, C_GA, C_QR, C_FR, C_IR, C_GR = 0, 1024, 1280, 1536, 2560, 3584, 4608, 5632
SEND_ROWS = 1536
import os
USE_CC = os.environ.get('KNOCC', '0') != '1'
STOP = os.environ.get('KSTOP', '')
P1L = int(os.environ.get('KP1', '9'))
PH = ['const', 'p0', 'norm', 'p15', 'p1', 'ex', 'p2', 'p3', 'out', 'fin']


def _on(ph):
    return STOP == '' or PH.index(ph) <= PH.index(STOP)


class Sched:
    def __init__(self, nc, stack):
        self.nc = nc
        self.stack = stack
        self.ops = []
        self.lastw = {}
        self.rds = {}
        self.eng = {'pe': nc.tensor, 'act': nc.scalar, 'dve': nc.vector, 'pool': nc.gpsimd, 'sp': nc.sync}
        self.sem = {e: stack.enter_context(nc.semaphore("s_" + e)) for e in self.eng}
        self.dsem = {}

    def add(self, eng, fn, reads=(), writes=(), dma=None):
        idx = len(self.ops)
        psr = [r for r in reads if r[0] == 'ps']
        if psr:
            reads = [r for r in reads if r[0] != 'ps']
            writes = list(writes) + [r for r in psr if r not in writes]
        deps = set()
        for r in reads:
            w = self.lastw.get(r)
            if w is not None:
                deps.add(w)
        for r in writes:
            w = self.lastw.get(r)
            if w is not None:
                deps.add(w)
            for x in self.rds.get(r, ()):
                deps.add(x)
        deps.discard(idx)
        if dma is not None and dma not in self.dsem:
            self.dsem[dma] = self.stack.enter_context(self.nc.semaphore("d%d" % len(self.dsem)))
        self.ops.append(dict(eng=eng, fn=fn, deps=deps, dma=dma, signal=False, sidx=0))
        for r in reads:
            self.rds.setdefault(r, []).append(idx)
        for r in writes:
            self.lastw[r] = idx
            self.rds[r] = []
        return idx

    def emit(self, final_slots):
        ops = self.ops

        def skip(p, op):
            return p['dma'] is None and op['dma'] is None and p['eng'] == 'pe' and op['eng'] == 'pe'

        for op in ops:
            for d in op['deps']:
                if not skip(ops[d], op):
                    ops[d]['signal'] = True
        cnt = {e: 0 for e in self.eng}
        dcnt = {}
        waited = {}
        for op in ops:
            e = op['eng']
            E = self.eng[e]
            need = {}
            for d in op['deps']:
                p = ops[d]
                if skip(p, op):
                    continue
                if p['dma'] is not None:
                    key = ('d', p['dma'])
                    val = 16 * dcnt[p['dma']]
                else:
                    key = ('e', p['eng'])
                    val = p['sidx']
                if val > need.get(key, 0):
                    need[key] = val
            for key, val in need.items():
                if waited.get((e, key), 0) >= val:
                    continue
                sem = self.dsem[key[1]] if key[0] == 'd' else self.sem[key[1]]
                E.wait_ge(sem, val)
                waited[(e, key)] = val
            inst = op['fn']()
            if op['dma'] is not None:
                inst.then_inc(self.dsem[op['dma']], 16)
                dcnt[op['dma']] = dcnt.get(op['dma'], 0) + 1
            elif op['signal']:
                cnt[e] += 1
                inst.then_inc(self.sem[e], 1)
                op['sidx'] = cnt[e]
        sp = self.eng['sp']
        for slot in final_slots:
            if slot in dcnt:
                sp.wait_ge(self.dsem[slot], 16 * dcnt[slot])


def build_nc(nl=DEPTH):
    nc = bass.Bass("TRN2", target_bir_lowering=False)
    dt_in = lambda name, shape: nc.dram_tensor(name, list(shape), F32, kind="ExternalInput").ap()
    dt_out = lambda name, shape: nc.dram_tensor(name, list(shape), F32, kind="ExternalOutput").ap()
    xin_p = [dt_in("xin%d" % p, (T, D)) for p in range(2)]
    ck_p = [dt_in("ck%d" % p, (DEPTH, NSEQ, 128, 256)) for p in range(2)]
    cv_p = [dt_in("cv%d" % p, (DEPTH, NSEQ, 128, 256)) for p in range(2)]
    st_p = [dt_in("st%d" % p, (DEPTH, NSEQ, 8, 128, 128)) for p in range(2)]
    SMALLW = os.environ.get("KSMALLW", "0") == "1"
    w_in = dt_in("w_in", (1, 128, 128) if SMALLW else (nl, D, DIN))
    w_out = dt_in("w_out", (1, 128, 128) if SMALLW else (nl, D, D))
    nw_in = dt_in("nw", (128, DEPTH * KC))
    fw_in = dt_in("fw", (128, KC))
    sink_in = dt_in("sink", (128, DEPTH * 16))
    biasT_in = dt_in("biasT", (16, 128, 256))
    mask_in = dt_in("mask01", (128, 256))
    tril_in = dt_in("tril", (128, 64))
    seg_in = dt_in("seg01", (128, T))
    lbl_in = dt_in("lbl", (128, DEPTH * 8))
    hw_in = dt_in("hw", (128, DEPTH * 8))
    identf_in = dt_in("identf", (128, 128))

    y_p = [dt_out("y%d" % p, (T, D)) for p in range(2)]
    pk_p = [dt_out("pk%d" % p, (DEPTH, 128, 256)) for p in range(2)]
    pv_p = [dt_out("pv%d" % p, (DEPTH, 128, 256)) for p in range(2)]
    ps_p = [dt_out("ps%d" % p, (DEPTH, 8, 128, 128)) for p in range(2)]
    sk_p = [dt_out("sk%d" % p, (DEPTH, NSEQ, 128, 256)) for p in range(2)]
    sv_p = [dt_out("sv%d" % p, (DEPTH, NSEQ, 128, 256)) for p in range(2)]
    ss_p = [dt_out("ss%d" % p, (DEPTH, NSEQ, 8, 128, 128)) for p in range(2)]
    send_p = [[nc.dram_tensor("send%d_%d" % (p, l), [SEND_ROWS, 128], F32).ap() for l in range(DEPTH)] for p in range(2)]
    emd = nc.dram_tensor("emd", [16, 128, 256], BF16).ap()

    stack = contextlib.ExitStack()
    with stack:
        S = Sched(nc, stack)
        cur = [16512]
        TOP = 229344
        nid = [0]

        def alloc(shape, dtype, at=None):
            isz = 4 if dtype == F32 else 2
            n = 1
            for s in shape[1:]:
                n *= s
            size = (n * isz + 31) // 32 * 32
            if at is None:
                off = cur[0]
                cur[0] += size
                assert cur[0] <= TOP, ("SBUF overflow", cur[0] - TOP)
            else:
                off = at
            nid[0] += 1
            return nc.alloc_sbuf_tensor_at("t%d" % nid[0], list(shape), dtype, offset=off)

        hT = alloc([128, KC, T], F32)
        xn = alloc([128, KC, T], BF16)
        mix = alloc([128, KC, T], BF16)
        XB = cur[0]
        cur[0] += 34048
        wbuf = alloc([128, NS, KC, 128], BF16)
        emg = alloc([128, 4, 256], BF16)
        identB = alloc([128, 128], BF16)
        identF = alloc([128, 128], F32)
        onesB = alloc([128, 128], BF16)
        onesFl = alloc([128, 128], BF16)
        tril = alloc([128, 64], F32)
        seg01 = alloc([128, T], BF16)
        nw = alloc([128, DEPTH * KC], F32)
        fw = alloc([128, KC], F32)
        lb = alloc([128, DEPTH * 8], F32)
        oml = alloc([128, DEPTH * 8], F32)
        noml = alloc([128, DEPTH * 8], F32)
        hw = alloc([128, DEPTH * 8], F32)
        esink = alloc([128, DEPTH * 16], F32)
        flag = alloc([128, 1], F32)
        epsT = alloc([128, 1], F32)
        zcol = alloc([128, 1], F32)
        carry = alloc([128, 1], F32)
        Dsave = alloc([128, 8], F32)
        YB = cur[0]
        yf = [alloc([128, 321], F32) for _ in range(6)]
        qs_t, sg_t, la_t, gx_t, eg_t, en_t = yf
        vT_t, qt_t, kt_t, kh_t = [alloc([128, 320], BF16) for _ in range(4)]
        khT = alloc([128, 2, 128], BF16)
        vtk2 = alloc([128, 2, 128], BF16)
        att = alloc([128, 2, 64], BF16)
        Sst = alloc([128, 128], F32)
        Sbf = alloc([128, 128], BF16)
        S0 = alloc([128, 4, 128], F32)
        sgr = alloc([128, T], BF16)
        sfst = alloc([128, 2, 128], F32)
        YEND = cur[0]
        print("SBUF used", cur[0] - 16512, "free", TOP - cur[0])
        kv32 = alloc([128, 4, 160], F32, at=YB)
        tokp = alloc([128, 512], F32, at=YB + 2560)
        toks = alloc([128, 512], F32, at=YB + 2560 + 2048)
        assert YB + 2560 + 4096 <= YEND
        rt = alloc([128, T], F32, at=YB)
        den = alloc([128, 4, 128], F32, at=YB)
        onr = alloc([128, 4, 128], F32, at=YB + 2048)
        ynt = alloc([128, 2, 128], F32, at=YB + 4288)
        oloc = alloc([128, 8, T], BF16, at=XB)
        Qd = alloc([128, 8, T], BF16, at=XB + 17024)
        sqtmp = alloc([128, 2, T], BF16, at=XB)
        xst = alloc([128, 2, 2048], F32, at=XB)
        emst_f = alloc([128, 2, 256], F32, at=XB + 16384)
        emst_t = alloc([128, 256], F32, at=XB + 16384 + 2048)
        emst_b = alloc([128, 2, 256], BF16, at=XB + 16384 + 3072)
        mask01 = alloc([128, 256], F32, at=XB + 16384 + 4096)
        lbtmp = alloc([128, DEPTH * 8], F32, at=XB + 16384 + 5120)
        lbsum = alloc([128, 8], F32, at=XB + 16384 + 5120 + 128)
        xo = [XB]

        def xalloc(shape, dtype):
            t_ = alloc(shape, dtype, at=xo[0])
            n = 1
            for s in shape[1:]:
                n *= s
            xo[0] += (n * (4 if dtype == F32 else 2) + 31) // 32 * 32
            assert xo[0] <= XB + 34048
            return t_

        qa = xalloc([128, 2, T], BF16)
        kd = xalloc([128, 128 + T], BF16)
        vdT = xalloc([128, T], BF16)
        sga = xalloc([128, 2, T], BF16)
        vtk = xalloc([128, 10, 128], BF16)
        kds = xalloc([128, NSEQ, 128], BF16)
        vtks = xalloc([128, NSEQ, 128], BF16)
        vtkn = xalloc([128, NSEQ, 128], BF16)
        ckst = xalloc([128, 2, 128], BF16)
        Eb = xalloc([128, 4, 2, 128], BF16)
        Pb = xalloc([128, 4, 2, 128], BF16)

        PS = []
        for b in range(8):
            if b == 4:
                PS.append(stack.enter_context(nc.psum_tensor("psb%d" % b, [128, 1024], BF16)))
            else:
                PS.append(stack.enter_context(nc.psum_tensor("psb%d" % b, [128, 512], F32)))
        pbrot = [0]

        def next_pb():
            b = pbrot[0] % 3
            pbrot[0] += 1
            return b

        V = nc.vector
        A = nc.scalar
        PE = nc.tensor

        bscr = alloc([128, 4], F32)

        def barrier(extra_writes=()):
            if os.environ.get('KNOBAR', '0') == '1':
                return
            S.add('act', lambda: A.copy(out=bscr[:, 0:1], in_=zcol[:, 0:1]), reads=[('zc',)], writes=[('bar', 'act')] + list(extra_writes))
            S.add('dve', lambda: V.tensor_copy(out=bscr[:, 1:2], in_=zcol[:, 0:1]), reads=[('zc',)], writes=[('bar', 'dve')])
            S.add('pe', lambda: PE.matmul(PS[7][0:1, 0:1], lhsT=identB[:, 0:1], rhs=identB[:, 0:1], start=True, stop=True),
                  reads=[('identB',)], writes=[('bar', 'pe'), ('ps', 7)])
            allb = [('bar', 'act'), ('bar', 'dve'), ('bar', 'pe')]
            S.add('act', lambda: A.copy(out=bscr[:, 2:3], in_=zcol[:, 0:1]), reads=allb + [('zc',)], writes=[('barj', 'act')])
            S.add('dve', lambda: V.tensor_copy(out=bscr[:, 3:4], in_=zcol[:, 0:1]), reads=allb + [('zc',)], writes=[('barj', 'dve')])
            S.add('pe', lambda: PE.matmul(PS[7][0:1, 0:1], lhsT=identB[:, 0:1], rhs=identB[:, 0:1], start=True, stop=True),
                  reads=allb + [('identB',)], writes=[('barj', 'pe'), ('ps', 7)])

        def ld(dst, src, key, q='sp'):
            eng = nc.sync if q == 'sp' else nc.gpsimd
            S.add(q, lambda: eng.dma_start(out=dst, in_=src), writes=[key], dma=('c', key))

        ld(identF[:, :], identf_in[:, :], ('identF',))
        ld(identB[:, :], identf_in[:, :], ('identB',), q='pool')
        ld(tril[:, :], tril_in[:, :], ('tril',))
        ld(seg01[:, :], seg_in[:, :], ('seg01',), q='pool')
        ld(nw[:, :], nw_in[:, :], ('nw',))
        ld(fw[:, :], fw_in[:, :], ('fw',))
        ld(hw[:, :], hw_in[:, :], ('hw',))
        ld(esink[:, :], sink_in[:, :], ('esink',))
        ld(lbtmp[:, :], lbl_in[:, :], ('lbtmp',))
        ld(mask01[:, :], mask_in[:, :], ('mask01',))
        S.add('dve', lambda: V.memset(zcol[:, :], 0.0), writes=[('zc',)])
        S.add('dve', lambda: V.memset(epsT[:, :], EPS), writes=[('eps',)])
        S.add('dve', lambda: V.memset(onesB[:, :], 1.0), writes=[('onesB',)])
        S.add('act', lambda: A.activation(out=esink[:, :], in_=esink[:, :], func=AF.Exp), reads=[('esink',)], writes=[('esink',)])
        S.add('act', lambda: A.activation(out=lbtmp[:, :], in_=lbtmp[:, :], func=AF.Exp), reads=[('lbtmp',)], writes=[('lbtmp',)])

        lb_steps = []

        def lb_chain():
            steps = [
                (lambda: V.tensor_tensor(out=lbsum[:, :], in0=lbtmp[:, 0:8], in1=lbtmp[:, 8:16], op=ALU.add)),
                (lambda: V.tensor_tensor(out=lbsum[:, :], in0=lbsum[:, :], in1=lbtmp[:, 16:24], op=ALU.add)),
                (lambda: V.tensor_tensor(out=lbsum[:, :], in0=lbsum[:, :], in1=lbtmp[:, 24:32], op=ALU.add)),
                (lambda: V.reciprocal(out=lbsum[:, :], in_=lbsum[:, :])),
                (lambda: V.memset(lb[:, 0:8], 0.0)),
                (lambda: V.tensor_tensor(out=lb[:, 8:16], in0=lbtmp[:, 8:16], in1=lbsum[:, :], op=ALU.mult)),
                (lambda: V.tensor_tensor(out=lb[:, 16:24], in0=lbtmp[:, 16:24], in1=lbsum[:, :], op=ALU.mult)),
                (lambda: V.tensor_tensor(out=lb[:, 24:32], in0=lbtmp[:, 24:32], in1=lbsum[:, :], op=ALU.mult)),
                (lambda: V.tensor_tensor(out=lb[:, 16:24], in0=lb[:, 16:24], in1=lb[:, 8:16], op=ALU.add)),
                (lambda: V.tensor_tensor(out=lb[:, 24:32], in0=lb[:, 24:32], in1=lb[:, 16:24], op=ALU.add)),
                (lambda: V.tensor_scalar(out=noml[:, :], in0=lb[:, :], scalar1=-1.0, scalar2=None, op0=ALU.add)),
                (lambda: V.tensor_scalar(out=oml[:, :], in0=noml[:, :], scalar1=-1.0, scalar2=None, op0=ALU.mult)),
            ]
            for f in steps:
                S.add('dve', f, reads=[('lbtmp',), ('lbc',)], writes=[('lbc',)])

        lb_chain()

        for h in range(16 if os.environ.get('KNOEM', '0') != '1' else 0):
            sl = h % 2
            S.add('sp', lambda h=h, sl=sl: nc.sync.dma_start(out=emst_f[:, sl, :], in_=biasT_in[h, :, :]),
                  writes=[('emf', sl)], dma=('emf', sl))
            S.add('act', lambda sl=sl: A.activation(out=emst_t[:, :], in_=emst_f[:, sl, :], func=AF.Exp),
                  reads=[('emf', sl)], writes=[('emt',)])
            S.add('dve', lambda sl=sl: V.tensor_tensor(out=emst_b[:, sl, :], in0=emst_t[:, :], in1=mask01[:, :], op=ALU.mult),
                  reads=[('emt',), ('mask01',)], writes=[('emb', sl)])
            S.add('sp', lambda h=h, sl=sl: nc.sync.dma_start(out=emd[h, :, :], in_=emst_b[:, sl, :]),
                  reads=[('emb', sl)], writes=[('emd',)], dma=('emd',))
        barrier(extra_writes=[('emb', 0), ('emb', 1), ('emf', 0), ('emf', 1), ('mask01',), ('lbtmp',)])

        def hkeys(k, t0=0, n=T):
            return [('h', k, i) for i, (a, m) in enumerate(TCH) if a < t0 + n and t0 < a + m]

        TT = [(i * 128, 128) for i in range(8)] + [(1024, 40)]
        units = []
        for l in range(nl):
            for u in range(4):
                units.append(('in', l, C_KA + 128 * u))
            for h in range(8):
                units.append(('in', l, C_QR + 128 * h))
                units.append(('in', l, C_FR + 128 * h))
                units.append(('in', l, C_IR + 128 * h))
            for h in range(8):
                units.append(('in', l, C_GR + 128 * h))
            for g in range(4):
                units.append(('in', l, C_QA + 256 * g))
                units.append(('in', l, C_QA + 256 * g + 128))
                units.append(('dup', l, C_KA + 64 * g))
                units.append(('dup', l, C_VA + 64 * g))
                units.append(('in', l, C_GA + 256 * g))
                units.append(('in', l, C_GA + 256 * g + 128))
            for n_ in range(16):
                units.append(('out', l, 128 * n_))
        units = units + units
        wst = dict(next=0, issued=0)

        def w_issue(i):
            kind, l, c0 = units[i]
            slot = i % NS
            if kind == 'in':
                src = w_in[l].rearrange("(k p) c -> p k c", p=128)[:, :, c0:c0 + 128]
                S.add('pool', lambda slot=slot, src=src: nc.gpsimd.dma_start(out=wbuf[:, slot, :, :], in_=src),
                      writes=[('w', slot)], dma=('w', slot))
            elif kind == 'out':
                src = w_out[l].rearrange("(k p) c -> p k c", p=128)[:, :, c0:c0 + 128]
                S.add('pool', lambda slot=slot, src=src: nc.gpsimd.dma_start(out=wbuf[:, slot, :, :], in_=src),
                      writes=[('w', slot)], dma=('w', slot))
            else:
                src = w_in[l].rearrange("(k p) c -> p k c", p=128)[:, :, c0:c0 + 64]
                S.add('pool', lambda slot=slot, src=src: nc.gpsimd.dma_start(out=wbuf[:, slot, :, 0:64], in_=src),
                      writes=[('w', slot)], dma=('w', slot))
                S.add('pool', lambda slot=slot, src=src: nc.gpsimd.dma_start(out=wbuf[:, slot, :, 64:128], in_=src),
                      writes=[('w', slot)], dma=('w', slot))

        def w_release():
            done = wst['next']
            while wst['issued'] < min(done + NS, len(units)):
                w_issue(wst['issued'])
                wst['issued'] += 1

        def w_begin(k=1):
            w_release()
            first = wst['next']
            wst['next'] += k
            assert wst['issued'] >= wst['next'], (wst, k)
            return [(first + j) % NS for j in range(k)]

        def w_next():
            return w_begin(1)[0]

        XN_ALL = [('xn', k) for k in range(KC)]
        BARK = [('barj', 'act'), ('barj', 'dve'), ('barj', 'pe')]
        RECV_KEYS = [('send', x_) for x_ in list(range(8)) + ['k', 'v']]

        def proj(slot, t0, n, b):
            def f():
                for k in range(KC):
                    last = PE.matmul(PS[b][:, 0:n], lhsT=wbuf[:, slot, k, :], rhs=xn[:, k, t0:t0 + n],
                                     start=(k == 0), stop=(k == KC - 1))
                return last
            S.add('pe', f, reads=[('w', slot)] + XN_ALL, writes=[('ps', b)])

        def do_norm(wt, wcol0):
            banks = [0, 1, 2, 5]
            for k in range(KC):
                sl = k % 2
                S.add('act', lambda k=k, sl=sl: A.activation(out=sqtmp[:, sl, :], in_=hT[:, k, :], func=AF.Square),
                      reads=hkeys(k), writes=[('sq', sl)])
                for i, (t0, n) in enumerate(TCH):
                    S.add('pe', lambda k=k, sl=sl, i=i, t0=t0, n=n: PE.matmul(PS[banks[i]][:, 0:n], lhsT=onesB[:, :], rhs=sqtmp[:, sl, t0:t0 + n],
                                                                             start=(k == 0), stop=(k == KC - 1)),
                          reads=[('sq', sl), ('onesB',)], writes=[('ps', banks[i])])
            for i, (t0, n) in enumerate(TCH):
                S.add('act', lambda i=i, t0=t0, n=n: A.activation(out=rt[:, t0:t0 + n], in_=PS[banks[i]][:, 0:n], func=AF.Sqrt,
                                                                  bias=epsT[:, 0:1], scale=1.0 / D),
                      reads=[('ps', banks[i]), ('eps',)], writes=[('rt', i)])
                S.add('dve', lambda t0=t0, n=n: V.reciprocal(out=rt[:, t0:t0 + n], in_=rt[:, t0:t0 + n]),
                      reads=[('rt', i)], writes=[('rt', i)])
            return [('rt', i) for i in range(4)]

        def tc_of(t0):
            return [i for i, (a, m) in enumerate(TCH) if a == t0][0]

        def run_pass(p, xin, ck_in, cv_in, st_in, y_out, pk_out, pv_out, ps_out, sk_out, sv_out, ss_out, send, recv):
            barrier(extra_writes=[('xst', 0), ('xst', 1), ('flag',), ('onesFl',)])
            S.add('dve', lambda: V.memset(flag[:, :], float(p)), writes=[('flag',)])
            S.add('dve', lambda: V.tensor_scalar(out=onesFl[:, :], in0=onesB[:, :], scalar1=flag[:, 0:1], scalar2=None, op0=ALU.mult),
                  reads=[('onesB',), ('flag',)], writes=[('onesFl',)])
            if _on('p0'):
                for ti, (t0, n) in enumerate(TT):
                    sl = ti % 2
                    S.add('sp', lambda sl=sl, t0=t0, n=n: nc.sync.dma_start(out=xst[0:n, sl, :], in_=xin[t0:t0 + n, :]),
                          reads=BARK, writes=[('xst', sl)], dma=('xst', sl))
                    for kq in range(4):
                        b = 5 + (ti * 4 + kq) % 3

                        def f(sl=sl, n=n, kq=kq, b=b):
                            for j in range(4):
                                k = kq * 4 + j
                                last = PE.transpose(out=PS[b][:, j * 128:j * 128 + n], in_=xst[0:n, sl, k * 128:(k + 1) * 128],
                                                    identity=identF[0:n, 0:n])
                            return last
                        S.add('pe', f, reads=[('xst', sl), ('identF',)], writes=[('ps', b)])
                        hk = []
                        for j in range(4):
                            hk += hkeys(kq * 4 + j, t0, n)
                        cp = (lambda kq=kq, t0=t0, n=n, b=b: A.copy(out=hT[:, kq * 4:kq * 4 + 4, t0:t0 + n],
                                                                   in_=PS[b][:, :].rearrange("p (j c) -> p j c", j=4)[:, :, 0:n]))
                        cpv = (lambda kq=kq, t0=t0, n=n, b=b: V.tensor_copy(out=hT[:, kq * 4:kq * 4 + 4, t0:t0 + n],
                                                                             in_=PS[b][:, :].rearrange("p (j c) -> p j c", j=4)[:, :, 0:n]))
                        if kq % 2 == 0:
                            S.add('act', cp, reads=[('ps', b)], writes=hk)
                        else:
                            S.add('dve', cpv, reads=[('ps', b)], writes=hk)
            barrier(extra_writes=[('xst', 0), ('xst', 1)])

            out_slots = []
            for l in range(nl):
                if not _on('norm'):
                    break
                rtk = do_norm(nw, l * KC)
                for k in range(KC):
                    S.add('dve', lambda k=k, l=l: V.scalar_tensor_tensor(out=xn[:, k, :], in0=hT[:, k, :], scalar=nw[:, l * KC + k:l * KC + k + 1],
                                                                         in1=rt[:, :], op0=ALU.mult, op1=ALU.mult),
                          reads=hkeys(k) + rtk + [('nw',)], writes=[('xn', k)])
                barrier(extra_writes=rtk + [('sq', 0), ('sq', 1)])

                if not _on('p15'):
                    break
                T15 = 904
                for u in range(4):
                    slot = w_next()
                    b = next_pb()
                    proj(slot, T15, 160, b)
                    S.add('act', lambda u=u, b=b: A.copy(out=kv32[:, u, :], in_=PS[b][:, 0:160]), reads=[('ps', b)], writes=[('kv32', u)])
                S.add('sp', lambda l=l: nc.sync.dma_start(out=send[l][1024:1280, :].rearrange("(c p) t -> p c t", p=128), in_=kv32[:, 0:2, 0:128]),
                      reads=[('kv32', 0), ('kv32', 1)], writes=[('send', 'k')], dma=('snd',))

                def f15():
                    for u in range(4):
                        PE.transpose(out=PS[7][:, u * 128:(u + 1) * 128], in_=kv32[:, u, 0:128], identity=identF[:, :])
                    for u in range(4):
                        last = PE.transpose(out=PS[6][0:32, u * 128:(u + 1) * 128], in_=kv32[:, u, 128:160], identity=identF[:, :])
                    return last
                S.add('pe', f15, reads=[('kv32', u) for u in range(4)] + [('identF',)], writes=[('ps', 7), ('ps', 6)])
                S.add('act', lambda: A.copy(out=tokp[:, :], in_=PS[7][:, :]), reads=[('ps', 7)], writes=[('tokp',)])
                S.add('dve', lambda: V.tensor_copy(out=toks[0:32, :], in_=PS[6][0:32, :]), reads=[('ps', 6)], writes=[('toks',)])
                S.add('sp', lambda l=l: nc.sync.dma_start(out=pk_out[l, :, :], in_=tokp[:, 0:256]), reads=[('tokp',)], dma=('o_pk',))
                S.add('sp', lambda l=l: nc.sync.dma_start(out=pv_out[l, :, :], in_=tokp[:, 256:512]), reads=[('tokp',)], dma=('o_pk',))
                S.add('sp', lambda l=l: nc.sync.dma_start(out=send[l][1280:1536, :].rearrange("(c s) f -> s c f", s=128),
                                                          in_=tokp[:, 256:512].rearrange("s (c f) -> s c f", c=2)),
                      reads=[('tokp',)], writes=[('send', 'v')], dma=('snd',))
                for s in range(NSEQ):
                    S.add('sp', lambda l=l, s=s: nc.sync.dma_start(out=sk_out[l, s, 120:128, :], in_=toks[8 * s:8 * s + 8, 0:256]),
                          reads=[('toks',)], dma=('o_sk',))
                    S.add('sp', lambda l=l, s=s: nc.sync.dma_start(out=sv_out[l, s, 120:128, :], in_=toks[8 * s:8 * s + 8, 256:512]),
                          reads=[('toks',)], dma=('o_sk',))
                S.add('sp', lambda l=l: nc.sync.dma_start(out=sk_out[l, :, 0:120, :], in_=ck_in[l, :, 8:128, :]), dma=('o_sk',))
                S.add('sp', lambda l=l: nc.sync.dma_start(out=sv_out[l, :, 0:120, :], in_=cv_in[l, :, 8:128, :]), dma=('o_sk',))
                barrier(extra_writes=[('kv32', u) for u in range(4)] + [('tokp',), ('toks',)])

                if not _on('p1'):
                    break
                for h in range(8):
                    lh = l * 8 + h
                    s_q, s_f, s_i = None, None, None
                    S.add('sp', lambda l=l, h=h: nc.sync.dma_start(out=S0[:, :, :], in_=st_in[l, :, h, :, :].rearrange("s k v -> k s v")),
                          writes=[('S0',)], dma=('S0',))
                    S.add('dve', lambda: V.memset(carry[:, :], 0.0), writes=[('carry',)])
                    slots = w_begin(3)
                    for i, (t0, n) in enumerate(TCH):
                        proj(slots[0], t0, n, 0)
                        proj(slots[1], t0, n, 1)
                        proj(slots[2], t0, n, 2)
                        if i == 3:
                            w_release()
                        S.add('act', lambda n=n: A.activation(out=qs_t[:, 0:n], in_=PS[0][:, 0:n], func=AF.Silu), reads=[('ps', 0)], writes=[('qs',)])
                        S.add('act', lambda n=n: A.activation(out=sg_t[:, 0:n], in_=PS[1][:, 0:n], func=AF.Sigmoid), reads=[('ps', 1)], writes=[('sg',)])
                        S.add('act', lambda n=n: A.copy(out=vT_t[:, 0:n], in_=PS[2][:, 0:n]), reads=[('ps', 2)], writes=[('vT',)])
                        S.add('act', lambda n=n, lh=lh: A.activation(out=la_t[:, 0:n], in_=sg_t[:, 0:n], func=AF.Ln,
                                                                     bias=lb[:, lh:lh + 1], scale=oml[:, lh:lh + 1]),
                              reads=[('sg',), ('lbc',)], writes=[('la',)])
                        S.add('dve', lambda n=n, lh=lh: V.tensor_scalar(out=sg_t[:, 0:n], in0=sg_t[:, 0:n], scalar1=noml[:, lh:lh + 1],
                                                                        scalar2=oml[:, lh:lh + 1], op0=ALU.mult, op1=ALU.add),
                              reads=[('sg',), ('la',), ('lbc',)], writes=[('sg',)])
                        if P1L < 2:
                            continue
                        S.add('dve', lambda: V.tensor_copy(out=gx_t[:, 0:1], in_=carry[:, 0:1]), reads=[('carry',)], writes=[('gx',)])
                        S.add('dve', lambda t0=t0, n=n: V.tensor_tensor_scan(out=gx_t[:, 1:1 + n], data0=seg01[:, t0:t0 + n], data1=la_t[:, 0:n],
                                                                             initial=carry[:, 0:1], op0=ALU.mult, op1=ALU.add),
                              reads=[('la',), ('carry',), ('seg01',), ('gx',)], writes=[('gx',)])
                        S.add('dve', lambda n=n: V.tensor_copy(out=carry[:, 0:1], in_=gx_t[:, n:n + 1]), reads=[('gx',)], writes=[('carry',)])
                        S.add('act', lambda n=n: A.activation(out=eg_t[:, 0:n], in_=gx_t[:, 1:1 + n], func=AF.Exp), reads=[('gx',)], writes=[('eg',)])
                        S.add('dve', lambda h=h, t0=t0, n=n: V.tensor_tensor(out=Qd[:, h, t0:t0 + n], in0=qs_t[:, 0:n], in1=eg_t[:, 0:n], op=ALU.mult),
                              reads=[('qs',), ('eg',)], writes=[('Qd', h)])
                        if P1L < 3:
                            continue
                        if i < 3:
                            chunks = [(64 * j, 64, 'p', j == 0 and i == 0, False, -1) for j in range(5)]
                        else:
                            chunks = [(0, 64, 'p', False, False, -1), (64, 8, 'p', False, True, -1)] + \
                                     [(72 + 8 * s, 8, 's', True, True, s) for s in range(NSEQ)]
                        for (o, C, kind, segstart, segend, sq) in chunks:
                            ref = zcol[:, 0:1] if kind == 's' else gx_t[:, o:o + 1]
                            S.add('dve', lambda o=o, C=C, ref=ref: V.tensor_scalar(out=la_t[:, o:o + C], in0=gx_t[:, 1 + o:1 + o + C], scalar1=ref,
                                                                                   scalar2=None, op0=ALU.subtract),
                                  reads=[('gx',), ('zc',), ('la',)], writes=[('la',)])
                        S.add('act', lambda n=n: A.activation(out=en_t[:, 0:n], in_=la_t[:, 0:n], func=AF.Exp, scale=-1.0), reads=[('la',)], writes=[('en',)])
                        S.add('act', lambda n=n: A.activation(out=la_t[:, 0:n], in_=la_t[:, 0:n], func=AF.Exp), reads=[('la',), ('en',)], writes=[('la',)])
                        S.add('dve', lambda n=n: V.tensor_tensor(out=qt_t[:, 0:n], in0=qs_t[:, 0:n], in1=la_t[:, 0:n], op=ALU.mult),
                              reads=[('qs',), ('la',)], writes=[('qt',)])
                        S.add('dve', lambda n=n: V.tensor_tensor(out=kt_t[:, 0:n], in0=sg_t[:, 0:n], in1=en_t[:, 0:n], op=ALU.mult),
                              reads=[('sg',), ('en',)], writes=[('kt',)])
                        for (o, C, kind, segstart, segend, sq) in chunks:
                            S.add('dve', lambda o=o, C=C: V.tensor_scalar(out=kh_t[:, o:o + C], in0=kt_t[:, o:o + C], scalar1=la_t[:, o + C - 1:o + C],
                                                                          scalar2=None, op0=ALU.mult),
                                  reads=[('kt',), ('la',)], writes=[('kh',)])
                        if P1L < 4:
                            continue
                        for ci, (o, C, kind, segstart, segend, sq) in enumerate(chunks):
                            par = ci % 2
                            pa = 0
                            pb4 = 0

                            def ft(o=o, C=C, pb4=pb4):
                                PE.transpose(out=PS[4][0:C, pb4:pb4 + 128], in_=kh_t[:, o:o + C], identity=identB[:, :])
                                return PE.transpose(out=PS[4][0:C, pb4 + 128:pb4 + 256], in_=vT_t[:, o:o + C], identity=identB[:, :])
                            S.add('pe', ft, reads=[('kh',), ('vT',), ('identB',)], writes=[('ps', 4)])
                            S.add('act', lambda C=C, pb4=pb4, par=par: A.copy(out=khT[0:C, par, :], in_=PS[4][0:C, pb4:pb4 + 128]),
                                  reads=[('ps', 4)], writes=[('khT', par)])
                            S.add('dve', lambda C=C, pb4=pb4, par=par: V.tensor_copy(out=vtk2[0:C, par, :], in_=PS[4][0:C, pb4 + 128:pb4 + 256]),
                                  reads=[('ps', 4)], writes=[('vtk2', par)])
                            if P1L < 5:
                                continue
                            S.add('pe', lambda o=o, C=C, pa=pa: PE.matmul(PS[3][0:C, pa:pa + C], lhsT=kt_t[:, o:o + C], rhs=qt_t[:, o:o + C], start=True, stop=True),
                                  reads=[('kt',), ('qt',)], writes=[('ps', 3)])
                            S.add('dve', lambda C=C, pa=pa, par=par: V.tensor_tensor(out=att[0:C, par, 0:C], in0=PS[3][0:C, pa:pa + C], in1=tril[0:C, 0:C], op=ALU.mult),
                                  reads=[('ps', 3), ('tril',)], writes=[('att', par)])
                            if kind == 's':
                                S.add('act', lambda sq=sq: A.copy(out=Sbf[:, :], in_=S0[:, sq, :]), reads=[('S0',)], writes=[('Sbf',)])
                            use_s = (kind == 's') or (not segstart)

                            def fo(o=o, C=C, pa=pa, par=par, use_s=use_s):
                                if use_s:
                                    PE.matmul(PS[5][:, 0:C], lhsT=Sbf[:, :], rhs=qt_t[:, o:o + C], start=True, stop=False)
                                return PE.matmul(PS[5][:, 0:C], lhsT=vtk2[0:C, par, :], rhs=att[0:C, par, 0:C], start=(not use_s), stop=True)
                            S.add('pe', fo, reads=[('Sbf',), ('qt',), ('vtk2', par), ('att', par)], writes=[('ps', 5)])
                            S.add('act', lambda h=h, t0=t0, o=o, C=C, pa=pa: A.copy(out=oloc[:, h, t0 + o:t0 + o + C], in_=PS[5][:, 0:C]),
                                  reads=[('ps', 5)], writes=[('oloc', h)])
                            if P1L < 6:
                                continue
                            S.add('pe', lambda C=C, pa=pa, par=par: PE.matmul(PS[6][:, 0:128], lhsT=khT[0:C, par, :], rhs=vtk2[0:C, par, :], start=True, stop=True),
                                  reads=[('khT', par), ('vtk2', par)], writes=[('ps', 6)])
                            elast = la_t[:, o + C - 1:o + C]
                            if kind == 's':
                                S.add('dve', lambda sq=sq, pa=pa, par=par, elast=elast: V.scalar_tensor_tensor(
                                    out=sfst[:, par, :], in0=S0[:, sq, :], scalar=elast, in1=PS[6][:, 0:128], op0=ALU.mult, op1=ALU.add),
                                    reads=[('S0',), ('la',), ('ps', 6)], writes=[('sfst', par)])
                                S.add('sp', lambda l=l, sq=sq, h=h, par=par: nc.sync.dma_start(out=ss_out[l, sq, h, :, :], in_=sfst[:, par, :]),
                                      reads=[('sfst', par)], dma=('o_ss', par))
                            else:
                                if segstart:
                                    S.add('dve', lambda pa=pa: V.tensor_copy(out=Sst[:, :], in_=PS[6][:, 0:128]),
                                          reads=[('ps', 6)], writes=[('Sst',)])
                                else:
                                    S.add('dve', lambda pa=pa, elast=elast: V.scalar_tensor_tensor(
                                        out=Sst[:, :], in0=Sst[:, :], scalar=elast, in1=PS[6][:, 0:128], op0=ALU.mult, op1=ALU.add),
                                        reads=[('Sst',), ('la',), ('ps', 6)], writes=[('Sst',)])
                                if segend:
                                    S.add('sp', lambda l=l, h=h: nc.sync.dma_start(out=send[l][h * 128:(h + 1) * 128, :], in_=Sst[:, :]),
                                          reads=[('Sst',)], writes=[('send', h)], dma=('snd',))
                                    S.add('dve', lambda h=h, o=o, C=C: V.tensor_copy(out=Dsave[:, h:h + 1], in_=eg_t[:, o + C - 1:o + C]),
                                          reads=[('eg',)], writes=[('Dsave', h)])
                                else:
                                    S.add('act', lambda: A.copy(out=Sbf[:, :], in_=Sst[:, :]), reads=[('Sst',)], writes=[('Sbf',)])

                if not _on('ex'):
                    break
                for h in range(8):
                    lh = l * 8 + h
                    slot = w_next()
                    S.add('sp', lambda l=l, h=h: nc.sync.dma_start(out=S0[:, 0, :], in_=recv[l][h * 128:(h + 1) * 128, :]),
                          reads=RECV_KEYS, writes=[('S0',)], dma=('S0',))
                    S.add('sp', lambda l=l, h=h: nc.sync.dma_start(out=S0[:, 1, :], in_=send[l][h * 128:(h + 1) * 128, :]),
                          reads=[('send', h)], writes=[('S0',)], dma=('S0',))
                    S.add('dve', lambda: V.tensor_scalar(out=S0[:, 0, :], in0=S0[:, 0, :], scalar1=flag[:, 0:1], scalar2=None, op0=ALU.mult),
                          reads=[('S0',), ('flag',)], writes=[('S0',)])
                    S.add('act', lambda: A.copy(out=Sbf[:, :], in_=S0[:, 0, :]), reads=[('S0',)], writes=[('Sbf',)])
                    par = h % 2
                    S.add('dve', lambda h=h, par=par: V.scalar_tensor_tensor(out=sfst[:, par, :], in0=S0[:, 0, :], scalar=Dsave[:, h:h + 1], in1=S0[:, 1, :],
                                                                             op0=ALU.mult, op1=ALU.add),
                          reads=[('S0',), ('Dsave', h)], writes=[('sfst', par)])
                    S.add('sp', lambda l=l, h=h, par=par: nc.sync.dma_start(out=ps_out[l, h, :, :], in_=sfst[:, par, :]),
                          reads=[('sfst', par)], dma=('o_ss', par))
                    for i, (t0, n) in enumerate(TCH):
                        b = next_pb()
                        proj(slot, t0, n, b)
                        S.add('act', lambda t0=t0, n=n, b=b: A.activation(out=sgr[:, t0:t0 + n], in_=PS[b][:, 0:n], func=AF.Silu),
                              reads=[('ps', b)], writes=[('sgr', i)])
                    for i, (t0, n) in enumerate(TCH):
                        npr = min(n, TP - t0)
                        S.add('pe', lambda h=h, t0=t0, npr=npr: PE.matmul(PS[5][:, 0:npr], lhsT=Sbf[:, :], rhs=Qd[:, h, t0:t0 + npr], start=True, stop=True),
                              reads=[('Sbf',), ('Qd', h)], writes=[('ps', 5)])
                        S.add('dve', lambda h=h, t0=t0, npr=npr: V.tensor_tensor(out=qs_t[:, 0:npr], in0=PS[5][:, 0:npr], in1=oloc[:, h, t0:t0 + npr], op=ALU.add),
                              reads=[('ps', 5), ('oloc', h)], writes=[('of',)])
                        if npr < n:
                            S.add('dve', lambda h=h, t0=t0, n=n, npr=npr: V.tensor_copy(out=qs_t[:, npr:n], in_=oloc[:, h, t0 + npr:t0 + n]),
                                  reads=[('oloc', h), ('of',)], writes=[('of',)])
                        S.add('act', lambda n=n: A.activation(out=qt_t[:, 0:n], in_=qs_t[:, 0:n], func=AF.Square), reads=[('of',)], writes=[('osq',)])
                        S.add('pe', lambda n=n: PE.matmul(PS[6][:, 0:n], lhsT=onesB[:, :], rhs=qt_t[:, 0:n], start=True, stop=True),
                              reads=[('osq',), ('onesB',)], writes=[('ps', 6)])
                        S.add('act', lambda n=n: A.activation(out=sg_t[:, 0:n], in_=PS[6][:, 0:n], func=AF.Sqrt, bias=epsT[:, 0:1], scale=1.0 / 128),
                              reads=[('ps', 6), ('eps',)], writes=[('rr',)])
                        S.add('dve', lambda n=n: V.reciprocal(out=sg_t[:, 0:n], in_=sg_t[:, 0:n]), reads=[('rr',)], writes=[('rr',)])
                        S.add('dve', lambda n=n: V.tensor_tensor(out=qs_t[:, 0:n], in0=qs_t[:, 0:n], in1=sg_t[:, 0:n], op=ALU.mult),
                              reads=[('of',), ('rr',)], writes=[('of',)])
                        S.add('dve', lambda h=h, lh=lh, t0=t0, n=n: V.scalar_tensor_tensor(out=mix[:, 8 + h, t0:t0 + n], in0=qs_t[:, 0:n], scalar=hw[:, lh:lh + 1],
                                                                                          in1=sgr[:, t0:t0 + n], op0=ALU.mult, op1=ALU.mult),
                              reads=[('of',), ('hw',), ('sgr', i)], writes=[('mix', 8 + h, i)])
                barrier(extra_writes=[('oloc', h) for h in range(8)] + [('Qd', h) for h in range(8)])

                if not _on('p3'):
                    break
                for g in range(4):
                    for jp in range(4):
                        hsrc = 4 * g + 2 * (jp % 2) + jp // 2
                        S.add('sp', lambda jp=jp, hsrc=hsrc: nc.sync.dma_start(out=emg[:, jp, :], in_=emd[hsrc, :, :]),
                              reads=[('emd',)], writes=[('emg',)], dma=('emg',))
                    for hf in range(2):
                        S.add('pool', lambda l=l, g=g, hf=hf: nc.gpsimd.dma_start(out=kd[64 * hf:64 * hf + 64, 0:128], in_=recv[l][1024 + 64 * g:1024 + 64 * g + 64, :]),
                              reads=RECV_KEYS + BARK, writes=[('kd', 'halo')], dma=('halo',))
                        S.add('pool', lambda l=l, g=g, hf=hf: nc.gpsimd.dma_start(
                            out=vtk[:, 0, 64 * hf:64 * hf + 64], in_=recv[l][1280 + 128 * (g // 2):1280 + 128 * (g // 2) + 128, 64 * (g % 2):64 * (g % 2) + 64]),
                            reads=RECV_KEYS + BARK, writes=[('vtk', 0)], dma=('halo',))
                        S.add('pool', lambda l=l, g=g, hf=hf: nc.gpsimd.dma_start(
                            out=vtks[:, :, 64 * hf:64 * hf + 64], in_=cv_in[l, :, :, 64 * g:64 * g + 64].rearrange("s t f -> t s f")),
                            reads=BARK, writes=[('vtks',)], dma=('halo',))
                    S.add('dve', lambda: V.tensor_scalar(out=vtk[:, 0, :], in0=vtk[:, 0, :], scalar1=flag[:, 0:1], scalar2=None, op0=ALU.mult),
                          reads=[('vtk', 0), ('flag',)], writes=[('vtk', 0)])
                    for s in range(NSEQ):
                        sl = s % 2
                        for hf in range(2):
                            S.add('pool', lambda l=l, g=g, s=s, sl=sl, hf=hf: nc.gpsimd.dma_start(out=ckst[:, sl, 64 * hf:64 * hf + 64], in_=ck_in[l, s, :, 64 * g:64 * g + 64]),
                                  reads=BARK, writes=[('ckst', sl)], dma=('ckst', sl))
                        S.add('pe', lambda sl=sl: PE.transpose(out=PS[4][:, 256 * sl:256 * sl + 128], in_=ckst[:, sl, :], identity=identB[:, :]),
                              reads=[('ckst', sl), ('identB',)], writes=[('ps', 4)])
                        S.add('act', lambda s=s, sl=sl: A.copy(out=kds[:, s, :], in_=PS[4][:, 256 * sl:256 * sl + 128]), reads=[('ps', 4)], writes=[('kds', s)])
                    s_q0 = w_next()
                    for i, (t0, n) in enumerate(TCH):
                        b = next_pb()
                        proj(s_q0, t0, n, b)
                        S.add('act', lambda t0=t0, n=n, b=b: A.copy(out=qa[:, 0, t0:t0 + n], in_=PS[b][:, 0:n]), reads=[('ps', b)], writes=[('qa', 0, i)])
                    s_q1 = w_next()
                    for i, (t0, n) in enumerate(TCH):
                        b = next_pb()
                        proj(s_q1, t0, n, b)
                        S.add('dve', lambda t0=t0, n=n, b=b: V.tensor_copy(out=qa[:, 1, t0:t0 + n], in_=PS[b][:, 0:n]), reads=[('ps', b)], writes=[('qa', 1, i)])
                    s_k = w_next()
                    for i, (t0, n) in enumerate(TCH):
                        b = next_pb()
                        proj(s_k, t0, n, b)
                        S.add('act', lambda t0=t0, n=n, b=b: A.copy(out=kd[:, 128 + t0:128 + t0 + n], in_=PS[b][:, 0:n]), reads=[('ps', b)], writes=[('kd', i)])
                    s_v = w_next()
                    for i, (t0, n) in enumerate(TCH):
                        b = next_pb()
                        proj(s_v, t0, n, b)
                        S.add('dve', lambda t0=t0, n=n, b=b: V.tensor_copy(out=vdT[:, t0:t0 + n], in_=PS[b][:, 0:n]), reads=[('ps', b)], writes=[('vdT', i)])
                    for jj in range(2):
                        s_g = w_next()
                        for i, (t0, n) in enumerate(TCH):
                            b = next_pb()
                            proj(s_g, t0, n, b)
                            S.add('act', lambda jj=jj, t0=t0, n=n, b=b: A.activation(out=sga[:, jj, t0:t0 + n], in_=PS[b][:, 0:n], func=AF.Silu),
                                  reads=[('ps', b)], writes=[('sga', jj, i)])
                    VD_ALL = [('vdT', i) for i in range(4)]
                    KD_ALL = [('kd', i) for i in range(4)] + [('kd', 'halo')]
                    QA_ALL = [('qa', jj, i) for jj in range(2) for i in range(4)]
                    SGA_ALL = [('sga', jj, i) for jj in range(2) for i in range(4)]
                    blocks = [(128 * i, 128) for i in range(8)] + [(1024, 8)]
                    for bi, (t0, n) in enumerate(blocks):
                        sl = bi % 2
                        S.add('pe', lambda t0=t0, n=n, sl=sl: PE.transpose(out=PS[4][0:n, 256 * sl:256 * sl + 128], in_=vdT[:, t0:t0 + n], identity=identB[:, :]),
                              reads=VD_ALL + [('identB',)], writes=[('ps', 4)])
                        if bi % 2 == 0:
                            S.add('act', lambda bi=bi, n=n, sl=sl: A.copy(out=vtk[0:n, 1 + bi, :], in_=PS[4][0:n, 256 * sl:256 * sl + 128]),
                                  reads=[('ps', 4)], writes=[('vtk', 1 + bi)])
                        else:
                            S.add('dve', lambda bi=bi, n=n, sl=sl: V.tensor_copy(out=vtk[0:n, 1 + bi, :], in_=PS[4][0:n, 256 * sl:256 * sl + 128]),
                                  reads=[('ps', 4)], writes=[('vtk', 1 + bi)])
                    for s in range(NSEQ):
                        sl = s % 2
                        t0 = TP + 8 * s
                        S.add('pe', lambda t0=t0, sl=sl: PE.transpose(out=PS[4][0:8, 256 * sl:256 * sl + 128], in_=vdT[:, t0:t0 + 8], identity=identB[:, :]),
                              reads=VD_ALL + [('identB',)], writes=[('ps', 4)])
                        S.add('act', lambda s=s, sl=sl: A.copy(out=vtkn[0:8, s, :], in_=PS[4][0:8, 256 * sl:256 * sl + 128]),
                              reads=[('ps', 4)], writes=[('vtkn', s)])
                    ablocks = []
                    for bi, (t0, n) in enumerate(blocks):
                        ablocks.append(dict(t0=t0, nq=n, kp=kd[:, 128 * bi:128 * bi + 128], kc=kd[:, 128 + t0:128 + t0 + n],
                                            vp=vtk[:, bi, :], vc=vtk[0:n, 1 + bi, :], op=(onesFl if bi == 0 else onesB),
                                            rk=[('vtk', bi), ('vtk', 1 + bi), ('onesFl',), ('onesB',)]))
                    for s in range(NSEQ):
                        t0 = TP + 8 * s
                        ablocks.append(dict(t0=t0, nq=8, kp=kds[:, s, :], kc=kd[:, 128 + t0:128 + t0 + 8],
                                            vp=vtks[:, s, :], vc=vtkn[0:8, s, :], op=onesB,
                                            rk=[('kds', s), ('vtks',), ('vtkn', s), ('onesB',)]))
                    for ab in ablocks:
                        t0, nq = ab['t0'], ab['nq']

                        def fs(ab=ab, t0=t0, nq=nq):
                            for j in range(4):
                                jj, hf = j // 2, j % 2
                                bank = 5 + hf
                                base = jj * 256
                                PE.matmul(PS[bank][:, base:base + nq], lhsT=ab['kp'][64 * hf:64 * hf + 64, :], rhs=qa[64 * hf:64 * hf + 64, jj, t0:t0 + nq],
                                          start=True, stop=True)
                                last = PE.matmul(PS[bank][0:nq, base + 128:base + 128 + nq], lhsT=ab['kc'][64 * hf:64 * hf + 64, :],
                                                 rhs=qa[64 * hf:64 * hf + 64, jj, t0:t0 + nq], start=True, stop=True)
                            return last
                        S.add('pe', fs, reads=KD_ALL + QA_ALL + ab['rk'], writes=[('ps', 5), ('ps', 6)])
                        for jj in range(2):
                            bank = 5 + jj
                            S.add('act', lambda jj=jj, bank=bank, nq=nq: A.activation(
                                out=Eb[:, 2 * jj:2 * jj + 2, 0, 0:nq], in_=PS[bank][:, :].rearrange("p (h c t) -> p h c t", h=2, c=2)[:, :, 0, 0:nq],
                                func=AF.Exp, scale=0.125), reads=[('ps', bank)], writes=[('E', jj, 0)])
                            S.add('act', lambda jj=jj, bank=bank, nq=nq: A.activation(
                                out=Eb[0:nq, 2 * jj:2 * jj + 2, 1, 0:nq], in_=PS[bank][:, :].rearrange("p (h c t) -> p h c t", h=2, c=2)[0:nq, :, 1, 0:nq],
                                func=AF.Exp, scale=0.125), reads=[('ps', bank)], writes=[('E', jj, 1)])
                        emv = emg[:, :, :].rearrange("p h (c t) -> p h c t", c=2)
                        S.add('dve', lambda nq=nq, emv=emv: V.tensor_tensor(out=Pb[:, :, 0, 0:nq], in0=Eb[:, :, 0, 0:nq], in1=emv[:, :, 0, 0:nq], op=ALU.mult),
                              reads=[('E', 0, 0), ('E', 1, 0), ('emg',)], writes=[('P', 0)])
                        S.add('dve', lambda nq=nq, emv=emv: V.tensor_tensor(out=Pb[0:nq, :, 1, 0:nq], in0=Eb[0:nq, :, 1, 0:nq], in1=emv[0:nq, :, 1, 0:nq], op=ALU.mult),
                              reads=[('E', 0, 1), ('E', 1, 1), ('emg',)], writes=[('P', 1)])

                        def fpv(ab=ab, nq=nq):
                            for j in range(4):
                                PE.matmul(PS[3][:, 128 * j:128 * j + nq], lhsT=ab['vp'], rhs=Pb[:, j, 0, 0:nq], start=True, stop=False)
                                PE.matmul(PS[3][:, 128 * j:128 * j + nq], lhsT=ab['vc'], rhs=Pb[0:nq, j, 1, 0:nq], start=False, stop=True)
                            for j in range(4):
                                PE.matmul(PS[7][:, 128 * j:128 * j + nq], lhsT=ab['op'][:, :], rhs=Pb[:, j, 0, 0:nq], start=True, stop=False)
                                last = PE.matmul(PS[7][:, 128 * j:128 * j + nq], lhsT=onesB[0:nq, :], rhs=Pb[0:nq, j, 1, 0:nq], start=False, stop=True)
                            return last
                        S.add('pe', fpv, reads=[('P', 0), ('P', 1)] + ab['rk'], writes=[('ps', 3), ('ps', 7)])
                        es = esink[:, l * 16 + 4 * g:l * 16 + 4 * g + 4]
                        S.add('dve', lambda nq=nq, es=es: V.tensor_tensor(out=den[:, :, 0:nq], in0=PS[7][:, :].rearrange("p (h t) -> p h t", h=4)[:, :, 0:nq],
                                                                          in1=es.rearrange("p (h o) -> p h o", o=1).to_broadcast([128, 4, nq]),
                                                                          op=ALU.add),
                              reads=[('ps', 7), ('esink',)], writes=[('den',)])
                        S.add('dve', lambda nq=nq: V.reciprocal(out=den[:, :, 0:nq], in_=den[:, :, 0:nq]), reads=[('den',)], writes=[('den',)])
                        S.add('dve', lambda nq=nq: V.tensor_tensor(out=onr[:, :, 0:nq], in0=PS[3][:, :].rearrange("p (h t) -> p h t", h=4)[:, :, 0:nq],
                                                                   in1=den[:, :, 0:nq], op=ALU.mult),
                              reads=[('ps', 3), ('den',)], writes=[('onr',)])
                        tcs = [i for i, (a, m) in enumerate(TCH) if a < t0 + nq and t0 < a + m]
                        for hf in range(2):
                            S.add('dve', lambda g=g, hf=hf, t0=t0, nq=nq: V.tensor_tensor(
                                out=mix[64 * hf:64 * hf + 64, 2 * g:2 * g + 2, t0:t0 + nq],
                                in0=onr[64 * hf:64 * hf + 64, 2 * hf:2 * hf + 2, 0:nq],
                                in1=sga[64 * hf:64 * hf + 64, :, t0:t0 + nq], op=ALU.mult),
                                reads=[('onr',)] + SGA_ALL, writes=[('mix', 2 * g + jj, i) for jj in range(2) for i in tcs])
                if not _on('out'):
                    break
                MIX_ALL = [('mix', c, i) for c in range(KC) for i in range(4)]
                for n_ in range(KC):
                    slot = w_next()
                    for i, (t0, n) in enumerate(TCH):
                        b = next_pb()

                        def fo2(slot=slot, t0=t0, n=n, b=b):
                            for c in range(KC):
                                last = PE.matmul(PS[b][:, 0:n], lhsT=wbuf[:, slot, c, :], rhs=mix[:, c, t0:t0 + n], start=(c == 0), stop=(c == KC - 1))
                            return last
                        S.add('pe', fo2, reads=[('w', slot)] + [('mix', c, i) for c in range(KC)], writes=[('ps', b)])
                        S.add('dve', lambda n_=n_, t0=t0, n=n, b=b: V.tensor_tensor(out=hT[:, n_, t0:t0 + n], in0=hT[:, n_, t0:t0 + n], in1=PS[b][:, 0:n], op=ALU.add),
                              reads=[('ps', b), ('h', n_, i)], writes=[('h', n_, i)])
                barrier(extra_writes=[('kd', 'halo'), ('vtk', 0), ('vtks',), ('ckst', 0), ('ckst', 1), ('emg',)])

            if _on('fin'):
                rtk = do_norm(fw, 0)
                barrier(extra_writes=[('sq', 0), ('sq', 1)])
                for ti, (t0, n) in enumerate(TT):
                    sl = ti % 2
                    for kq in range(4):
                        b = 5 + (ti * 4 + kq) % 3
                        for j in range(4):
                            k = kq * 4 + j
                            yp = (ti * 16 + k) % 2
                            S.add('dve', lambda k=k, t0=t0, n=n, yp=yp: V.scalar_tensor_tensor(out=ynt[:, yp, 0:n], in0=hT[:, k, t0:t0 + n], scalar=fw[:, k:k + 1],
                                                                                              in1=rt[:, t0:t0 + n], op0=ALU.mult, op1=ALU.mult),
                                  reads=hkeys(k, t0, n) + rtk + [('fw',)], writes=[('ynt', yp)])
                            S.add('pe', lambda j=j, n=n, yp=yp, b=b: PE.transpose(out=PS[b][0:n, j * 128:(j + 1) * 128], in_=ynt[:, yp, 0:n], identity=identF[:, :]),
                                  reads=[('ynt', yp), ('identF',)], writes=[('ps', b)])
                        if kq % 2 == 0:
                            S.add('act', lambda sl=sl, kq=kq, n=n, b=b: A.copy(out=xst[0:n, sl, kq * 512:(kq + 1) * 512], in_=PS[b][0:n, :]),
                                  reads=[('ps', b)], writes=[('xst', sl)])
                        else:
                            S.add('dve', lambda sl=sl, kq=kq, n=n, b=b: V.tensor_copy(out=xst[0:n, sl, kq * 512:(kq + 1) * 512], in_=PS[b][0:n, :]),
                                  reads=[('ps', b)], writes=[('xst', sl)])
                    S.add('sp', lambda sl=sl, t0=t0, n=n: nc.sync.dma_start(out=y_out[t0:t0 + n, :], in_=xst[0:n, sl, :]),
                          reads=[('xst', sl)], dma=('o_y', sl))

        for p_ in range(2):
            run_pass(p_, xin_p[p_], ck_p[p_], cv_p[p_], st_p[p_], y_p[p_], pk_p[p_], pv_p[p_], ps_p[p_], sk_p[p_], sv_p[p_], ss_p[p_], send_p[p_], send_p[0])

        S.emit([('o_y', 0), ('o_y', 1), ('o_pk',), ('o_sk',), ('o_ss', 0), ('o_ss', 1), ('snd',), ('emd',)])
    return nc


def _t5_bucket(dist):
    d = np.maximum(dist, 0)
    df = np.maximum(d, 1).astype(np.float32)
    large = 16 + (np.log(df / np.float32(16)) / np.float32(np.log(128 / 16)) * np.float32(16)).astype(np.int32)
    large = np.minimum(large, 31)
    return np.where(d < 16, d, large)


_NC_CACHE = {}


def kernel(x_prompt, x_sample, cache_k, cache_v, state_h, meta_tokens, w_in, w_out, norm_w,
           final_norm_w, attn_sinks, rel_bias_table, hgrn_lb_logits, hgrn_norm_w, _nl=DEPTH):
    f32 = np.float32
    x_prompt = np.asarray(x_prompt, f32)
    x_sample = np.asarray(x_sample, f32)
    cache_k = np.asarray(cache_k, f32).reshape(DEPTH, 32, 128, 256)
    cache_v = np.asarray(cache_v, f32).reshape(DEPTH, 32, 128, 256)
    state_h = np.asarray(state_h, f32)
    w_in = np.asarray(w_in, f32)
    w_out = np.asarray(w_out, f32)
    rel = np.asarray(rel_bias_table, f32)

    def pk(a, n):
        a = np.asarray(a, f32).reshape(-1, n, 128)
        return np.ascontiguousarray(a.transpose(2, 0, 1).reshape(128, -1))

    nw = pk(norm_w, KC)
    fw = pk(final_norm_w, KC)
    lbl = pk(hgrn_lb_logits, 8)
    hw = pk(hgrn_norm_w, 8)
    perm = np.array([4 * g_ + 2 * (jp_ % 2) + jp_ // 2 for g_ in range(4) for jp_ in range(4)])
    sink = np.ascontiguousarray(np.broadcast_to(np.asarray(attn_sinks, f32)[:, perm].reshape(1, -1), (128, DEPTH * 16)))
    s_ = np.arange(128)[:, None]
    t_ = np.arange(128)[None, :]
    dprev = np.clip(128 + t_ - s_, 0, 127)
    dcur = np.clip(t_ - s_, 0, 127)
    bidx = _t5_bucket(np.arange(128))
    bias_d = rel[bidx]
    biasT = np.empty((16, 128, 256), f32)
    biasT[:, :, 0:128] = bias_d[dprev].transpose(2, 0, 1)
    biasT[:, :, 128:256] = bias_d[dcur].transpose(2, 0, 1)
    mask01 = np.zeros((128, 256), f32)
    mask01[:, 0:128] = (s_ > t_)
    mask01[:, 128:256] = (s_ <= t_)
    tril = np.zeros((128, 64), f32)
    tril[0:64, :] = (np.arange(64)[:, None] <= np.arange(64)[None, :])
    seg01 = np.ones((128, T), f32)
    for s in range(NSEQ):
        seg01[:, TP + 8 * s] = 0.0
    identf = np.eye(128, dtype=f32)

    NCORE = int(os.environ.get('KNCORE', '4'))
    wi = w_in[:_nl] if _nl < DEPTH else w_in
    wo = w_out[:_nl] if _nl < DEPTH else w_out
    if os.environ.get('KSMALLW', '0') == '1':
        wi = np.ascontiguousarray(w_in[:1, :128, :128])
        wo = np.ascontiguousarray(w_out[:1, :128, :128])
    in_maps = []
    for c in range(NCORE):
        full = np.concatenate([np.asarray(meta_tokens, f32), x_prompt[c]], axis=0)
        m = dict(w_in=wi, w_out=wo, nw=nw, fw=fw, sink=sink, biasT=biasT, mask01=mask01, tril=tril,
                 seg01=seg01, lbl=lbl, hw=hw, identf=identf)
        for p in range(2):
            s0 = 8 * c + 4 * p
            m["xin%d" % p] = np.ascontiguousarray(np.concatenate([full[p * TP:(p + 1) * TP], x_sample[s0:s0 + 4].reshape(TS, D)], axis=0))
            m["ck%d" % p] = np.ascontiguousarray(cache_k[:, s0:s0 + 4])
            m["cv%d" % p] = np.ascontiguousarray(cache_v[:, s0:s0 + 4])
            m["st%d" % p] = np.ascontiguousarray(state_h[:, s0:s0 + 4])
        in_maps.append(m)
    if _nl not in _NC_CACHE:
        _NC_CACHE[_nl] = build_nc(_nl)
    nc = _NC_CACHE[_nl]
    res = run_bass_kernel_spmd(nc, in_maps, core_ids=list(range(NCORE)))
    R = res.results
    y_prompt = np.empty((4, 2048, D), f32)
    y_sample = np.empty((32, 8, D), f32)
    pkk = np.empty((DEPTH, 4, 128, 4, 64), f32)
    pvv = np.empty((DEPTH, 4, 128, 4, 64), f32)
    pss = np.empty((DEPTH, 4, 8, 128, 128), f32)
    skk = np.empty((DEPTH, 32, 128, 4, 64), f32)
    svv = np.empty((DEPTH, 32, 128, 4, 64), f32)
    sss = np.empty((DEPTH, 32, 8, 128, 128), f32)
    for c in range(NCORE):
        y0, y1 = R[c]["y0"], R[c]["y1"]
        y_prompt[c, 0:TP - 16] = y0[16:TP]
        y_prompt[c, TP - 16:] = y1[0:TP]
        pkk[:, c] = R[c]["pk1"].reshape(DEPTH, 128, 4, 64)
        pvv[:, c] = R[c]["pv1"].reshape(DEPTH, 128, 4, 64)
        pss[:, c] = R[c]["ps1"]
        for p in range(2):
            s0 = 8 * c + 4 * p
            y_sample[s0:s0 + 4] = R[c]["y%d" % p][TP:].reshape(4, 8, D)
            skk[:, s0:s0 + 4] = R[c]["sk%d" % p].reshape(DEPTH, 4, 128, 4, 64)
            svv[:, s0:s0 + 4] = R[c]["sv%d" % p].reshape(DEPTH, 4, 128, 4, 64)
            sss[:, s0:s0 + 4] = R[c]["ss%d" % p]
    return (y_prompt, y_sample, pkk, pvv, pss, skk, svv, sss)
```

```python
import contextlib
import numpy as np
import concourse.bass as bass
import concourse.mybir as mybir
from concourse.bass_utils import run_bass_kernel_spmd

F32 = mybir.dt.float32
BF16 = mybir.dt.bfloat16
AF = mybir.ActivationFunctionType
ALU = mybir.AluOpType

DEPTH = 4
D = 2048
KC = 16
TP = 1032
NSEQ = 4
TS = 32
T = TP + TS
DIN = 6656
TCH = [(0, 320), (320, 320), (640, 320), (960, 104)]
EPS = 1e-6
NS = 3
C_QA, C_KA, C_VA, C_GA, C_QR, C_FR, C_IR, C_GR = 0, 1024, 1280, 1536, 2560, 3584, 4608, 5632
SEND_ROWS = 1536
import os
USE_CC = os.environ.get('KNOCC', '0') != '1'
STOP = os.environ.get('KSTOP', '')
P1L = int(os.environ.get('KP1', '9'))
PH = ['const', 'p0', 'norm', 'p15', 'p1', 'ex', 'p2', 'p3', 'out', 'fin']


def _on(ph):
    return STOP == '' or PH.index(ph) <= PH.index(STOP)


class Sched:
    def __init__(self, nc, stack):
        self.nc = nc
        self.stack = stack
        self.ops = []
        self.lastw = {}
        self.rds = {}
        self.eng = {'pe': nc.tensor, 'act': nc.scalar, 'dve': nc.vector, 'pool': nc.gpsimd, 'sp': nc.sync}
        self.sem = {e: stack.enter_context(nc.semaphore("s_" + e)) for e in self.eng}
        self.dsem = {}

    def add(self, eng, fn, reads=(), writes=(), dma=None):
        idx = len(self.ops)
        psr = [r for r in reads if r[0] == 'ps']
        if psr:
            reads = [r for r in reads if r[0] != 'ps']
            writes = list(writes) + [r for r in psr if r not in writes]
        deps = set()
        for r in reads:
            w = self.lastw.get(r)
            if w is not None:
                deps.add(w)
        for r in writes:
            w = self.lastw.get(r)
            if w is not None:
                deps.add(w)
            for x in self.rds.get(r, ()):
                deps.add(x)
        deps.discard(idx)
        if dma is not None and dma not in self.dsem:
            self.dsem[dma] = self.stack.enter_context(self.nc.semaphore("d%d" % len(self.dsem)))
        self.ops.append(dict(eng=eng, fn=fn, deps=deps, dma=dma, signal=False, sidx=0))
        for r in reads:
            self.rds.setdefault(r, []).append(idx)
        for r in writes:
            self.lastw[r] = idx
            self.rds[r] = []
        return idx

    def emit(self, final_slots):
        ops = self.ops

        def skip(p, op):
            return p['dma'] is None and op['dma'] is None and p['eng'] == 'pe' and op['eng'] == 'pe'

        for op in ops:
            for d in op['deps']:
                if not skip(ops[d], op):
                    ops[d]['signal'] = True
        cnt = {e: 0 for e in self.eng}
        dcnt = {}
        waited = {}
        for op in ops:
            e = op['eng']
            E = self.eng[e]
            need = {}
            for d in op['deps']:
                p = ops[d]
                if skip(p, op):
                    continue
                if p['dma'] is not None:
                    key = ('d', p['dma'])
                    val = 16 * dcnt[p['dma']]
                else:
                    key = ('e', p['eng'])
                    val = p['sidx']
                if val > need.get(key, 0):
                    need[key] = val
            for key, val in need.items():
                if waited.get((e, key), 0) >= val:
                    continue
                sem = self.dsem[key[1]] if key[0] == 'd' else self.sem[key[1]]
                E.wait_ge(sem, val)
                waited[(e, key)] = val
            inst = op['fn']()
            if op['dma'] is not None:
                inst.then_inc(self.dsem[op['dma']], 16)
                dcnt[op['dma']] = dcnt.get(op['dma'], 0) + 1
            elif op['signal']:
                cnt[e] += 1
                inst.then_inc(self.sem[e], 1)
                op['sidx'] = cnt[e]
        sp = self.eng['sp']
        for slot in final_slots:
            if slot in dcnt:
                sp.wait_ge(self.dsem[slot], 16 * dcnt[slot])


def build_nc(nl=DEPTH):
    nc = bass.Bass("TRN2", target_bir_lowering=False)
    dt_in = lambda name, shape: nc.dram_tensor(name, list(shape), F32, kind="ExternalInput").ap()
    dt_out = lambda name, shape: nc.dram_tensor(name, list(shape), F32, kind="ExternalOutput").ap()
    xin_p = [dt_in("xin%d" % p, (T, D)) for p in range(2)]
    ck_p = [dt_in("ck%d" % p, (DEPTH, NSEQ, 128, 256)) for p in range(2)]
    cv_p = [dt_in("cv%d" % p, (DEPTH, NSEQ, 128, 256)) for p in range(2)]
    st_p = [dt_in("st%d" % p, (DEPTH, NSEQ, 8, 128, 128)) for p in range(2)]
    SMALLW = os.environ.get("KSMALLW", "0") == "1"
    w_in = dt_in("w_in", (1, 128, 128) if SMALLW else (nl, D, DIN))
    w_out = dt_in("w_out", (1, 128, 128) if SMALLW else (nl, D, D))
    nw_in = dt_in("nw", (128, DEPTH * KC))
    fw_in = dt_in("fw", (128, KC))
    sink_in = dt_in("sink", (128, DEPTH * 16))
    biasT_in = dt_in("biasT", (16, 128, 256))
    mask_in = dt_in("mask01", (128, 256))
    tril_in = dt_in("tril", (128, 64))
    seg_in = dt_in("seg01", (128, T))
    lbl_in = dt_in("lbl", (128, DEPTH * 8))
    hw_in = dt_in("hw", (128, DEPTH * 8))
    identf_in = dt_in("identf", (128, 128))

    y_p = [dt_out("y%d" % p, (T, D)) for p in range(2)]
    pk_p = [dt_out("pk%d" % p, (DEPTH, 128, 256)) for p in range(2)]
    pv_p = [dt_out("pv%d" % p, (DEPTH, 128, 256)) for p in range(2)]
    ps_p = [dt_out("ps%d" % p, (DEPTH, 8, 128, 128)) for p in range(2)]
    sk_p = [dt_out("sk%d" % p, (DEPTH, NSEQ, 128, 256)) for p in range(2)]
    sv_p = [dt_out("sv%d" % p, (DEPTH, NSEQ, 128, 256)) for p in range(2)]
    ss_p = [dt_out("ss%d" % p, (DEPTH, NSEQ, 8, 128, 128)) for p in range(2)]
    send_p = [[nc.dram_tensor("send%d_%d" % (p, l), [SEND_ROWS, 128], F32).ap() for l in range(DEPTH)] for p in range(2)]
    emd = nc.dram_tensor("emd", [16, 128, 256], BF16).ap()

    stack = contextlib.ExitStack()
    with stack:
        S = Sched(nc, stack)
        cur = [16512]
        TOP = 229344
        nid = [0]

        def alloc(shape, dtype, at=None):
            isz = 4 if dtype == F32 else 2
            n = 1
            for s in shape[1:]:
                n *= s
            size = (n * isz + 31) // 32 * 32
            if at is None:
                off = cur[0]
                cur[0] += size
                assert cur[0] <= TOP, ("SBUF overflow", cur[0] - TOP)
            else:
                off = at
            nid[0] += 1
            return nc.alloc_sbuf_tensor_at("t%d" % nid[0], list(shape), dtype, offset=off)

        hT = alloc([128, KC, T], F32)
        xn = alloc([128, KC, T], BF16)
        mix = alloc([128, KC, T], BF16)
        XB = cur[0]
        cur[0] += 34048
        wbuf = alloc([128, NS, KC, 128], BF16)
        emg = alloc([128, 4, 256], BF16)
        identB = alloc([128, 128], BF16)
        identF = alloc([128, 128], F32)
        onesB = alloc([128, 128], BF16)
        onesFl = alloc([128, 128], BF16)
        tril = alloc([128, 64], F32)
        seg01 = alloc([128, T], BF16)
        nw = alloc([128, DEPTH * KC], F32)
        fw = alloc([128, KC], F32)
        lb = alloc([128, DEPTH * 8], F32)
        oml = alloc([128, DEPTH * 8], F32)
        noml = alloc([128, DEPTH * 8], F32)
        hw = alloc([128, DEPTH * 8], F32)
        esink = alloc([128, DEPTH * 16], F32)
        flag = alloc([128, 1], F32)
        epsT = alloc([128, 1], F32)
        zcol = alloc([128, 1], F32)
        carry = alloc([128, 1], F32)
        Dsave = alloc([128, 8], F32)
        YB = cur[0]
        yf = [alloc([128, 321], F32) for _ in range(6)]
        qs_t, sg_t, la_t, gx_t, eg_t, en_t = yf
        vT_t, qt_t, kt_t, kh_t = [alloc([128, 320], BF16) for _ in range(4)]
        khT = alloc([128, 2, 128], BF16)
        vtk2 = alloc([128, 2, 128], BF16)
        att = alloc([128, 2, 64], BF16)
        Sst = alloc([128, 128], F32)
        Sbf = alloc([128, 128], BF16)
        S0 = alloc([128, 4, 128], F32)
        sgr = alloc([128, T], BF16)
        sfst = alloc([128, 2, 128], F32)
        YEND = cur[0]
        print("SBUF used", cur[0] - 16512, "free", TOP - cur[0])
        kv32 = alloc([128, 4, 160], F32, at=YB)
        tokp = alloc([128, 512], F32, at=YB + 2560)
        toks = alloc([128, 512], F32, at=YB + 2560 + 2048)
        assert YB + 2560 + 4096 <= YEND
        rt = alloc([128, T], F32, at=YB)
        den = alloc([128, 4, 128], F32, at=YB)
        onr = alloc([128, 4, 128], F32, at=YB + 2048)
        ynt = alloc([128, 2, 128], F32, at=YB + 4288)
        oloc = alloc([128, 8, T], BF16, at=XB)
        Qd = alloc([128, 8, T], BF16, at=XB + 17024)
        sqtmp = alloc([128, 2, T], BF16, at=XB)
        xst = alloc([128, 2, 2048], F32, at=XB)
        emst_f = alloc([128, 2, 256], F32, at=XB + 16384)
        emst_t = alloc([128, 256], F32, at=XB + 16384 + 2048)
        emst_b = alloc([128, 2, 256], BF16, at=XB + 16384 + 3072)
        mask01 = alloc([128, 256], F32, at=XB + 16384 + 4096)
        lbtmp = alloc([128, DEPTH * 8], F32, at=XB + 16384 + 5120)
        lbsum = alloc([128, 8], F32, at=XB + 16384 + 5120 + 128)
        xo = [XB]

        def xalloc(shape, dtype):
            t_ = alloc(shape, dtype, at=xo[0])
            n = 1
            for s in shape[1:]:
                n *= s
            xo[0] += (n * (4 if dtype == F32 else 2) + 31) // 32 * 32
            assert xo[0] <= XB + 34048
            return t_

        qa = xalloc([128, 2, T], BF16)
        kd = xalloc([128, 128 + T], BF16)
        vdT = xalloc([128, T], BF16)
        sga = xalloc([128, 2, T], BF16)
        vtk = xalloc([128, 10, 128], BF16)
        kds = xalloc([128, NSEQ, 128], BF16)
        vtks = xalloc([128, NSEQ, 128], BF16)
        vtkn = xalloc([128, NSEQ, 128], BF16)
        ckst = xalloc([128, 2, 128], BF16)
        Eb = xalloc([128, 4, 2, 128], BF16)
        Pb = xalloc([128, 4, 2, 128], BF16)

        PS = []
        for b in range(8):
            if b == 4:
                PS.append(stack.enter_context(nc.psum_tensor("psb%d" % b, [128, 1024], BF16)))
            else:
                PS.append(stack.enter_context(nc.psum_tensor("psb%d" % b, [128, 512], F32)))
        pbrot = [0]

        def next_pb():
            b = pbrot[0] % 3
            pbrot[0] += 1
            return b

        V = nc.vector
        A = nc.scalar
        PE = nc.tensor

        bscr = alloc([128, 4], F32)

        def barrier(extra_writes=()):
            if os.environ.get('KNOBAR', '0') == '1':
                return
            S.add('act', lambda: A.copy(out=bscr[:, 0:1], in_=zcol[:, 0:1]), reads=[('zc',)], writes=[('bar', 'act')] + list(extra_writes))
            S.add('dve', lambda: V.tensor_copy(out=bscr[:, 1:2], in_=zcol[:, 0:1]), reads=[('zc',)], writes=[('bar', 'dve')])
            S.add('pe', lambda: PE.matmul(PS[7][0:1, 0:1], lhsT=identB[:, 0:1], rhs=identB[:, 0:1], start=True, stop=True),
                  reads=[('identB',)], writes=[('bar', 'pe'), ('ps', 7)])
            allb = [('bar', 'act'), ('bar', 'dve'), ('bar', 'pe')]
            S.add('act', lambda: A.copy(out=bscr[:, 2:3], in_=zcol[:, 0:1]), reads=allb + [('zc',)], writes=[('barj', 'act')])
            S.add('dve', lambda: V.tensor_copy(out=bscr[:, 3:4], in_=zcol[:, 0:1]), reads=allb + [('zc',)], writes=[('barj', 'dve')])
            S.add('pe', lambda: PE.matmul(PS[7][0:1, 0:1], lhsT=identB[:, 0:1], rhs=identB[:, 0:1], start=True, stop=True),
                  reads=allb + [('identB',)], writes=[('barj', 'pe'), ('ps', 7)])

        def ld(dst, src, key, q='sp'):
            eng = nc.sync if q == 'sp' else nc.gpsimd
            S.add(q, lambda: eng.dma_start(out=dst, in_=src), writes=[key], dma=('c', key))

        ld(identF[:, :], identf_in[:, :], ('identF',))
        ld(identB[:, :], identf_in[:, :], ('identB',), q='pool')
        ld(tril[:, :], tril_in[:, :], ('tril',))
        ld(seg01[:, :], seg_in[:, :], ('seg01',), q='pool')
        ld(nw[:, :], nw_in[:, :], ('nw',))
        ld(fw[:, :], fw_in[:, :], ('fw',))
        ld(hw[:, :], hw_in[:, :], ('hw',))
        ld(esink[:, :], sink_in[:, :], ('esink',))
        ld(lbtmp[:, :], lbl_in[:, :], ('lbtmp',))
        ld(mask01[:, :], mask_in[:, :], ('mask01',))
        S.add('dve', lambda: V.memset(zcol[:, :], 0.0), writes=[('zc',)])
        S.add('dve', lambda: V.memset(epsT[:, :], EPS), writes=[('eps',)])
        S.add('dve', lambda: V.memset(onesB[:, :], 1.0), writes=[('onesB',)])
        S.add('act', lambda: A.activation(out=esink[:, :], in_=esink[:, :], func=AF.Exp), reads=[('esink',)], writes=[('esink',)])
        S.add('act', lambda: A.activation(out=lbtmp[:, :], in_=lbtmp[:, :], func=AF.Exp), reads=[('lbtmp',)], writes=[('lbtmp',)])

        lb_steps = []

        def lb_chain():
            steps = [
                (lambda: V.tensor_tensor(out=lbsum[:, :], in0=lbtmp[:, 0:8], in1=lbtmp[:, 8:16], op=ALU.add)),
                (lambda: V.tensor_tensor(out=lbsum[:, :], in0=lbsum[:, :], in1=lbtmp[:, 16:24], op=ALU.add)),
                (lambda: V.tensor_tensor(out=lbsum[:, :], in0=lbsum[:, :], in1=lbtmp[:, 24:32], op=ALU.add)),
                (lambda: V.reciprocal(out=lbsum[:, :], in_=lbsum[:, :])),
                (lambda: V.memset(lb[:, 0:8], 0.0)),
                (lambda: V.tensor_tensor(out=lb[:, 8:16], in0=lbtmp[:, 8:16], in1=lbsum[:, :], op=ALU.mult)),
                (lambda: V.tensor_tensor(out=lb[:, 16:24], in0=lbtmp[:, 16:24], in1=lbsum[:, :], op=ALU.mult)),
                (lambda: V.tensor_tensor(out=lb[:, 24:32], in0=lbtmp[:, 24:32], in1=lbsum[:, :], op=ALU.mult)),
                (lambda: V.tensor_tensor(out=lb[:, 16:24], in0=lb[:, 16:24], in1=lb[:, 8:16], op=ALU.add)),
                (lambda: V.tensor_tensor(out=lb[:, 24:32], in0=lb[:, 24:32], in1=lb[:, 16:24], op=ALU.add)),
                (lambda: V.tensor_scalar(out=noml[:, :], in0=lb[:, :], scalar1=-1.0, scalar2=None, op0=ALU.add)),
                (lambda: V.tensor_scalar(out=oml[:, :], in0=noml[:, :], scalar1=-1.0, scalar2=None, op0=ALU.mult)),
            ]
            for f in steps:
                S.add('dve', f, reads=[('lbtmp',), ('lbc',)], writes=[('lbc',)])

        lb_chain()

        for h in range(16 if os.environ.get('KNOEM', '0') != '1' else 0):
            sl = h % 2
            S.add('sp', lambda h=h, sl=sl: nc.sync.dma_start(out=emst_f[:, sl, :], in_=biasT_in[h, :, :]),
                  writes=[('emf', sl)], dma=('emf', sl))
            S.add('act', lambda sl=sl: A.activation(out=emst_t[:, :], in_=emst_f[:, sl, :], func=AF.Exp),
                  reads=[('emf', sl)], writes=[('emt',)])
            S.add('dve', lambda sl=sl: V.tensor_tensor(out=emst_b[:, sl, :], in0=emst_t[:, :], in1=mask01[:, :], op=ALU.mult),
                  reads=[('emt',), ('mask01',)], writes=[('emb', sl)])
            S.add('sp', lambda h=h, sl=sl: nc.sync.dma_start(out=emd[h, :, :], in_=emst_b[:, sl, :]),
                  reads=[('emb', sl)], writes=[('emd',)], dma=('emd',))
        barrier(extra_writes=[('emb', 0), ('emb', 1), ('emf', 0), ('emf', 1), ('mask01',), ('lbtmp',)])

        def hkeys(k, t0=0, n=T):
            return [('h', k, i) for i, (a, m) in enumerate(TCH) if a < t0 + n and t0 < a + m]

        TT = [(i * 128, 128) for i in range(8)] + [(1024, 40)]
        units = []
        for l in range(nl):
            for u in range(4):
                units.append(('in', l, C_KA + 128 * u))
            for h in range(8):
                units.append(('in', l, C_QR + 128 * h))
                units.append(('in', l, C_FR + 128 * h))
                units.append(('in', l, C_IR + 128 * h))
            for h in range(8):
                units.append(('in', l, C_GR + 128 * h))
            for g in range(4):
                units.append(('in', l, C_QA + 256 * g))
                units.append(('in', l, C_QA + 256 * g + 128))
                units.append(('dup', l, C_KA + 64 * g))
                units.append(('dup', l, C_VA + 64 * g))
                units.append(('in', l, C_GA + 256 * g))
                units.append(('in', l, C_GA + 256 * g + 128))
            for n_ in range(16):
                units.append(('out', l, 128 * n_))
        units = units + units
        wst = dict(next=0, issued=0)

        def w_issue(i):
            kind, l, c0 = units[i]
            slot = i % NS
            if kind == 'in':
                src = w_in[l].rearrange("(k p) c -> p k c", p=128)[:, :, c0:c0 + 128]
                S.add('pool', lambda slot=slot, src=src: nc.gpsimd.dma_start(out=wbuf[:, slot, :, :], in_=src),
                      writes=[('w', slot)], dma=('w', slot))
            elif kind == 'out':
                src = w_out[l].rearrange("(k p) c -> p k c", p=128)[:, :, c0:c0 + 128]
                S.add('pool', lambda slot=slot, src=src: nc.gpsimd.dma_start(out=wbuf[:, slot, :, :], in_=src),
                      writes=[('w', slot)], dma=('w', slot))
            else:
                src = w_in[l].rearrange("(k p) c -> p k c", p=128)[:, :, c0:c0 + 64]
                S.add('pool', lambda slot=slot, src=src: nc.gpsimd.dma_start(out=wbuf[:, slot, :, 0:64], in_=src),
                      writes=[('w', slot)], dma=('w', slot))
                S.add('pool', lambda slot=slot, src=src: nc.gpsimd.dma_start(out=wbuf[:, slot, :, 64:128], in_=src),
                      writes=[('w', slot)], dma=('w', slot))

        def w_release():
            done = wst['next']
            while wst['issued'] < min(done + NS, len(units)):
                w_issue(wst['issued'])
                wst['issued'] += 1

        def w_begin(k=1):
            w_release()
            first = wst['next']
            wst['next'] += k
            assert wst['issued'] >= wst['next'], (wst, k)
            return [(first + j) % NS for j in range(k)]

        def w_next():
            return w_begin(1)[0]

        XN_ALL = [('xn', k) for k in range(KC)]
        BARK = [('barj', 'act'), ('barj', 'dve'), ('barj', 'pe')]
        RECV_KEYS = [('send', x_) for x_ in list(range(8)) + ['k', 'v']]

        def proj(slot, t0, n, b):
            def f():
                for k in range(KC):
                    last = PE.matmul(PS[b][:, 0:n], lhsT=wbuf[:, slot, k, :], rhs=xn[:, k, t0:t0 + n],
                                     start=(k == 0), stop=(k == KC - 1))
                return last
            S.add('pe', f, reads=[('w', slot)] + XN_ALL, writes=[('ps', b)])

        def do_norm(wt, wcol0):
            banks = [0, 1, 2, 5]
            for k in range(KC):
                sl = k % 2
                S.add('act', lambda k=k, sl=sl: A.activation(out=sqtmp[:, sl, :], in_=hT[:, k, :], func=AF.Square),
                      reads=hkeys(k), writes=[('sq', sl)])
                for i, (t0, n) in enumerate(TCH):
                    S.add('pe', lambda k=k, sl=sl, i=i, t0=t0, n=n: PE.matmul(PS[banks[i]][:, 0:n], lhsT=onesB[:, :], rhs=sqtmp[:, sl, t0:t0 + n],
                                                                             start=(k == 0), stop=(k == KC - 1)),
                          reads=[('sq', sl), ('onesB',)], writes=[('ps', banks[i])])
            for i, (t0, n) in enumerate(TCH):
                S.add('act', lambda i=i, t0=t0, n=n: A.activation(out=rt[:, t0:t0 + n], in_=PS[banks[i]][:, 0:n], func=AF.Sqrt,
                                                                  bias=epsT[:, 0:1], scale=1.0 / D),
                      reads=[('ps', banks[i]), ('eps',)], writes=[('rt', i)])
                S.add('dve', lambda t0=t0, n=n: V.reciprocal(out=rt[:, t0:t0 + n], in_=rt[:, t0:t0 + n]),
                      reads=[('rt', i)], writes=[('rt', i)])
            return [('rt', i) for i in range(4)]

        def tc_of(t0):
            return [i for i, (a, m) in enumerate(TCH) if a == t0][0]

        def run_pass(p, xin, ck_in, cv_in, st_in, y_out, pk_out, pv_out, ps_out, sk_out, sv_out, ss_out, send, recv):
            barrier(extra_writes=[('xst', 0), ('xst', 1), ('flag',), ('onesFl',)])
            S.add('dve', lambda: V.memset(flag[:, :], float(p)), writes=[('flag',)])
            S.add('dve', lambda: V.tensor_scalar(out=onesFl[:, :], in0=onesB[:, :], scalar1=flag[:, 0:1], scalar2=None, op0=ALU.mult),
                  reads=[('onesB',), ('flag',)], writes=[('onesFl',)])
            if _on('p0'):
                for ti, (t0, n) in enumerate(TT):
                    sl = ti % 2
                    S.add('sp', lambda sl=sl, t0=t0, n=n: nc.sync.dma_start(out=xst[0:n, sl, :], in_=xin[t0:t0 + n, :]),
                          reads=BARK, writes=[('xst', sl)], dma=('xst', sl))
                    for kq in range(4):
                        b = 5 + (ti * 4 + kq) % 3

                        def f(sl=sl, n=n, kq=kq, b=b):
                            for j in range(4):
                                k = kq * 4 + j
                                last = PE.transpose(out=PS[b][:, j * 128:j * 128 + n], in_=xst[0:n, sl, k * 128:(k + 1) * 128],
                                                    identity=identF[0:n, 0:n])
                            return last
                        S.add('pe', f, reads=[('xst', sl), ('identF',)], writes=[('ps', b)])
                        hk = []
                        for j in range(4):
                            hk += hkeys(kq * 4 + j, t0, n)
                        cp = (lambda kq=kq, t0=t0, n=n, b=b: A.copy(out=hT[:, kq * 4:kq * 4 + 4, t0:t0 + n],
                                                                   in_=PS[b][:, :].rearrange("p (j c) -> p j c", j=4)[:, :, 0:n]))
                        cpv = (lambda kq=kq, t0=t0, n=n, b=b: V.tensor_copy(out=hT[:, kq * 4:kq * 4 + 4, t0:t0 + n],
                                                                             in_=PS[b][:, :].rearrange("p (j c) -> p j c", j=4)[:, :, 0:n]))
                        if kq % 2 == 0:
                            S.add('act', cp, reads=[('ps', b)], writes=hk)
                        else:
                            S.add('dve', cpv, reads=[('ps', b)], writes=hk)
            barrier(extra_writes=[('xst', 0), ('xst', 1)])

            out_slots = []
            for l in range(nl):
                if not _on('norm'):
                    break
                rtk = do_norm(nw, l * KC)
                for k in range(KC):
                    S.add('dve', lambda k=k, l=l: V.scalar_tensor_tensor(out=xn[:, k, :], in0=hT[:, k, :], scalar=nw[:, l * KC + k:l * KC + k + 1],
                                                                         in1=rt[:, :], op0=ALU.mult, op1=ALU.mult),
                          reads=hkeys(k) + rtk + [('nw',)], writes=[('xn', k)])
                barrier(extra_writes=rtk + [('sq', 0), ('sq', 1)])

                if not _on('p15'):
                    break
                T15 = 904
                for u in range(4):
                    slot = w_next()
                    b = next_pb()
                    proj(slot, T15, 160, b)
                    S.add('act', lambda u=u, b=b: A.copy(out=kv32[:, u, :], in_=PS[b][:, 0:160]), reads=[('ps', b)], writes=[('kv32', u)])
                S.add('sp', lambda l=l: nc.sync.dma_start(out=send[l][1024:1280, :].rearrange("(c p) t -> p c t", p=128), in_=kv32[:, 0:2, 0:128]),
                      reads=[('kv32', 0), ('kv32', 1)], writes=[('send', 'k')], dma=('snd',))

                def f15():
                    for u in range(4):
                        PE.transpose(out=PS[7][:, u * 128:(u + 1) * 128], in_=kv32[:, u, 0:128], identity=identF[:, :])
                    for u in range(4):
                        last = PE.transpose(out=PS[6][0:32, u * 128:(u + 1) * 128], in_=kv32[:, u, 128:160], identity=identF[:, :])
                    return last
                S.add('pe', f15, reads=[('kv32', u) for u in range(4)] + [('identF',)], writes=[('ps', 7), ('ps', 6)])
                S.add('act', lambda: A.copy(out=tokp[:, :], in_=PS[7][:, :]), reads=[('ps', 7)], writes=[('tokp',)])
                S.add('dve', lambda: V.tensor_copy(out=toks[0:32, :], in_=PS[6][0:32, :]), reads=[('ps', 6)], writes=[('toks',)])
                S.add('sp', lambda l=l: nc.sync.dma_start(out=pk_out[l, :, :], in_=tokp[:, 0:256]), reads=[('tokp',)], dma=('o_pk',))
                S.add('sp', lambda l=l: nc.sync.dma_start(out=pv_out[l, :, :], in_=tokp[:, 256:512]), reads=[('tokp',)], dma=('o_pk',))
                S.add('sp', lambda l=l: nc.sync.dma_start(out=send[l][1280:1536, :].rearrange("(c s) f -> s c f", s=128),
                                                          in_=tokp[:, 256:512].rearrange("s (c f) -> s c f", c=2)),
                      reads=[('tokp',)], writes=[('send', 'v')], dma=('snd',))
                for s in range(NSEQ):
                    S.add('sp', lambda l=l, s=s: nc.sync.dma_start(out=sk_out[l, s, 120:128, :], in_=toks[8 * s:8 * s + 8, 0:256]),
                          reads=[('toks',)], dma=('o_sk',))
                    S.add('sp', lambda l=l, s=s: nc.sync.dma_start(out=sv_out[l, s, 120:128, :], in_=toks[8 * s:8 * s + 8, 256:512]),
                          reads=[('toks',)], dma=('o_sk',))
                S.add('sp', lambda l=l: nc.sync.dma_start(out=sk_out[l, :, 0:120, :], in_=ck_in[l, :, 8:128, :]), dma=('o_sk',))
                S.add('sp', lambda l=l: nc.sync.dma_start(out=sv_out[l, :, 0:120, :], in_=cv_in[l, :, 8:128, :]), dma=('o_sk',))
                barrier(extra_writes=[('kv32', u) for u in range(4)] + [('tokp',), ('toks',)])

                if not _on('p1'):
                    break
                for h in range(8):
                    lh = l * 8 + h
                    s_q, s_f, s_i = None, None, None
                    S.add('sp', lambda l=l, h=h: nc.sync.dma_start(out=S0[:, :, :], in_=st_in[l, :, h, :, :].rearrange("s k v -> k s v")),
                          writes=[('S0',)], dma=('S0',))
                    S.add('dve', lambda: V.memset(carry[:, :], 0.0), writes=[('carry',)])
                    slots = w_begin(3)
                    for i, (t0, n) in enumerate(TCH):
                        proj(slots[0], t0, n, 0)
                        proj(slots[1], t0, n, 1)
                        proj(slots[2], t0, n, 2)
                        if i == 3:
                            w_release()
                        S.add('act', lambda n=n: A.activation(out=qs_t[:, 0:n], in_=PS[0][:, 0:n], func=AF.Silu), reads=[('ps', 0)], writes=[('qs',)])
                        S.add('act', lambda n=n: A.activation(out=sg_t[:, 0:n], in_=PS[1][:, 0:n], func=AF.Sigmoid), reads=[('ps', 1)], writes=[('sg',)])
                        S.add('act', lambda n=n: A.copy(out=vT_t[:, 0:n], in_=PS[2][:, 0:n]), reads=[('ps', 2)], writes=[('vT',)])
                        S.add('act', lambda n=n, lh=lh: A.activation(out=la_t[:, 0:n], in_=sg_t[:, 0:n], func=AF.Ln,
                                                                     bias=lb[:, lh:lh + 1], scale=oml[:, lh:lh + 1]),
                              reads=[('sg',), ('lbc',)], writes=[('la',)])
                        S.add('dve', lambda n=n, lh=lh: V.tensor_scalar(out=sg_t[:, 0:n], in0=sg_t[:, 0:n], scalar1=noml[:, lh:lh + 1],
                                                                        scalar2=oml[:, lh:lh + 1], op0=ALU.mult, op1=ALU.add),
                              reads=[('sg',), ('la',), ('lbc',)], writes=[('sg',)])
                        if P1L < 2:
                            continue
                        S.add('dve', lambda: V.tensor_copy(out=gx_t[:, 0:1], in_=carry[:, 0:1]), reads=[('carry',)], writes=[('gx',)])
                        S.add('dve', lambda t0=t0, n=n: V.tensor_tensor_scan(out=gx_t[:, 1:1 + n], data0=seg01[:, t0:t0 + n], data1=la_t[:, 0:n],
                                                                             initial=carry[:, 0:1], op0=ALU.mult, op1=ALU.add),
                              reads=[('la',), ('carry',), ('seg01',), ('gx',)], writes=[('gx',)])
                        S.add('dve', lambda n=n: V.tensor_copy(out=carry[:, 0:1], in_=gx_t[:, n:n + 1]), reads=[('gx',)], writes=[('carry',)])
                        S.add('act', lambda n=n: A.activation(out=eg_t[:, 0:n], in_=gx_t[:, 1:1 + n], func=AF.Exp), reads=[('gx',)], writes=[('eg',)])
                        S.add('dve', lambda h=h, t0=t0, n=n: V.tensor_tensor(out=Qd[:, h, t0:t0 + n], in0=qs_t[:, 0:n], in1=eg_t[:, 0:n], op=ALU.mult),
                              reads=[('qs',), ('eg',)], writes=[('Qd', h)])
                        if P1L < 3:
                            continue
                        if i < 3:
                            chunks = [(64 * j, 64, 'p', j == 0 and i == 0, False, -1) for j in range(5)]
                        else:
                            chunks = [(0, 64, 'p', False, False, -1), (64, 8, 'p', False, True, -1)] + \
                                     [(72 + 8 * s, 8, 's', True, True, s) for s in range(NSEQ)]
                        for (o, C, kind, segstart, segend, sq) in chunks:
                            ref = zcol[:, 0:1] if kind == 's' else gx_t[:, o:o + 1]
                            S.add('dve', lambda o=o, C=C, ref=ref: V.tensor_scalar(out=la_t[:, o:o + C], in0=gx_t[:, 1 + o:1 + o + C], scalar1=ref,
                                                                                   scalar2=None, op0=ALU.subtract),
                                  reads=[('gx',), ('zc',), ('la',)], writes=[('la',)])
                        S.add('act', lambda n=n: A.activation(out=en_t[:, 0:n], in_=la_t[:, 0:n], func=AF.Exp, scale=-1.0), reads=[('la',)], writes=[('en',)])
                        S.add('act', lambda n=n: A.activation(out=la_t[:, 0:n], in_=la_t[:, 0:n], func=AF.Exp), reads=[('la',), ('en',)], writes=[('la',)])
                        S.add('dve', lambda n=n: V.tensor_tensor(out=qt_t[:, 0:n], in0=qs_t[:, 0:n], in1=la_t[:, 0:n], op=ALU.mult),
                              reads=[('qs',), ('la',)], writes=[('qt',)])
                        S.add('dve', lambda n=n: V.tensor_tensor(out=kt_t[:, 0:n], in0=sg_t[:, 0:n], in1=en_t[:, 0:n], op=ALU.mult),
                              reads=[('sg',), ('en',)], writes=[('kt',)])
                        for (o, C, kind, segstart, segend, sq) in chunks:
                            S.add('dve', lambda o=o, C=C: V.tensor_scalar(out=kh_t[:, o:o + C], in0=kt_t[:, o:o + C], scalar1=la_t[:, o + C - 1:o + C],
                                                                          scalar2=None, op0=ALU.mult),
                                  reads=[('kt',), ('la',)], writes=[('kh',)])
                        if P1L < 4:
                            continue
                        def stageA(ci, o, C, kind, segstart, segend, sq):
                            par = ci % 2

                            def ft(o=o, C=C):
                                PE.transpose(out=PS[4][0:C, 0:128], in_=kh_t[:, o:o + C], identity=identB[:, :])
                                return PE.transpose(out=PS[4][0:C, 128:256], in_=vT_t[:, o:o + C], identity=identB[:, :])
                            S.add('pe', ft, reads=[('kh',), ('vT',), ('identB',)], writes=[('ps', 4)])
                            S.add('act', lambda C=C, par=par: A.copy(out=khT[0:C, par, :], in_=PS[4][0:C, 0:128]),
                                  reads=[('ps', 4)], writes=[('khT', par)])
                            S.add('dve', lambda C=C, par=par: V.tensor_copy(out=vtk2[0:C, par, :], in_=PS[4][0:C, 128:256]),
                                  reads=[('ps', 4)], writes=[('vtk2', par)])
                            S.add('pe', lambda o=o, C=C: PE.matmul(PS[3][0:C, 0:C], lhsT=kt_t[:, o:o + C], rhs=qt_t[:, o:o + C], start=True, stop=True),
                                  reads=[('kt',), ('qt',)], writes=[('ps', 3)])
                            S.add('dve', lambda C=C, par=par: V.tensor_tensor(out=att[0:C, par, 0:C], in0=PS[3][0:C, 0:C], in1=tril[0:C, 0:C], op=ALU.mult),
                                  reads=[('ps', 3), ('tril',)], writes=[('att', par)])

                        def stageB(ci, o, C, kind, segstart, segend, sq, i=i, t0=t0, h=h, l=l):
                            par = ci % 2
                            sb = 6 + par
                            S.add('pe', lambda C=C, par=par, sb=sb: PE.matmul(PS[sb][:, 0:128], lhsT=khT[0:C, par, :], rhs=vtk2[0:C, par, :], start=True, stop=True),
                                  reads=[('khT', par), ('vtk2', par)], writes=[('ps', sb)])
                            if kind == 's':
                                S.add('act', lambda sq=sq: A.copy(out=Sbf[:, :], in_=S0[:, sq, :]), reads=[('S0',)], writes=[('Sbf',)])
                            use_s = (kind == 's') or (not segstart)

                            def fo(o=o, C=C, par=par, use_s=use_s):
                                if use_s:
                                    PE.matmul(PS[5][:, 0:C], lhsT=Sbf[:, :], rhs=qt_t[:, o:o + C], start=True, stop=False)
                                return PE.matmul(PS[5][:, 0:C], lhsT=vtk2[0:C, par, :], rhs=att[0:C, par, 0:C], start=(not use_s), stop=True)
                            S.add('pe', fo, reads=[('Sbf',), ('qt',), ('vtk2', par), ('att', par)], writes=[('ps', 5)])
                            S.add('act', lambda o=o, C=C: A.copy(out=oloc[:, h, t0 + o:t0 + o + C], in_=PS[5][:, 0:C]),
                                  reads=[('ps', 5)], writes=[('oloc', h)])
                            elast = la_t[:, o + C - 1:o + C]
                            if kind == 's':
                                S.add('dve', lambda sq=sq, sb=sb, par=par, elast=elast: V.scalar_tensor_tensor(
                                    out=sfst[:, par, :], in0=S0[:, sq, :], scalar=elast, in1=PS[sb][:, 0:128], op0=ALU.mult, op1=ALU.add),
                                    reads=[('S0',), ('la',), ('ps', sb)], writes=[('sfst', par)])
                                S.add('sp', lambda sq=sq, par=par: nc.sync.dma_start(out=ss_out[l, sq, h, :, :], in_=sfst[:, par, :]),
                                      reads=[('sfst', par)], dma=('o_ss', par))
                            else:
                                if segstart:
                                    S.add('dve', lambda sb=sb: V.tensor_copy(out=Sst[:, :], in_=PS[sb][:, 0:128]),
                                          reads=[('ps', sb)], writes=[('Sst',)])
                                else:
                                    S.add('dve', lambda sb=sb, elast=elast: V.scalar_tensor_tensor(
                                        out=Sst[:, :], in0=Sst[:, :], scalar=elast, in1=PS[sb][:, 0:128], op0=ALU.mult, op1=ALU.add),
                                        reads=[('Sst',), ('la',), ('ps', sb)], writes=[('Sst',)])
                                if segend:
                                    S.add('sp', lambda: nc.sync.dma_start(out=send[l][h * 128:(h + 1) * 128, :], in_=Sst[:, :]),
                                          reads=[('Sst',)], writes=[('send', h)], dma=('snd',))
                                    S.add('dve', lambda o=o, C=C: V.tensor_copy(out=Dsave[:, h:h + 1], in_=eg_t[:, o + C - 1:o + C]),
                                          reads=[('eg',)], writes=[('Dsave', h)])
                                else:
                                    S.add('act', lambda: A.copy(out=Sbf[:, :], in_=Sst[:, :]), reads=[('Sst',)], writes=[('Sbf',)])

                        stageA(0, *chunks[0])
                        for ci in range(len(chunks)):
                            if ci + 1 < len(chunks):
                                stageA(ci + 1, *chunks[ci + 1])
                            stageB(ci, *chunks[ci])

                if not _on('ex'):
                    break
                for h in range(8):
                    lh = l * 8 + h
                    slot = w_next()
                    S.add('sp', lambda l=l, h=h: nc.sync.dma_start(out=S0[:, 0, :], in_=recv[l][h * 128:(h + 1) * 128, :]),
                          reads=RECV_KEYS, writes=[('S0',)], dma=('S0',))
                    S.add('sp', lambda l=l, h=h: nc.sync.dma_start(out=S0[:, 1, :], in_=send[l][h * 128:(h + 1) * 128, :]),
                          reads=[('send', h)], writes=[('S0',)], dma=('S0',))
                    S.add('dve', lambda: V.tensor_scalar(out=S0[:, 0, :], in0=S0[:, 0, :], scalar1=flag[:, 0:1], scalar2=None, op0=ALU.mult),
                          reads=[('S0',), ('flag',)], writes=[('S0',)])
                    S.add('act', lambda: A.copy(out=Sbf[:, :], in_=S0[:, 0, :]), reads=[('S0',)], writes=[('Sbf',)])
                    par = h % 2
                    S.add('dve', lambda h=h, par=par: V.scalar_tensor_tensor(out=sfst[:, par, :], in0=S0[:, 0, :], scalar=Dsave[:, h:h + 1], in1=S0[:, 1, :],
                                                                             op0=ALU.mult, op1=ALU.add),
                          reads=[('S0',), ('Dsave', h)], writes=[('sfst', par)])
                    S.add('sp', lambda l=l, h=h, par=par: nc.sync.dma_start(out=ps_out[l, h, :, :], in_=sfst[:, par, :]),
                          reads=[('sfst', par)], dma=('o_ss', par))
                    for i, (t0, n) in enumerate(TCH):
                        b = next_pb()
                        proj(slot, t0, n, b)
                        S.add('act', lambda t0=t0, n=n, b=b: A.activation(out=sgr[:, t0:t0 + n], in_=PS[b][:, 0:n], func=AF.Silu),
                              reads=[('ps', b)], writes=[('sgr', i)])
                    for i, (t0, n) in enumerate(TCH):
                        npr = min(n, TP - t0)
                        S.add('pe', lambda h=h, t0=t0, npr=npr: PE.matmul(PS[5][:, 0:npr], lhsT=Sbf[:, :], rhs=Qd[:, h, t0:t0 + npr], start=True, stop=True),
                              reads=[('Sbf',), ('Qd', h)], writes=[('ps', 5)])
                        S.add('dve', lambda h=h, t0=t0, npr=npr: V.tensor_tensor(out=qs_t[:, 0:npr], in0=PS[5][:, 0:npr], in1=oloc[:, h, t0:t0 + npr], op=ALU.add),
                              reads=[('ps', 5), ('oloc', h)], writes=[('of',)])
                        if npr < n:
                            S.add('dve', lambda h=h, t0=t0, n=n, npr=npr: V.tensor_copy(out=qs_t[:, npr:n], in_=oloc[:, h, t0 + npr:t0 + n]),
                                  reads=[('oloc', h), ('of',)], writes=[('of',)])
                        S.add('act', lambda n=n: A.activation(out=qt_t[:, 0:n], in_=qs_t[:, 0:n], func=AF.Square), reads=[('of',)], writes=[('osq',)])
                        S.add('pe', lambda n=n: PE.matmul(PS[6][:, 0:n], lhsT=onesB[:, :], rhs=qt_t[:, 0:n], start=True, stop=True),
                              reads=[('osq',), ('onesB',)], writes=[('ps', 6)])
                        S.add('act', lambda n=n: A.activation(out=sg_t[:, 0:n], in_=PS[6][:, 0:n], func=AF.Sqrt, bias=epsT[:, 0:1], scale=1.0 / 128),
                              reads=[('ps', 6), ('eps',)], writes=[('rr',)])
                        S.add('dve', lambda n=n: V.reciprocal(out=sg_t[:, 0:n], in_=sg_t[:, 0:n]), reads=[('rr',)], writes=[('rr',)])
                        S.add('dve', lambda n=n: V.tensor_tensor(out=qs_t[:, 0:n], in0=qs_t[:, 0:n], in1=sg_t[:, 0:n], op=ALU.mult),
                              reads=[('of',), ('rr',)], writes=[('of',)])
                        S.add('dve', lambda h=h, lh=lh, t0=t0, n=n: V.scalar_tensor_tensor(out=mix[:, 8 + h, t0:t0 + n], in0=qs_t[:, 0:n], scalar=hw[:, lh:lh + 1],
                                                                                          in1=sgr[:, t0:t0 + n], op0=ALU.mult, op1=ALU.mult),
                              reads=[('of',), ('hw',), ('sgr', i)], writes=[('mix', 8 + h, i)])
                barrier(extra_writes=[('oloc', h) for h in range(8)] + [('Qd', h) for h in range(8)])

                if not _on('p3'):
                    break
                for g in range(4):
                    for jp in range(4):
                        hsrc = 4 * g + 2 * (jp % 2) + jp // 2
                        S.add('sp', lambda jp=jp, hsrc=hsrc: nc.sync.dma_start(out=emg[:, jp, :], in_=emd[hsrc, :, :]),
                              reads=[('emd',)], writes=[('emg',)], dma=('emg',))
                    for hf in range(2):
                        S.add('pool', lambda l=l, g=g, hf=hf: nc.gpsimd.dma_start(out=kd[64 * hf:64 * hf + 64, 0:128], in_=recv[l][1024 + 64 * g:1024 + 64 * g + 64, :]),
                              reads=RECV_KEYS + BARK, writes=[('kd', 'halo')], dma=('halo',))
                        S.add('pool', lambda l=l, g=g, hf=hf: nc.gpsimd.dma_start(
                            out=vtk[:, 0, 64 * hf:64 * hf + 64], in_=recv[l][1280 + 128 * (g // 2):1280 + 128 * (g // 2) + 128, 64 * (g % 2):64 * (g % 2) + 64]),
                            reads=RECV_KEYS + BARK, writes=[('vtk', 0)], dma=('halo',))
                        S.add('pool', lambda l=l, g=g, hf=hf: nc.gpsimd.dma_start(
                            out=vtks[:, :, 64 * hf:64 * hf + 64], in_=cv_in[l, :, :, 64 * g:64 * g + 64].rearrange("s t f -> t s f")),
                            reads=BARK, writes=[('vtks',)], dma=('halo',))
                    S.add('dve', lambda: V.tensor_scalar(out=vtk[:, 0, :], in0=vtk[:, 0, :], scalar1=flag[:, 0:1], scalar2=None, op0=ALU.mult),
                          reads=[('vtk', 0), ('flag',)], writes=[('vtk', 0)])
                    for s in range(NSEQ):
                        sl = s % 2
                        for hf in range(2):
                            S.add('pool', lambda l=l, g=g, s=s, sl=sl, hf=hf: nc.gpsimd.dma_start(out=ckst[:, sl, 64 * hf:64 * hf + 64], in_=ck_in[l, s, :, 64 * g:64 * g + 64]),
                                  reads=BARK, writes=[('ckst', sl)], dma=('ckst', sl))
                        S.add('pe', lambda sl=sl: PE.transpose(out=PS[4][:, 256 * sl:256 * sl + 128], in_=ckst[:, sl, :], identity=identB[:, :]),
                              reads=[('ckst', sl), ('identB',)], writes=[('ps', 4)])
                        S.add('act', lambda s=s, sl=sl: A.copy(out=kds[:, s, :], in_=PS[4][:, 256 * sl:256 * sl + 128]), reads=[('ps', 4)], writes=[('kds', s)])
                    s_q0 = w_next()
                    for i, (t0, n) in enumerate(TCH):
                        b = next_pb()
                        proj(s_q0, t0, n, b)
                        S.add('act', lambda t0=t0, n=n, b=b: A.copy(out=qa[:, 0, t0:t0 + n], in_=PS[b][:, 0:n]), reads=[('ps', b)], writes=[('qa', 0, i)])
                    s_q1 = w_next()
                    for i, (t0, n) in enumerate(TCH):
                        b = next_pb()
                        proj(s_q1, t0, n, b)
                        S.add('dve', lambda t0=t0, n=n, b=b: V.tensor_copy(out=qa[:, 1, t0:t0 + n], in_=PS[b][:, 0:n]), reads=[('ps', b)], writes=[('qa', 1, i)])
                    s_k = w_next()
                    for i, (t0, n) in enumerate(TCH):
                        b = next_pb()
                        proj(s_k, t0, n, b)
                        S.add('act', lambda t0=t0, n=n, b=b: A.copy(out=kd[:, 128 + t0:128 + t0 + n], in_=PS[b][:, 0:n]), reads=[('ps', b)], writes=[('kd', i)])
                    s_v = w_next()
                    for i, (t0, n) in enumerate(TCH):
                        b = next_pb()
                        proj(s_v, t0, n, b)
                        S.add('dve', lambda t0=t0, n=n, b=b: V.tensor_copy(out=vdT[:, t0:t0 + n], in_=PS[b][:, 0:n]), reads=[('ps', b)], writes=[('vdT', i)])
                    for jj in range(2):
                        s_g = w_next()
                        for i, (t0, n) in enumerate(TCH):
                            b = next_pb()
                            proj(s_g, t0, n, b)
                            S.add('act', lambda jj=jj, t0=t0, n=n, b=b: A.activation(out=sga[:, jj, t0:t0 + n], in_=PS[b][:, 0:n], func=AF.Silu),
                                  reads=[('ps', b)], writes=[('sga', jj, i)])
                    VD_ALL = [('vdT', i) for i in range(4)]
                    KD_ALL = [('kd', i) for i in range(4)] + [('kd', 'halo')]
                    QA_ALL = [('qa', jj, i) for jj in range(2) for i in range(4)]
                    SGA_ALL = [('sga', jj, i) for jj in range(2) for i in range(4)]
                    blocks = [(128 * i, 128) for i in range(8)] + [(1024, 8)]
                    for bi, (t0, n) in enumerate(blocks):
                        sl = bi % 2
                        S.add('pe', lambda t0=t0, n=n, sl=sl: PE.transpose(out=PS[4][0:n, 256 * sl:256 * sl + 128], in_=vdT[:, t0:t0 + n], identity=identB[:, :]),
                              reads=VD_ALL + [('identB',)], writes=[('ps', 4)])
                        if bi % 2 == 0:
                            S.add('act', lambda bi=bi, n=n, sl=sl: A.copy(out=vtk[0:n, 1 + bi, :], in_=PS[4][0:n, 256 * sl:256 * sl + 128]),
                                  reads=[('ps', 4)], writes=[('vtk', 1 + bi)])
                        else:
                            S.add('dve', lambda bi=bi, n=n, sl=sl: V.tensor_copy(out=vtk[0:n, 1 + bi, :], in_=PS[4][0:n, 256 * sl:256 * sl + 128]),
                                  reads=[('ps', 4)], writes=[('vtk', 1 + bi)])
                    for s in range(NSEQ):
                        sl = s % 2
                        t0 = TP + 8 * s
                        S.add('pe', lambda t0=t0, sl=sl: PE.transpose(out=PS[4][0:8, 256 * sl:256 * sl + 128], in_=vdT[:, t0:t0 + 8], identity=identB[:, :]),
                              reads=VD_ALL + [('identB',)], writes=[('ps', 4)])
                        S.add('act', lambda s=s, sl=sl: A.copy(out=vtkn[0:8, s, :], in_=PS[4][0:8, 256 * sl:256 * sl + 128]),
                              reads=[('ps', 4)], writes=[('vtkn', s)])
                    ablocks = []
                    for bi, (t0, n) in enumerate(blocks):
                        ablocks.append(dict(t0=t0, nq=n, kp=kd[:, 128 * bi:128 * bi + 128], kc=kd[:, 128 + t0:128 + t0 + n],
                                            vp=vtk[:, bi, :], vc=vtk[0:n, 1 + bi, :], op=(onesFl if bi == 0 else onesB),
                                            rk=[('vtk', bi), ('vtk', 1 + bi), ('onesFl',), ('onesB',)]))
                    for s in range(NSEQ):
                        t0 = TP + 8 * s
                        ablocks.append(dict(t0=t0, nq=8, kp=kds[:, s, :], kc=kd[:, 128 + t0:128 + t0 + 8],
                                            vp=vtks[:, s, :], vc=vtkn[0:8, s, :], op=onesB,
                                            rk=[('kds', s), ('vtks',), ('vtkn', s), ('onesB',)]))
                    for ab in ablocks:
                        t0, nq = ab['t0'], ab['nq']

                        def fs(ab=ab, t0=t0, nq=nq):
                            for j in range(4):
                                jj, hf = j // 2, j % 2
                                bank = 5 + hf
                                base = jj * 256
                                PE.matmul(PS[bank][:, base:base + nq], lhsT=ab['kp'][64 * hf:64 * hf + 64, :], rhs=qa[64 * hf:64 * hf + 64, jj, t0:t0 + nq],
                                          start=True, stop=True)
                                last = PE.matmul(PS[bank][0:nq, base + 128:base + 128 + nq], lhsT=ab['kc'][64 * hf:64 * hf + 64, :],
                                                 rhs=qa[64 * hf:64 * hf + 64, jj, t0:t0 + nq], start=True, stop=True)
                            return last
                        S.add('pe', fs, reads=KD_ALL + QA_ALL + ab['rk'], writes=[('ps', 5), ('ps', 6)])
                        for jj in range(2):
                            bank = 5 + jj
                            S.add('act', lambda jj=jj, bank=bank, nq=nq: A.activation(
                                out=Eb[:, 2 * jj:2 * jj + 2, 0, 0:nq], in_=PS[bank][:, :].rearrange("p (h c t) -> p h c t", h=2, c=2)[:, :, 0, 0:nq],
                                func=AF.Exp, scale=0.125), reads=[('ps', bank)], writes=[('E', jj, 0)])
                            S.add('act', lambda jj=jj, bank=bank, nq=nq: A.activation(
                                out=Eb[0:nq, 2 * jj:2 * jj + 2, 1, 0:nq], in_=PS[bank][:, :].rearrange("p (h c t) -> p h c t", h=2, c=2)[0:nq, :, 1, 0:nq],
                                func=AF.Exp, scale=0.125), reads=[('ps', bank)], writes=[('E', jj, 1)])
                        emv = emg[:, :, :].rearrange("p h (c t) -> p h c t", c=2)
                        S.add('dve', lambda nq=nq, emv=emv: V.tensor_tensor(out=Pb[:, :, 0, 0:nq], in0=Eb[:, :, 0, 0:nq], in1=emv[:, :, 0, 0:nq], op=ALU.mult),
                              reads=[('E', 0, 0), ('E', 1, 0), ('emg',)], writes=[('P', 0)])
                        S.add('dve', lambda nq=nq, emv=emv: V.tensor_tensor(out=Pb[0:nq, :, 1, 0:nq], in0=Eb[0:nq, :, 1, 0:nq], in1=emv[0:nq, :, 1, 0:nq], op=ALU.mult),
                              reads=[('E', 0, 1), ('E', 1, 1), ('emg',)], writes=[('P', 1)])

                        def fpv(ab=ab, nq=nq):
                            for j in range(4):
                                PE.matmul(PS[3][:, 128 * j:128 * j + nq], lhsT=ab['vp'], rhs=Pb[:, j, 0, 0:nq], start=True, stop=False)
                                PE.matmul(PS[3][:, 128 * j:128 * j + nq], lhsT=ab['vc'], rhs=Pb[0:nq, j, 1, 0:nq], start=False, stop=True)
                            for j in range(4):
                                PE.matmul(PS[7][:, 128 * j:128 * j + nq], lhsT=ab['op'][:, :], rhs=Pb[:, j, 0, 0:nq], start=True, stop=False)
                                last = PE.matmul(PS[7][:, 128 * j:128 * j + nq], lhsT=onesB[0:nq, :], rhs=Pb[0:nq, j, 1, 0:nq], start=False, stop=True)
                            return last
                        S.add('pe', fpv, reads=[('P', 0), ('P', 1)] + ab['rk'], writes=[('ps', 3), ('ps', 7)])
                        es = esink[:, l * 16 + 4 * g:l * 16 + 4 * g + 4]
                        S.add('dve', lambda nq=nq, es=es: V.tensor_tensor(out=den[:, :, 0:nq], in0=PS[7][:, :].rearrange("p (h t) -> p h t", h=4)[:, :, 0:nq],
                                                                          in1=es.rearrange("p (h o) -> p h o", o=1).to_broadcast([128, 4, nq]),
                                                                          op=ALU.add),
                              reads=[('ps', 7), ('esink',)], writes=[('den',)])
                        S.add('dve', lambda nq=nq: V.reciprocal(out=den[:, :, 0:nq], in_=den[:, :, 0:nq]), reads=[('den',)], writes=[('den',)])
                        S.add('dve', lambda nq=nq: V.tensor_tensor(out=onr[:, :, 0:nq], in0=PS[3][:, :].rearrange("p (h t) -> p h t", h=4)[:, :, 0:nq],
                                                                   in1=den[:, :, 0:nq], op=ALU.mult),
                              reads=[('ps', 3), ('den',)], writes=[('onr',)])
                        tcs = [i for i, (a, m) in enumerate(TCH) if a < t0 + nq and t0 < a + m]
                        for hf in range(2):
                            S.add('dve', lambda g=g, hf=hf, t0=t0, nq=nq: V.tensor_tensor(
                                out=mix[64 * hf:64 * hf + 64, 2 * g:2 * g + 2, t0:t0 + nq],
                                in0=onr[64 * hf:64 * hf + 64, 2 * hf:2 * hf + 2, 0:nq],
                                in1=sga[64 * hf:64 * hf + 64, :, t0:t0 + nq], op=ALU.mult),
                                reads=[('onr',)] + SGA_ALL, writes=[('mix', 2 * g + jj, i) for jj in range(2) for i in tcs])
                if not _on('out'):
                    break
                MIX_ALL = [('mix', c, i) for c in range(KC) for i in range(4)]
                for n_ in range(KC):
                    slot = w_next()
                    for i, (t0, n) in enumerate(TCH):
                        b = next_pb()

                        def fo2(slot=slot, t0=t0, n=n, b=b):
                            for c in range(KC):
                                last = PE.matmul(PS[b][:, 0:n], lhsT=wbuf[:, slot, c, :], rhs=mix[:, c, t0:t0 + n], start=(c == 0), stop=(c == KC - 1))
                            return last
                        S.add('pe', fo2, reads=[('w', slot)] + [('mix', c, i) for c in range(KC)], writes=[('ps', b)])
                        S.add('dve', lambda n_=n_, t0=t0, n=n, b=b: V.tensor_tensor(out=hT[:, n_, t0:t0 + n], in0=hT[:, n_, t0:t0 + n], in1=PS[b][:, 0:n], op=ALU.add),
                              reads=[('ps', b), ('h', n_, i)], writes=[('h', n_, i)])
                barrier(extra_writes=[('kd', 'halo'), ('vtk', 0), ('vtks',), ('ckst', 0), ('ckst', 1), ('emg',)])

            if _on('fin'):
                rtk = do_norm(fw, 0)
                barrier(extra_writes=[('sq', 0), ('sq', 1)])
                for ti, (t0, n) in enumerate(TT):
                    sl = ti % 2
                    for kq in range(4):
                        b = 5 + (ti * 4 + kq) % 3
                        for j in range(4):
                            k = kq * 4 + j
                            yp = (ti * 16 + k) % 2
                            S.add('dve', lambda k=k, t0=t0, n=n, yp=yp: V.scalar_tensor_tensor(out=ynt[:, yp, 0:n], in0=hT[:, k, t0:t0 + n], scalar=fw[:, k:k + 1],
                                                                                              in1=rt[:, t0:t0 + n], op0=ALU.mult, op1=ALU.mult),
                                  reads=hkeys(k, t0, n) + rtk + [('fw',)], writes=[('ynt', yp)])
                            S.add('pe', lambda j=j, n=n, yp=yp, b=b: PE.transpose(out=PS[b][0:n, j * 128:(j + 1) * 128], in_=ynt[:, yp, 0:n], identity=identF[:, :]),
                                  reads=[('ynt', yp), ('identF',)], writes=[('ps', b)])
                        if kq % 2 == 0:
                            S.add('act', lambda sl=sl, kq=kq, n=n, b=b: A.copy(out=xst[0:n, sl, kq * 512:(kq + 1) * 512], in_=PS[b][0:n, :]),
                                  reads=[('ps', b)], writes=[('xst', sl)])
                        else:
                            S.add('dve', lambda sl=sl, kq=kq, n=n, b=b: V.tensor_copy(out=xst[0:n, sl, kq * 512:(kq + 1) * 512], in_=PS[b][0:n, :]),
                                  reads=[('ps', b)], writes=[('xst', sl)])
                    S.add('sp', lambda sl=sl, t0=t0, n=n: nc.sync.dma_start(out=y_out[t0:t0 + n, :], in_=xst[0:n, sl, :]),
                          reads=[('xst', sl)], dma=('o_y', sl))

        for p_ in range(2):
            run_pass(p_, xin_p[p_], ck_p[p_], cv_p[p_], st_p[p_], y_p[p_], pk_p[p_], pv_p[p_], ps_p[p_], sk_p[p_], sv_p[p_], ss_p[p_], send_p[p_], send_p[0])

        S.emit([('o_y', 0), ('o_y', 1), ('o_pk',), ('o_sk',), ('o_ss', 0), ('o_ss', 1), ('snd',), ('emd',)])
    return nc


def _t5_bucket(dist):
    d = np.maximum(dist, 0)
    df = np.maximum(d, 1).astype(np.float32)
    large = 16 + (np.log(df / np.float32(16)) / np.float32(np.log(128 / 16)) * np.float32(16)).astype(np.int32)
    large = np.minimum(large, 31)
    return np.where(d < 16, d, large)


_NC_CACHE = {}


def kernel(x_prompt, x_sample, cache_k, cache_v, state_h, meta_tokens, w_in, w_out, norm_w,
           final_norm_w, attn_sinks, rel_bias_table, hgrn_lb_logits, hgrn_norm_w, _nl=DEPTH):
    f32 = np.float32
    x_prompt = np.asarray(x_prompt, f32)
    x_sample = np.asarray(x_sample, f32)
    cache_k = np.asarray(cache_k, f32).reshape(DEPTH, 32, 128, 256)
    cache_v = np.asarray(cache_v, f32).reshape(DEPTH, 32, 128, 256)
    state_h = np.asarray(state_h, f32)
    w_in = np.asarray(w_in, f32)
    w_out = np.asarray(w_out, f32)
    rel = np.asarray(rel_bias_table, f32)

    def pk(a, n):
        a = np.asarray(a, f32).reshape(-1, n, 128)
        return np.ascontiguousarray(a.transpose(2, 0, 1).reshape(128, -1))

    nw = pk(norm_w, KC)
    fw = pk(final_norm_w, KC)
    lbl = pk(hgrn_lb_logits, 8)
    hw = pk(hgrn_norm_w, 8)
    perm = np.array([4 * g_ + 2 * (jp_ % 2) + jp_ // 2 for g_ in range(4) for jp_ in range(4)])
    sink = np.ascontiguousarray(np.broadcast_to(np.asarray(attn_sinks, f32)[:, perm].reshape(1, -1), (128, DEPTH * 16)))
    s_ = np.arange(128)[:, None]
    t_ = np.arange(128)[None, :]
    dprev = np.clip(128 + t_ - s_, 0, 127)
    dcur = np.clip(t_ - s_, 0, 127)
    bidx = _t5_bucket(np.arange(128))
    bias_d = rel[bidx]
    biasT = np.empty((16, 128, 256), f32)
    biasT[:, :, 0:128] = bias_d[dprev].transpose(2, 0, 1)
    biasT[:, :, 128:256] = bias_d[dcur].transpose(2, 0, 1)
    mask01 = np.zeros((128, 256), f32)
    mask01[:, 0:128] = (s_ > t_)
    mask01[:, 128:256] = (s_ <= t_)
    tril = np.zeros((128, 64), f32)
    tril[0:64, :] = (np.arange(64)[:, None] <= np.arange(64)[None, :])
    seg01 = np.ones((128, T), f32)
    for s in range(NSEQ):
        seg01[:, TP + 8 * s] = 0.0
    identf = np.eye(128, dtype=f32)

    NCORE = int(os.environ.get('KNCORE', '4'))
    wi = w_in[:_nl] if _nl < DEPTH else w_in
    wo = w_out[:_nl] if _nl < DEPTH else w_out
    if os.environ.get('KSMALLW', '0') == '1':
        wi = np.ascontiguousarray(w_in[:1, :128, :128])
        wo = np.ascontiguousarray(w_out[:1, :128, :128])
    in_maps = []
    for c in range(NCORE):
        full = np.concatenate([np.asarray(meta_tokens, f32), x_prompt[c]], axis=0)
        m = dict(w_in=wi, w_out=wo, nw=nw, fw=fw, sink=sink, biasT=biasT, mask01=mask01, tril=tril,
                 seg01=seg01, lbl=lbl, hw=hw, identf=identf)
        for p in range(2):
            s0 = 8 * c + 4 * p
            m["xin%d" % p] = np.ascontiguousarray(np.concatenate([full[p * TP:(p + 1) * TP], x_sample[s0:s0 + 4].reshape(TS, D)], axis=0))
            m["ck%d" % p] = np.ascontiguousarray(cache_k[:, s0:s0 + 4])
            m["cv%d" % p] = np.ascontiguousarray(cache_v[:, s0:s0 + 4])
            m["st%d" % p] = np.ascontiguousarray(state_h[:, s0:s0 + 4])
        in_maps.append(m)
    if _nl not in _NC_CACHE:
        _NC_CACHE[_nl] = build_nc(_nl)
    nc = _NC_CACHE[_nl]
    res = run_bass_kernel_spmd(nc, in_maps, core_ids=list(range(NCORE)))
    R = res.results
    y_prompt = np.empty((4, 2048, D), f32)
    y_sample = np.empty((32, 8, D), f32)
    pkk = np.empty((DEPTH, 4, 128, 4, 64), f32)
    pvv = np.empty((DEPTH, 4, 128, 4, 64), f32)
    pss = np.empty((DEPTH, 4, 8, 128, 128), f32)
    skk = np.empty((DEPTH, 32, 128, 4, 64), f32)
    svv = np.empty((DEPTH, 32, 128, 4, 64), f32)
    sss = np.empty((DEPTH, 32, 8, 128, 128), f32)
    for c in range(NCORE):
        y0, y1 = R[c]["y0"], R[c]["y1"]
        y_prompt[c, 0:TP - 16] = y0[16:TP]
        y_prompt[c, TP - 16:] = y1[0:TP]
        pkk[:, c] = R[c]["pk1"].reshape(DEPTH, 128, 4, 64)
        pvv[:, c] = R[c]["pv1"].reshape(DEPTH, 128, 4, 64)
        pss[:, c] = R[c]["ps1"]
        for p in range(2):
            s0 = 8 * c + 4 * p
            y_sample[s0:s0 + 4] = R[c]["y%d" % p][TP:].reshape(4, 8, D)
            skk[:, s0:s0 + 4] = R[c]["sk%d" % p].reshape(DEPTH, 4, 128, 4, 64)
            svv[:, s0:s0 + 4] = R[c]["sv%d" % p].reshape(DEPTH, 4, 128, 4, 64)
            sss[:, s0:s0 + 4] = R[c]["ss%d" % p]
    return (y_prompt, y_sample, pkk, pvv, pss, skk, svv, sss)
```

```python
import contextlib
import numpy as np
import concourse.bass as bass
import concourse.mybir as mybir
from concourse.bass_utils import run_bass_kernel_spmd

F32 = mybir.dt.float32
BF16 = mybir.dt.bfloat16
AF = mybir.ActivationFunctionType
ALU = mybir.AluOpType

DEPTH = 4
D = 2048
KC = 16
TP = 1032
NSEQ = 4
TS = 32
T = TP + TS
DIN = 6656
TCH = [(0, 320), (320, 320), (640, 320), (960, 104)]
EPS = 1e-6
NS = 3
C_QA, C_KA, C_VA, C_GA, C_QR, C_FR, C_IR, C_GR = 0, 1024, 1280, 1536, 2560, 3584, 4608, 5632
SEND_ROWS = 1536
import os
USE_CC = os.environ.get('KNOCC', '0') != '1'
STOP = os.environ.get('KSTOP', '')
P1L = int(os.environ.get('KP1', '9'))
PH = ['const', 'p0', 'norm', 'p15', 'p1', 'ex', 'p2', 'p3', 'out', 'fin']


def _on(ph):
    return STOP == '' or PH.index(ph) <= PH.index(STOP)


class Sched:
    def __init__(self, nc, stack):
        self.nc = nc
        self.stack = stack
        self.ops = []
        self.lastw = {}
        self.rds = {}
        self.eng = {'pe': nc.tensor, 'act': nc.scalar, 'dve': nc.vector, 'pool': nc.gpsimd, 'sp': nc.sync}
        self.sem = {e: stack.enter_context(nc.semaphore("s_" + e)) for e in self.eng}
        self.dsem = {}

    def add(self, eng, fn, reads=(), writes=(), dma=None):
        idx = len(self.ops)
        psr = [r for r in reads if r[0] == 'ps']
        if psr:
            reads = [r for r in reads if r[0] != 'ps']
            writes = list(writes) + [r for r in psr if r not in writes]
        deps = set()
        for r in reads:
            w = self.lastw.get(r)
            if w is not None:
                deps.add(w)
        for r in writes:
            w = self.lastw.get(r)
            if w is not None:
                deps.add(w)
            for x in self.rds.get(r, ()):
                deps.add(x)
        deps.discard(idx)
        if dma is not None and dma not in self.dsem:
            self.dsem[dma] = self.stack.enter_context(self.nc.semaphore("d%d" % len(self.dsem)))
        self.ops.append(dict(eng=eng, fn=fn, deps=deps, dma=dma, signal=False, sidx=0))
        for r in reads:
            self.rds.setdefault(r, []).append(idx)
        for r in writes:
            self.lastw[r] = idx
            self.rds[r] = []
        return idx

    def emit(self, final_slots):
        ops = self.ops

        def skip(p, op):
            return p['dma'] is None and op['dma'] is None and p['eng'] == 'pe' and op['eng'] == 'pe'

        for op in ops:
            for d in op['deps']:
                if not skip(ops[d], op):
                    ops[d]['signal'] = True
        cnt = {e: 0 for e in self.eng}
        dcnt = {}
        waited = {}
        for op in ops:
            e = op['eng']
            E = self.eng[e]
            need = {}
            for d in op['deps']:
                p = ops[d]
                if skip(p, op):
                    continue
                if p['dma'] is not None:
                    key = ('d', p['dma'])
                    val = 16 * dcnt[p['dma']]
                else:
                    key = ('e', p['eng'])
                    val = p['sidx']
                if val > need.get(key, 0):
                    need[key] = val
            for key, val in need.items():
                if waited.get((e, key), 0) >= val:
                    continue
                sem = self.dsem[key[1]] if key[0] == 'd' else self.sem[key[1]]
                E.wait_ge(sem, val)
                waited[(e, key)] = val
            inst = op['fn']()
            if op['dma'] is not None:
                inst.then_inc(self.dsem[op['dma']], 16)
                dcnt[op['dma']] = dcnt.get(op['dma'], 0) + 1
            elif op['signal']:
                cnt[e] += 1
                inst.then_inc(self.sem[e], 1)
                op['sidx'] = cnt[e]
        sp = self.eng['sp']
        for slot in final_slots:
            if slot in dcnt:
                sp.wait_ge(self.dsem[slot], 16 * dcnt[slot])


def build_nc(nl=DEPTH):
    nc = bass.Bass("TRN2", target_bir_lowering=False)
    dt_in = lambda name, shape: nc.dram_tensor(name, list(shape), F32, kind="ExternalInput").ap()
    dt_out = lambda name, shape: nc.dram_tensor(name, list(shape), F32, kind="ExternalOutput").ap()
    xin_p = [dt_in("xin%d" % p, (T, D)) for p in range(2)]
    ck_p = [dt_in("ck%d" % p, (DEPTH, NSEQ, 128, 256)) for p in range(2)]
    cv_p = [dt_in("cv%d" % p, (DEPTH, NSEQ, 128, 256)) for p in range(2)]
    st_p = [dt_in("st%d" % p, (DEPTH, NSEQ, 8, 128, 128)) for p in range(2)]
    SMALLW = os.environ.get("KSMALLW", "0") == "1"
    w_in = dt_in("w_in", (1, 128, 128) if SMALLW else (nl, D, DIN))
    w_out = dt_in("w_out", (1, 128, 128) if SMALLW else (nl, D, D))
    nw_in = dt_in("nw", (128, DEPTH * KC))
    fw_in = dt_in("fw", (128, KC))
    sink_in = dt_in("sink", (128, DEPTH * 16))
    biasT_in = dt_in("biasT", (16, 128, 256))
    mask_in = dt_in("mask01", (128, 256))
    tril_in = dt_in("tril", (128, 64))
    seg_in = dt_in("seg01", (128, T))
    lbl_in = dt_in("lbl", (128, DEPTH * 8))
    hw_in = dt_in("hw", (128, DEPTH * 8))
    identf_in = dt_in("identf", (128, 128))

    y_p = [dt_out("y%d" % p, (T, D)) for p in range(2)]
    pk_p = [dt_out("pk%d" % p, (DEPTH, 128, 256)) for p in range(2)]
    pv_p = [dt_out("pv%d" % p, (DEPTH, 128, 256)) for p in range(2)]
    ps_p = [dt_out("ps%d" % p, (DEPTH, 8, 128, 128)) for p in range(2)]
    sk_p = [dt_out("sk%d" % p, (DEPTH, NSEQ, 128, 256)) for p in range(2)]
    sv_p = [dt_out("sv%d" % p, (DEPTH, NSEQ, 128, 256)) for p in range(2)]
    ss_p = [dt_out("ss%d" % p, (DEPTH, NSEQ, 8, 128, 128)) for p in range(2)]
    send_p = [[nc.dram_tensor("send%d_%d" % (p, l), [SEND_ROWS, 128], F32).ap() for l in range(DEPTH)] for p in range(2)]
    emd = nc.dram_tensor("emd", [16, 128, 256], BF16).ap()

    stack = contextlib.ExitStack()
    with stack:
        S = Sched(nc, stack)
        cur = [16512]
        TOP = 229344
        nid = [0]

        def alloc(shape, dtype, at=None):
            isz = 4 if dtype == F32 else 2
            n = 1
            for s in shape[1:]:
                n *= s
            size = (n * isz + 31) // 32 * 32
            if at is None:
                off = cur[0]
                cur[0] += size
                assert cur[0] <= TOP, ("SBUF overflow", cur[0] - TOP)
            else:
                off = at
            nid[0] += 1
            return nc.alloc_sbuf_tensor_at("t%d" % nid[0], list(shape), dtype, offset=off)

        hT = alloc([128, KC, T], F32)
        xn = alloc([128, KC, T], BF16)
        mix = alloc([128, KC, T], BF16)
        XB = cur[0]
        cur[0] += 34048
        wbuf = alloc([128, NS, KC, 128], BF16)
        emg = alloc([128, 4, 256], BF16)
        identB = alloc([128, 128], BF16)
        identF = alloc([128, 128], F32)
        onesB = alloc([128, 128], BF16)
        onesFl = alloc([128, 128], BF16)
        tril = alloc([128, 64], F32)
        seg01 = alloc([128, T], BF16)
        nw = alloc([128, DEPTH * KC], F32)
        fw = alloc([128, KC], F32)
        lb = alloc([128, DEPTH * 8], F32)
        oml = alloc([128, DEPTH * 8], F32)
        noml = alloc([128, DEPTH * 8], F32)
        hw = alloc([128, DEPTH * 8], F32)
        esink = alloc([128, DEPTH * 16], F32)
        flag = alloc([128, 1], F32)
        epsT = alloc([128, 1], F32)
        zcol = alloc([128, 1], F32)
        carry = alloc([128, 1], F32)
        Dsave = alloc([128, 8], F32)
        YB = cur[0]
        yf = [alloc([128, 321], F32) for _ in range(6)]
        qs_t, sg_t, la_t, gx_t, eg_t, en_t = yf
        vT_t, qt_t, kt_t, kh_t = [alloc([128, 320], BF16) for _ in range(4)]
        khT = alloc([128, 2, 128], BF16)
        vtk2 = alloc([128, 2, 128], BF16)
        att = alloc([128, 2, 64], BF16)
        Sst = alloc([128, 128], F32)
        Sbf = alloc([128, 128], BF16)
        S0 = alloc([128, 4, 128], F32)
        sgr = alloc([128, T], BF16)
        sfst = alloc([128, 2, 128], F32)
        YEND = cur[0]
        print("SBUF used", cur[0] - 16512, "free", TOP - cur[0])
        kv32 = alloc([128, 4, 160], F32, at=YB)
        tokp = alloc([128, 512], F32, at=YB + 2560)
        toks = alloc([128, 512], F32, at=YB + 2560 + 2048)
        assert YB + 2560 + 4096 <= YEND
        rt = alloc([128, T], F32, at=YB)
        den = alloc([128, 4, 128], F32, at=YB)
        onr = alloc([128, 4, 128], F32, at=YB + 2048)
        ynt = alloc([128, 2, 128], F32, at=YB + 4288)
        oloc = alloc([128, 8, T], BF16, at=XB)
        Qd = alloc([128, 8, T], BF16, at=XB + 17024)
        sqtmp = alloc([128, 2, T], BF16, at=XB)
        xst = alloc([128, 2, 2048], F32, at=XB)
        emst_f = alloc([128, 2, 256], F32, at=XB + 16384)
        emst_t = alloc([128, 256], F32, at=XB + 16384 + 2048)
        emst_b = alloc([128, 2, 256], BF16, at=XB + 16384 + 3072)
        mask01 = alloc([128, 256], F32, at=XB + 16384 + 4096)
        lbtmp = alloc([128, DEPTH * 8], F32, at=XB + 16384 + 5120)
        lbsum = alloc([128, 8], F32, at=XB + 16384 + 5120 + 128)
        xo = [XB]

        def xalloc(shape, dtype):
            t_ = alloc(shape, dtype, at=xo[0])
            n = 1
            for s in shape[1:]:
                n *= s
            xo[0] += (n * (4 if dtype == F32 else 2) + 31) // 32 * 32
            assert xo[0] <= XB + 34048
            return t_

        qa = xalloc([128, 2, T], BF16)
        kd = xalloc([128, 128 + T], BF16)
        vdT = xalloc([128, T], BF16)
        sga = xalloc([128, 2, T], BF16)
        vtk = xalloc([128, 10, 128], BF16)
        kds = xalloc([128, NSEQ, 128], BF16)
        vtks = xalloc([128, NSEQ, 128], BF16)
        vtkn = xalloc([128, NSEQ, 128], BF16)
        ckst = xalloc([128, 2, 128], BF16)
        Eb2 = [xalloc([128, 4, 2, 128], BF16) for _ in range(2)]
        Pb2 = [xalloc([128, 4, 2, 128], BF16) for _ in range(2)]

        PS = []
        for b in range(8):
            if b == 4:
                PS.append(stack.enter_context(nc.psum_tensor("psb%d" % b, [128, 1024], BF16)))
            else:
                PS.append(stack.enter_context(nc.psum_tensor("psb%d" % b, [128, 512], F32)))
        pbrot = [0]

        def next_pb():
            b = pbrot[0] % 3
            pbrot[0] += 1
            return b

        V = nc.vector
        A = nc.scalar
        PE = nc.tensor

        bscr = alloc([128, 4], F32)

        def barrier(extra_writes=()):
            if os.environ.get('KNOBAR', '0') == '1':
                return
            S.add('act', lambda: A.copy(out=bscr[:, 0:1], in_=zcol[:, 0:1]), reads=[('zc',)], writes=[('bar', 'act')] + list(extra_writes))
            S.add('dve', lambda: V.tensor_copy(out=bscr[:, 1:2], in_=zcol[:, 0:1]), reads=[('zc',)], writes=[('bar', 'dve')])
            S.add('pe', lambda: PE.matmul(PS[7][0:1, 0:1], lhsT=identB[:, 0:1], rhs=identB[:, 0:1], start=True, stop=True),
                  reads=[('identB',)], writes=[('bar', 'pe'), ('ps', 7)])
            allb = [('bar', 'act'), ('bar', 'dve'), ('bar', 'pe')]
            S.add('act', lambda: A.copy(out=bscr[:, 2:3], in_=zcol[:, 0:1]), reads=allb + [('zc',)], writes=[('barj', 'act')])
            S.add('dve', lambda: V.tensor_copy(out=bscr[:, 3:4], in_=zcol[:, 0:1]), reads=allb + [('zc',)], writes=[('barj', 'dve')])
            S.add('pe', lambda: PE.matmul(PS[7][0:1, 0:1], lhsT=identB[:, 0:1], rhs=identB[:, 0:1], start=True, stop=True),
                  reads=allb + [('identB',)], writes=[('barj', 'pe'), ('ps', 7)])

        def ld(dst, src, key, q='sp'):
            eng = nc.sync if q == 'sp' else nc.gpsimd
            S.add(q, lambda: eng.dma_start(out=dst, in_=src), writes=[key], dma=('c', key))

        ld(identF[:, :], identf_in[:, :], ('identF',))
        ld(identB[:, :], identf_in[:, :], ('identB',), q='pool')
        ld(tril[:, :], tril_in[:, :], ('tril',))
        ld(seg01[:, :], seg_in[:, :], ('seg01',), q='pool')
        ld(nw[:, :], nw_in[:, :], ('nw',))
        ld(fw[:, :], fw_in[:, :], ('fw',))
        ld(hw[:, :], hw_in[:, :], ('hw',))
        ld(esink[:, :], sink_in[:, :], ('esink',))
        ld(lbtmp[:, :], lbl_in[:, :], ('lbtmp',))
        ld(mask01[:, :], mask_in[:, :], ('mask01',))
        S.add('dve', lambda: V.memset(zcol[:, :], 0.0), writes=[('zc',)])
        S.add('dve', lambda: V.memset(epsT[:, :], EPS), writes=[('eps',)])
        S.add('dve', lambda: V.memset(onesB[:, :], 1.0), writes=[('onesB',)])
        S.add('act', lambda: A.activation(out=esink[:, :], in_=esink[:, :], func=AF.Exp), reads=[('esink',)], writes=[('esink',)])
        S.add('act', lambda: A.activation(out=lbtmp[:, :], in_=lbtmp[:, :], func=AF.Exp), reads=[('lbtmp',)], writes=[('lbtmp',)])

        lb_steps = []

        def lb_chain():
            steps = [
                (lambda: V.tensor_tensor(out=lbsum[:, :], in0=lbtmp[:, 0:8], in1=lbtmp[:, 8:16], op=ALU.add)),
                (lambda: V.tensor_tensor(out=lbsum[:, :], in0=lbsum[:, :], in1=lbtmp[:, 16:24], op=ALU.add)),
                (lambda: V.tensor_tensor(out=lbsum[:, :], in0=lbsum[:, :], in1=lbtmp[:, 24:32], op=ALU.add)),
                (lambda: V.reciprocal(out=lbsum[:, :], in_=lbsum[:, :])),
                (lambda: V.memset(lb[:, 0:8], 0.0)),
                (lambda: V.tensor_tensor(out=lb[:, 8:16], in0=lbtmp[:, 8:16], in1=lbsum[:, :], op=ALU.mult)),
                (lambda: V.tensor_tensor(out=lb[:, 16:24], in0=lbtmp[:, 16:24], in1=lbsum[:, :], op=ALU.mult)),
                (lambda: V.tensor_tensor(out=lb[:, 24:32], in0=lbtmp[:, 24:32], in1=lbsum[:, :], op=ALU.mult)),
                (lambda: V.tensor_tensor(out=lb[:, 16:24], in0=lb[:, 16:24], in1=lb[:, 8:16], op=ALU.add)),
                (lambda: V.tensor_tensor(out=lb[:, 24:32], in0=lb[:, 24:32], in1=lb[:, 16:24], op=ALU.add)),
                (lambda: V.tensor_scalar(out=noml[:, :], in0=lb[:, :], scalar1=-1.0, scalar2=None, op0=ALU.add)),
                (lambda: V.tensor_scalar(out=oml[:, :], in0=noml[:, :], scalar1=-1.0, scalar2=None, op0=ALU.mult)),
            ]
            for f in steps:
                S.add('dve', f, reads=[('lbtmp',), ('lbc',)], writes=[('lbc',)])

        lb_chain()

        for h in range(16 if os.environ.get('KNOEM', '0') != '1' else 0):
            sl = h % 2
            S.add('sp', lambda h=h, sl=sl: nc.sync.dma_start(out=emst_f[:, sl, :], in_=biasT_in[h, :, :]),
                  writes=[('emf', sl)], dma=('emf', sl))
            S.add('act', lambda sl=sl: A.activation(out=emst_t[:, :], in_=emst_f[:, sl, :], func=AF.Exp),
                  reads=[('emf', sl)], writes=[('emt',)])
            S.add('dve', lambda sl=sl: V.tensor_tensor(out=emst_b[:, sl, :], in0=emst_t[:, :], in1=mask01[:, :], op=ALU.mult),
                  reads=[('emt',), ('mask01',)], writes=[('emb', sl)])
            S.add('sp', lambda h=h, sl=sl: nc.sync.dma_start(out=emd[h, :, :], in_=emst_b[:, sl, :]),
                  reads=[('emb', sl)], writes=[('emd',)], dma=('emd',))
        barrier(extra_writes=[('emb', 0), ('emb', 1), ('emf', 0), ('emf', 1), ('mask01',), ('lbtmp',)])

        def hkeys(k, t0=0, n=T):
            return [('h', k, i) for i, (a, m) in enumerate(TCH) if a < t0 + n and t0 < a + m]

        TT = [(i * 128, 128) for i in range(8)] + [(1024, 40)]
        units = []
        for l in range(nl):
            for u in range(4):
                units.append(('in', l, C_KA + 128 * u))
            for h in range(8):
                units.append(('in', l, C_QR + 128 * h))
                units.append(('in', l, C_FR + 128 * h))
                units.append(('in', l, C_IR + 128 * h))
            for h in range(8):
                units.append(('in', l, C_GR + 128 * h))
            for g in range(4):
                units.append(('in', l, C_QA + 256 * g))
                units.append(('in', l, C_QA + 256 * g + 128))
                units.append(('dup', l, C_KA + 64 * g))
                units.append(('dup', l, C_VA + 64 * g))
                units.append(('in', l, C_GA + 256 * g))
                units.append(('in', l, C_GA + 256 * g + 128))
            for n_ in range(16):
                units.append(('out', l, 128 * n_))
        units = units + units
        wst = dict(next=0, issued=0)

        def w_issue(i):
            kind, l, c0 = units[i]
            slot = i % NS
            if kind == 'in':
                src = w_in[l].rearrange("(k p) c -> p k c", p=128)[:, :, c0:c0 + 128]
                S.add('pool', lambda slot=slot, src=src: nc.gpsimd.dma_start(out=wbuf[:, slot, :, :], in_=src),
                      writes=[('w', slot)], dma=('w', slot))
            elif kind == 'out':
                src = w_out[l].rearrange("(k p) c -> p k c", p=128)[:, :, c0:c0 + 128]
                S.add('pool', lambda slot=slot, src=src: nc.gpsimd.dma_start(out=wbuf[:, slot, :, :], in_=src),
                      writes=[('w', slot)], dma=('w', slot))
            else:
                src = w_in[l].rearrange("(k p) c -> p k c", p=128)[:, :, c0:c0 + 64]
                S.add('pool', lambda slot=slot, src=src: nc.gpsimd.dma_start(out=wbuf[:, slot, :, 0:64], in_=src),
                      writes=[('w', slot)], dma=('w', slot))
                S.add('pool', lambda slot=slot, src=src: nc.gpsimd.dma_start(out=wbuf[:, slot, :, 64:128], in_=src),
                      writes=[('w', slot)], dma=('w', slot))

        def w_release():
            done = wst['next']
            while wst['issued'] < min(done + NS, len(units)):
                w_issue(wst['issued'])
                wst['issued'] += 1

        def w_begin(k=1):
            w_release()
            first = wst['next']
            wst['next'] += k
            assert wst['issued'] >= wst['next'], (wst, k)
            return [(first + j) % NS for j in range(k)]

        def w_next():
            return w_begin(1)[0]

        XN_ALL = [('xn', k) for k in range(KC)]
        BARK = [('barj', 'act'), ('barj', 'dve'), ('barj', 'pe')]
        RECV_KEYS = [('send', x_) for x_ in list(range(8)) + ['k', 'v']]

        def proj(slot, t0, n, b):
            def f():
                for k in range(KC):
                    last = PE.matmul(PS[b][:, 0:n], lhsT=wbuf[:, slot, k, :], rhs=xn[:, k, t0:t0 + n],
                                     start=(k == 0), stop=(k == KC - 1))
                return last
            S.add('pe', f, reads=[('w', slot)] + XN_ALL, writes=[('ps', b)])

        def do_norm(wt, wcol0):
            banks = [0, 1, 2, 5]
            for k in range(KC):
                sl = k % 2
                S.add('act', lambda k=k, sl=sl: A.activation(out=sqtmp[:, sl, :], in_=hT[:, k, :], func=AF.Square),
                      reads=hkeys(k), writes=[('sq', sl)])
                for i, (t0, n) in enumerate(TCH):
                    S.add('pe', lambda k=k, sl=sl, i=i, t0=t0, n=n: PE.matmul(PS[banks[i]][:, 0:n], lhsT=onesB[:, :], rhs=sqtmp[:, sl, t0:t0 + n],
                                                                             start=(k == 0), stop=(k == KC - 1)),
                          reads=[('sq', sl), ('onesB',)], writes=[('ps', banks[i])])
            for i, (t0, n) in enumerate(TCH):
                S.add('act', lambda i=i, t0=t0, n=n: A.activation(out=rt[:, t0:t0 + n], in_=PS[banks[i]][:, 0:n], func=AF.Sqrt,
                                                                  bias=epsT[:, 0:1], scale=1.0 / D),
                      reads=[('ps', banks[i]), ('eps',)], writes=[('rt', i)])
                S.add('dve', lambda t0=t0, n=n: V.reciprocal(out=rt[:, t0:t0 + n], in_=rt[:, t0:t0 + n]),
                      reads=[('rt', i)], writes=[('rt', i)])
            return [('rt', i) for i in range(4)]

        def tc_of(t0):
            return [i for i, (a, m) in enumerate(TCH) if a == t0][0]

        def run_pass(p, xin, ck_in, cv_in, st_in, y_out, pk_out, pv_out, ps_out, sk_out, sv_out, ss_out, send, recv):
            barrier(extra_writes=[('xst', 0), ('xst', 1), ('flag',), ('onesFl',)])
            S.add('dve', lambda: V.memset(flag[:, :], float(p)), writes=[('flag',)])
            S.add('dve', lambda: V.tensor_scalar(out=onesFl[:, :], in0=onesB[:, :], scalar1=flag[:, 0:1], scalar2=None, op0=ALU.mult),
                  reads=[('onesB',), ('flag',)], writes=[('onesFl',)])
            if _on('p0'):
                for ti, (t0, n) in enumerate(TT):
                    sl = ti % 2
                    S.add('sp', lambda sl=sl, t0=t0, n=n: nc.sync.dma_start(out=xst[0:n, sl, :], in_=xin[t0:t0 + n, :]),
                          reads=BARK, writes=[('xst', sl)], dma=('xst', sl))
                    for kq in range(4):
                        b = 5 + (ti * 4 + kq) % 3

                        def f(sl=sl, n=n, kq=kq, b=b):
                            for j in range(4):
                                k = kq * 4 + j
                                last = PE.transpose(out=PS[b][:, j * 128:j * 128 + n], in_=xst[0:n, sl, k * 128:(k + 1) * 128],
                                                    identity=identF[0:n, 0:n])
                            return last
                        S.add('pe', f, reads=[('xst', sl), ('identF',)], writes=[('ps', b)])
                        hk = []
                        for j in range(4):
                            hk += hkeys(kq * 4 + j, t0, n)
                        cp = (lambda kq=kq, t0=t0, n=n, b=b: A.copy(out=hT[:, kq * 4:kq * 4 + 4, t0:t0 + n],
                                                                   in_=PS[b][:, :].rearrange("p (j c) -> p j c", j=4)[:, :, 0:n]))
                        cpv = (lambda kq=kq, t0=t0, n=n, b=b: V.tensor_copy(out=hT[:, kq * 4:kq * 4 + 4, t0:t0 + n],
                                                                             in_=PS[b][:, :].rearrange("p (j c) -> p j c", j=4)[:, :, 0:n]))
                        if kq % 2 == 0:
                            S.add('act', cp, reads=[('ps', b)], writes=hk)
                        else:
                            S.add('dve', cpv, reads=[('ps', b)], writes=hk)
            barrier(extra_writes=[('xst', 0), ('xst', 1)])

            out_slots = []
            for l in range(nl):
                if not _on('norm'):
                    break
                rtk = do_norm(nw, l * KC)
                for k in range(KC):
                    S.add('dve', lambda k=k, l=l: V.scalar_tensor_tensor(out=xn[:, k, :], in0=hT[:, k, :], scalar=nw[:, l * KC + k:l * KC + k + 1],
                                                                         in1=rt[:, :], op0=ALU.mult, op1=ALU.mult),
                          reads=hkeys(k) + rtk + [('nw',)], writes=[('xn', k)])
                barrier(extra_writes=rtk + [('sq', 0), ('sq', 1)])

                if not _on('p15'):
                    break
                T15 = 904
                for u in range(4):
                    slot = w_next()
                    b = next_pb()
                    proj(slot, T15, 160, b)
                    S.add('act', lambda u=u, b=b: A.copy(out=kv32[:, u, :], in_=PS[b][:, 0:160]), reads=[('ps', b)], writes=[('kv32', u)])
                S.add('sp', lambda l=l: nc.sync.dma_start(out=send[l][1024:1280, :].rearrange("(c p) t -> p c t", p=128), in_=kv32[:, 0:2, 0:128]),
                      reads=[('kv32', 0), ('kv32', 1)], writes=[('send', 'k')], dma=('snd',))

                def f15():
                    for u in range(4):
                        PE.transpose(out=PS[7][:, u * 128:(u + 1) * 128], in_=kv32[:, u, 0:128], identity=identF[:, :])
                    for u in range(4):
                        last = PE.transpose(out=PS[6][0:32, u * 128:(u + 1) * 128], in_=kv32[:, u, 128:160], identity=identF[:, :])
                    return last
                S.add('pe', f15, reads=[('kv32', u) for u in range(4)] + [('identF',)], writes=[('ps', 7), ('ps', 6)])
                S.add('act', lambda: A.copy(out=tokp[:, :], in_=PS[7][:, :]), reads=[('ps', 7)], writes=[('tokp',)])
                S.add('dve', lambda: V.tensor_copy(out=toks[0:32, :], in_=PS[6][0:32, :]), reads=[('ps', 6)], writes=[('toks',)])
                S.add('sp', lambda l=l: nc.sync.dma_start(out=pk_out[l, :, :], in_=tokp[:, 0:256]), reads=[('tokp',)], dma=('o_pk',))
                S.add('sp', lambda l=l: nc.sync.dma_start(out=pv_out[l, :, :], in_=tokp[:, 256:512]), reads=[('tokp',)], dma=('o_pk',))
                S.add('sp', lambda l=l: nc.sync.dma_start(out=send[l][1280:1536, :].rearrange("(c s) f -> s c f", s=128),
                                                          in_=tokp[:, 256:512].rearrange("s (c f) -> s c f", c=2)),
                      reads=[('tokp',)], writes=[('send', 'v')], dma=('snd',))
                for s in range(NSEQ):
                    S.add('sp', lambda l=l, s=s: nc.sync.dma_start(out=sk_out[l, s, 120:128, :], in_=toks[8 * s:8 * s + 8, 0:256]),
                          reads=[('toks',)], dma=('o_sk',))
                    S.add('sp', lambda l=l, s=s: nc.sync.dma_start(out=sv_out[l, s, 120:128, :], in_=toks[8 * s:8 * s + 8, 256:512]),
                          reads=[('toks',)], dma=('o_sk',))
                S.add('sp', lambda l=l: nc.sync.dma_start(out=sk_out[l, :, 0:120, :], in_=ck_in[l, :, 8:128, :]), dma=('o_sk',))
                S.add('sp', lambda l=l: nc.sync.dma_start(out=sv_out[l, :, 0:120, :], in_=cv_in[l, :, 8:128, :]), dma=('o_sk',))
                barrier(extra_writes=[('kv32', u) for u in range(4)] + [('tokp',), ('toks',)])

                if not _on('p1'):
                    break
                for h in range(8):
                    lh = l * 8 + h
                    s_q, s_f, s_i = None, None, None
                    S.add('sp', lambda l=l, h=h: nc.sync.dma_start(out=S0[:, :, :], in_=st_in[l, :, h, :, :].rearrange("s k v -> k s v")),
                          writes=[('S0',)], dma=('S0',))
                    S.add('dve', lambda: V.memset(carry[:, :], 0.0), writes=[('carry',)])
                    slots = w_begin(3)
                    def tc_gen(i, t0, n, h=h, lh=lh, l=l, slots=slots):
                        proj(slots[0], t0, n, 0)
                        proj(slots[1], t0, n, 1)
                        proj(slots[2], t0, n, 2)
                        if i == 3:
                            w_release()
                        yield 'proj'
                        S.add('act', lambda n=n: A.activation(out=qs_t[:, 0:n], in_=PS[0][:, 0:n], func=AF.Silu), reads=[('ps', 0)], writes=[('qs',)])
                        S.add('act', lambda n=n: A.activation(out=sg_t[:, 0:n], in_=PS[1][:, 0:n], func=AF.Sigmoid), reads=[('ps', 1)], writes=[('sg',)])
                        S.add('act', lambda n=n: A.copy(out=vT_t[:, 0:n], in_=PS[2][:, 0:n]), reads=[('ps', 2)], writes=[('vT',)])
                        S.add('act', lambda n=n, lh=lh: A.activation(out=la_t[:, 0:n], in_=sg_t[:, 0:n], func=AF.Ln,
                                                                     bias=lb[:, lh:lh + 1], scale=oml[:, lh:lh + 1]),
                              reads=[('sg',), ('lbc',)], writes=[('la',)])
                        S.add('dve', lambda n=n, lh=lh: V.tensor_scalar(out=sg_t[:, 0:n], in0=sg_t[:, 0:n], scalar1=noml[:, lh:lh + 1],
                                                                        scalar2=oml[:, lh:lh + 1], op0=ALU.mult, op1=ALU.add),
                              reads=[('sg',), ('la',), ('lbc',)], writes=[('sg',)])
                        S.add('dve', lambda: V.tensor_copy(out=gx_t[:, 0:1], in_=carry[:, 0:1]), reads=[('carry',)], writes=[('gx',)])
                        S.add('dve', lambda t0=t0, n=n: V.tensor_tensor_scan(out=gx_t[:, 1:1 + n], data0=seg01[:, t0:t0 + n], data1=la_t[:, 0:n],
                                                                             initial=carry[:, 0:1], op0=ALU.mult, op1=ALU.add),
                              reads=[('la',), ('carry',), ('seg01',), ('gx',)], writes=[('gx',)])
                        S.add('dve', lambda n=n: V.tensor_copy(out=carry[:, 0:1], in_=gx_t[:, n:n + 1]), reads=[('gx',)], writes=[('carry',)])
                        S.add('act', lambda n=n: A.activation(out=eg_t[:, 0:n], in_=gx_t[:, 1:1 + n], func=AF.Exp), reads=[('gx',)], writes=[('eg',)])
                        S.add('dve', lambda h=h, t0=t0, n=n: V.tensor_tensor(out=Qd[:, h, t0:t0 + n], in0=qs_t[:, 0:n], in1=eg_t[:, 0:n], op=ALU.mult),
                              reads=[('qs',), ('eg',)], writes=[('Qd', h)])
                        if i < 3:
                            chunks = [(64 * j, 64, 'p', j == 0 and i == 0, False, -1) for j in range(5)]
                        else:
                            chunks = [(0, 64, 'p', False, False, -1), (64, 8, 'p', False, True, -1)] + \
                                     [(72 + 8 * s, 8, 's', True, True, s) for s in range(NSEQ)]
                        for (o, C, kind, segstart, segend, sq) in chunks:
                            ref = zcol[:, 0:1] if kind == 's' else gx_t[:, o:o + 1]
                            S.add('dve', lambda o=o, C=C, ref=ref: V.tensor_scalar(out=la_t[:, o:o + C], in0=gx_t[:, 1 + o:1 + o + C], scalar1=ref,
                                                                                   scalar2=None, op0=ALU.subtract),
                                  reads=[('gx',), ('zc',), ('la',)], writes=[('la',)])
                        S.add('act', lambda n=n: A.activation(out=en_t[:, 0:n], in_=la_t[:, 0:n], func=AF.Exp, scale=-1.0), reads=[('la',)], writes=[('en',)])
                        S.add('act', lambda n=n: A.activation(out=la_t[:, 0:n], in_=la_t[:, 0:n], func=AF.Exp), reads=[('la',), ('en',)], writes=[('la',)])
                        S.add('dve', lambda n=n: V.tensor_tensor(out=qt_t[:, 0:n], in0=qs_t[:, 0:n], in1=la_t[:, 0:n], op=ALU.mult),
                              reads=[('qs',), ('la',)], writes=[('qt',)])
                        S.add('dve', lambda n=n: V.tensor_tensor(out=kt_t[:, 0:n], in0=sg_t[:, 0:n], in1=en_t[:, 0:n], op=ALU.mult),
                              reads=[('sg',), ('en',)], writes=[('kt',)])
                        for (o, C, kind, segstart, segend, sq) in chunks:
                            S.add('dve', lambda o=o, C=C: V.tensor_scalar(out=kh_t[:, o:o + C], in0=kt_t[:, o:o + C], scalar1=la_t[:, o + C - 1:o + C],
                                                                          scalar2=None, op0=ALU.mult),
                                  reads=[('kt',), ('la',)], writes=[('kh',)])
                        yield 'chain'
                        def stageA(ci, o, C, kind, segstart, segend, sq):
                            par = ci % 2

                            def ft(o=o, C=C):
                                PE.transpose(out=PS[4][0:C, 0:128], in_=kh_t[:, o:o + C], identity=identB[:, :])
                                return PE.transpose(out=PS[4][0:C, 128:256], in_=vT_t[:, o:o + C], identity=identB[:, :])
                            S.add('pe', ft, reads=[('kh',), ('vT',), ('identB',)], writes=[('ps', 4)])
                            S.add('act', lambda C=C, par=par: A.copy(out=khT[0:C, par, :], in_=PS[4][0:C, 0:128]),
                                  reads=[('ps', 4)], writes=[('khT', par)])
                            S.add('dve', lambda C=C, par=par: V.tensor_copy(out=vtk2[0:C, par, :], in_=PS[4][0:C, 128:256]),
                                  reads=[('ps', 4)], writes=[('vtk2', par)])
                            S.add('pe', lambda o=o, C=C: PE.matmul(PS[3][0:C, 0:C], lhsT=kt_t[:, o:o + C], rhs=qt_t[:, o:o + C], start=True, stop=True),
                                  reads=[('kt',), ('qt',)], writes=[('ps', 3)])
                            S.add('dve', lambda C=C, par=par: V.tensor_tensor(out=att[0:C, par, 0:C], in0=PS[3][0:C, 0:C], in1=tril[0:C, 0:C], op=ALU.mult),
                                  reads=[('ps', 3), ('tril',)], writes=[('att', par)])

                        def stageB(ci, o, C, kind, segstart, segend, sq, i=i, t0=t0, h=h, l=l):
                            par = ci % 2
                            sb = 6 + par
                            S.add('pe', lambda C=C, par=par, sb=sb: PE.matmul(PS[sb][:, 0:128], lhsT=khT[0:C, par, :], rhs=vtk2[0:C, par, :], start=True, stop=True),
                                  reads=[('khT', par), ('vtk2', par)], writes=[('ps', sb)])
                            if kind == 's':
                                S.add('act', lambda sq=sq: A.copy(out=Sbf[:, :], in_=S0[:, sq, :]), reads=[('S0',)], writes=[('Sbf',)])
                            use_s = (kind == 's') or (not segstart)

                            def fo(o=o, C=C, par=par, use_s=use_s):
                                if use_s:
                                    PE.matmul(PS[5][:, 0:C], lhsT=Sbf[:, :], rhs=qt_t[:, o:o + C], start=True, stop=False)
                                return PE.matmul(PS[5][:, 0:C], lhsT=vtk2[0:C, par, :], rhs=att[0:C, par, 0:C], start=(not use_s), stop=True)
                            S.add('pe', fo, reads=[('Sbf',), ('qt',), ('vtk2', par), ('att', par)], writes=[('ps', 5)])
                            S.add('act', lambda o=o, C=C: A.copy(out=oloc[:, h, t0 + o:t0 + o + C], in_=PS[5][:, 0:C]),
                                  reads=[('ps', 5)], writes=[('oloc', h)])
                            elast = la_t[:, o + C - 1:o + C]
                            if kind == 's':
                                S.add('dve', lambda sq=sq, sb=sb, par=par, elast=elast: V.scalar_tensor_tensor(
                                    out=sfst[:, par, :], in0=S0[:, sq, :], scalar=elast, in1=PS[sb][:, 0:128], op0=ALU.mult, op1=ALU.add),
                                    reads=[('S0',), ('la',), ('ps', sb)], writes=[('sfst', par)])
                                S.add('sp', lambda sq=sq, par=par: nc.sync.dma_start(out=ss_out[l, sq, h, :, :], in_=sfst[:, par, :]),
                                      reads=[('sfst', par)], dma=('o_ss', par))
                            else:
                                if segstart:
                                    S.add('dve', lambda sb=sb: V.tensor_copy(out=Sst[:, :], in_=PS[sb][:, 0:128]),
                                          reads=[('ps', sb)], writes=[('Sst',)])
                                else:
                                    S.add('dve', lambda sb=sb, elast=elast: V.scalar_tensor_tensor(
                                        out=Sst[:, :], in0=Sst[:, :], scalar=elast, in1=PS[sb][:, 0:128], op0=ALU.mult, op1=ALU.add),
                                        reads=[('Sst',), ('la',), ('ps', sb)], writes=[('Sst',)])
                                if segend:
                                    S.add('sp', lambda: nc.sync.dma_start(out=send[l][h * 128:(h + 1) * 128, :], in_=Sst[:, :]),
                                          reads=[('Sst',)], writes=[('send', h)], dma=('snd',))
                                    S.add('dve', lambda o=o, C=C: V.tensor_copy(out=Dsave[:, h:h + 1], in_=eg_t[:, o + C - 1:o + C]),
                                          reads=[('eg',)], writes=[('Dsave', h)])
                                else:
                                    S.add('act', lambda: A.copy(out=Sbf[:, :], in_=Sst[:, :]), reads=[('Sst',)], writes=[('Sbf',)])

                        stageA(0, *chunks[0])
                        for ci in range(len(chunks)):
                            if ci + 1 < len(chunks):
                                stageA(ci + 1, *chunks[ci + 1])
                            stageB(ci, *chunks[ci])
                        yield 'chunks'

                    gens = [tc_gen(i_, t0_, n_) for i_, (t0_, n_) in enumerate(TCH)]
                    next(gens[0])
                    for i_ in range(len(TCH)):
                        next(gens[i_])
                        if i_ + 1 < len(TCH):
                            next(gens[i_ + 1])
                        next(gens[i_])

                if not _on('ex'):
                    break
                for h in range(8):
                    lh = l * 8 + h
                    slot = w_next()
                    S.add('sp', lambda l=l, h=h: nc.sync.dma_start(out=S0[:, 0, :], in_=recv[l][h * 128:(h + 1) * 128, :]),
                          reads=RECV_KEYS, writes=[('S0',)], dma=('S0',))
                    S.add('sp', lambda l=l, h=h: nc.sync.dma_start(out=S0[:, 1, :], in_=send[l][h * 128:(h + 1) * 128, :]),
                          reads=[('send', h)], writes=[('S0',)], dma=('S0',))
                    S.add('dve', lambda: V.tensor_scalar(out=S0[:, 0, :], in0=S0[:, 0, :], scalar1=flag[:, 0:1], scalar2=None, op0=ALU.mult),
                          reads=[('S0',), ('flag',)], writes=[('S0',)])
                    S.add('act', lambda: A.copy(out=Sbf[:, :], in_=S0[:, 0, :]), reads=[('S0',)], writes=[('Sbf',)])
                    par = h % 2
                    S.add('dve', lambda h=h, par=par: V.scalar_tensor_tensor(out=sfst[:, par, :], in0=S0[:, 0, :], scalar=Dsave[:, h:h + 1], in1=S0[:, 1, :],
                                                                             op0=ALU.mult, op1=ALU.add),
                          reads=[('S0',), ('Dsave', h)], writes=[('sfst', par)])
                    S.add('sp', lambda l=l, h=h, par=par: nc.sync.dma_start(out=ps_out[l, h, :, :], in_=sfst[:, par, :]),
                          reads=[('sfst', par)], dma=('o_ss', par))
                    for i, (t0, n) in enumerate(TCH):
                        b = next_pb()
                        proj(slot, t0, n, b)
                        S.add('act', lambda t0=t0, n=n, b=b: A.activation(out=sgr[:, t0:t0 + n], in_=PS[b][:, 0:n], func=AF.Silu),
                              reads=[('ps', b)], writes=[('sgr', i)])
                    for i, (t0, n) in enumerate(TCH):
                        npr = min(n, TP - t0)
                        S.add('pe', lambda h=h, t0=t0, npr=npr: PE.matmul(PS[5][:, 0:npr], lhsT=Sbf[:, :], rhs=Qd[:, h, t0:t0 + npr], start=True, stop=True),
                              reads=[('Sbf',), ('Qd', h)], writes=[('ps', 5)])
                        S.add('dve', lambda h=h, t0=t0, npr=npr: V.tensor_tensor(out=qs_t[:, 0:npr], in0=PS[5][:, 0:npr], in1=oloc[:, h, t0:t0 + npr], op=ALU.add),
                              reads=[('ps', 5), ('oloc', h)], writes=[('of',)])
                        if npr < n:
                            S.add('dve', lambda h=h, t0=t0, n=n, npr=npr: V.tensor_copy(out=qs_t[:, npr:n], in_=oloc[:, h, t0 + npr:t0 + n]),
                                  reads=[('oloc', h), ('of',)], writes=[('of',)])
                        S.add('act', lambda n=n: A.activation(out=qt_t[:, 0:n], in_=qs_t[:, 0:n], func=AF.Square), reads=[('of',)], writes=[('osq',)])
                        S.add('pe', lambda n=n: PE.matmul(PS[6][:, 0:n], lhsT=onesB[:, :], rhs=qt_t[:, 0:n], start=True, stop=True),
                              reads=[('osq',), ('onesB',)], writes=[('ps', 6)])
                        S.add('act', lambda n=n: A.activation(out=sg_t[:, 0:n], in_=PS[6][:, 0:n], func=AF.Sqrt, bias=epsT[:, 0:1], scale=1.0 / 128),
                              reads=[('ps', 6), ('eps',)], writes=[('rr',)])
                        S.add('dve', lambda n=n: V.reciprocal(out=sg_t[:, 0:n], in_=sg_t[:, 0:n]), reads=[('rr',)], writes=[('rr',)])
                        S.add('dve', lambda n=n: V.tensor_tensor(out=qs_t[:, 0:n], in0=qs_t[:, 0:n], in1=sg_t[:, 0:n], op=ALU.mult),
                              reads=[('of',), ('rr',)], writes=[('of',)])
                        S.add('dve', lambda h=h, lh=lh, t0=t0, n=n: V.scalar_tensor_tensor(out=mix[:, 8 + h, t0:t0 + n], in0=qs_t[:, 0:n], scalar=hw[:, lh:lh + 1],
                                                                                          in1=sgr[:, t0:t0 + n], op0=ALU.mult, op1=ALU.mult),
                              reads=[('of',), ('hw',), ('sgr', i)], writes=[('mix', 8 + h, i)])
                barrier(extra_writes=[('oloc', h) for h in range(8)] + [('Qd', h) for h in range(8)])

                if not _on('p3'):
                    break
                for g in range(4):
                    for jp in range(4):
                        hsrc = 4 * g + 2 * (jp % 2) + jp // 2
                        S.add('sp', lambda jp=jp, hsrc=hsrc: nc.sync.dma_start(out=emg[:, jp, :], in_=emd[hsrc, :, :]),
                              reads=[('emd',)], writes=[('emg',)], dma=('emg',))
                    for hf in range(2):
                        S.add('pool', lambda l=l, g=g, hf=hf: nc.gpsimd.dma_start(out=kd[64 * hf:64 * hf + 64, 0:128], in_=recv[l][1024 + 64 * g:1024 + 64 * g + 64, :]),
                              reads=RECV_KEYS + BARK, writes=[('kd', 'halo')], dma=('halo',))
                        S.add('pool', lambda l=l, g=g, hf=hf: nc.gpsimd.dma_start(
                            out=vtk[:, 0, 64 * hf:64 * hf + 64], in_=recv[l][1280 + 128 * (g // 2):1280 + 128 * (g // 2) + 128, 64 * (g % 2):64 * (g % 2) + 64]),
                            reads=RECV_KEYS + BARK, writes=[('vtk', 0)], dma=('halo',))
                        S.add('pool', lambda l=l, g=g, hf=hf: nc.gpsimd.dma_start(
                            out=vtks[:, :, 64 * hf:64 * hf + 64], in_=cv_in[l, :, :, 64 * g:64 * g + 64].rearrange("s t f -> t s f")),
                            reads=BARK, writes=[('vtks',)], dma=('halo',))
                    S.add('dve', lambda: V.tensor_scalar(out=vtk[:, 0, :], in0=vtk[:, 0, :], scalar1=flag[:, 0:1], scalar2=None, op0=ALU.mult),
                          reads=[('vtk', 0), ('flag',)], writes=[('vtk', 0)])
                    for s in range(NSEQ):
                        sl = s % 2
                        for hf in range(2):
                            S.add('pool', lambda l=l, g=g, s=s, sl=sl, hf=hf: nc.gpsimd.dma_start(out=ckst[:, sl, 64 * hf:64 * hf + 64], in_=ck_in[l, s, :, 64 * g:64 * g + 64]),
                                  reads=BARK, writes=[('ckst', sl)], dma=('ckst', sl))
                        S.add('pe', lambda sl=sl: PE.transpose(out=PS[4][:, 256 * sl:256 * sl + 128], in_=ckst[:, sl, :], identity=identB[:, :]),
                              reads=[('ckst', sl), ('identB',)], writes=[('ps', 4)])
                        S.add('act', lambda s=s, sl=sl: A.copy(out=kds[:, s, :], in_=PS[4][:, 256 * sl:256 * sl + 128]), reads=[('ps', 4)], writes=[('kds', s)])
                    s_q0 = w_next()
                    for i, (t0, n) in enumerate(TCH):
                        b = next_pb()
                        proj(s_q0, t0, n, b)
                        S.add('act', lambda t0=t0, n=n, b=b: A.copy(out=qa[:, 0, t0:t0 + n], in_=PS[b][:, 0:n]), reads=[('ps', b)], writes=[('qa', 0, i)])
                    s_q1 = w_next()
                    for i, (t0, n) in enumerate(TCH):
                        b = next_pb()
                        proj(s_q1, t0, n, b)
                        S.add('dve', lambda t0=t0, n=n, b=b: V.tensor_copy(out=qa[:, 1, t0:t0 + n], in_=PS[b][:, 0:n]), reads=[('ps', b)], writes=[('qa', 1, i)])
                    s_k = w_next()
                    for i, (t0, n) in enumerate(TCH):
                        b = next_pb()
                        proj(s_k, t0, n, b)
                        S.add('act', lambda t0=t0, n=n, b=b: A.copy(out=kd[:, 128 + t0:128 + t0 + n], in_=PS[b][:, 0:n]), reads=[('ps', b)], writes=[('kd', i)])
                    s_v = w_next()
                    for i, (t0, n) in enumerate(TCH):
                        b = next_pb()
                        proj(s_v, t0, n, b)
                        S.add('dve', lambda t0=t0, n=n, b=b: V.tensor_copy(out=vdT[:, t0:t0 + n], in_=PS[b][:, 0:n]), reads=[('ps', b)], writes=[('vdT', i)])
                    for jj in range(2):
                        s_g = w_next()
                        for i, (t0, n) in enumerate(TCH):
                            b = next_pb()
                            proj(s_g, t0, n, b)
                            S.add('act', lambda jj=jj, t0=t0, n=n, b=b: A.activation(out=sga[:, jj, t0:t0 + n], in_=PS[b][:, 0:n], func=AF.Silu),
                                  reads=[('ps', b)], writes=[('sga', jj, i)])
                    VD_ALL = [('vdT', i) for i in range(4)]
                    KD_ALL = [('kd', i) for i in range(4)] + [('kd', 'halo')]
                    QA_ALL = [('qa', jj, i) for jj in range(2) for i in range(4)]
                    SGA_ALL = [('sga', jj, i) for jj in range(2) for i in range(4)]
                    blocks = [(128 * i, 128) for i in range(8)] + [(1024, 8)]
                    for bi, (t0, n) in enumerate(blocks):
                        sl = bi % 2
                        S.add('pe', lambda t0=t0, n=n, sl=sl: PE.transpose(out=PS[4][0:n, 256 * sl:256 * sl + 128], in_=vdT[:, t0:t0 + n], identity=identB[:, :]),
                              reads=VD_ALL + [('identB',)], writes=[('ps', 4)])
                        if bi % 2 == 0:
                            S.add('act', lambda bi=bi, n=n, sl=sl: A.copy(out=vtk[0:n, 1 + bi, :], in_=PS[4][0:n, 256 * sl:256 * sl + 128]),
                                  reads=[('ps', 4)], writes=[('vtk', 1 + bi)])
                        else:
                            S.add('dve', lambda bi=bi, n=n, sl=sl: V.tensor_copy(out=vtk[0:n, 1 + bi, :], in_=PS[4][0:n, 256 * sl:256 * sl + 128]),
                                  reads=[('ps', 4)], writes=[('vtk', 1 + bi)])
                    for s in range(NSEQ):
                        sl = s % 2
                        t0 = TP + 8 * s
                        S.add('pe', lambda t0=t0, sl=sl: PE.transpose(out=PS[4][0:8, 256 * sl:256 * sl + 128], in_=vdT[:, t0:t0 + 8], identity=identB[:, :]),
                              reads=VD_ALL + [('identB',)], writes=[('ps', 4)])
                        S.add('act', lambda s=s, sl=sl: A.copy(out=vtkn[0:8, s, :], in_=PS[4][0:8, 256 * sl:256 * sl + 128]),
                              reads=[('ps', 4)], writes=[('vtkn', s)])
                    ablocks = []
                    for bi, (t0, n) in enumerate(blocks):
                        ablocks.append(dict(t0=t0, nq=n, kp=kd[:, 128 * bi:128 * bi + 128], kc=kd[:, 128 + t0:128 + t0 + n],
                                            vp=vtk[:, bi, :], vc=vtk[0:n, 1 + bi, :], op=(onesFl if bi == 0 else onesB),
                                            rk=[('vtk', bi), ('vtk', 1 + bi), ('onesFl',), ('onesB',)]))
                    for s in range(NSEQ):
                        t0 = TP + 8 * s
                        ablocks.append(dict(t0=t0, nq=8, kp=kds[:, s, :], kc=kd[:, 128 + t0:128 + t0 + 8],
                                            vp=vtks[:, s, :], vc=vtkn[0:8, s, :], op=onesB,
                                            rk=[('kds', s), ('vtks',), ('vtkn', s), ('onesB',)]))
                    def blk_gen(ab, par, g=g, l=l):
                        t0, nq = ab['t0'], ab['nq']
                        Ebp, Pbp = Eb2[par], Pb2[par]

                        def fs(ab=ab, t0=t0, nq=nq):
                            for j in range(4):
                                jj, hf = j // 2, j % 2
                                bank = 5 + hf
                                base = jj * 256
                                PE.matmul(PS[bank][:, base:base + nq], lhsT=ab['kp'][64 * hf:64 * hf + 64, :], rhs=qa[64 * hf:64 * hf + 64, jj, t0:t0 + nq],
                                          start=True, stop=True)
                                last = PE.matmul(PS[bank][0:nq, base + 128:base + 128 + nq], lhsT=ab['kc'][64 * hf:64 * hf + 64, :],
                                                 rhs=qa[64 * hf:64 * hf + 64, jj, t0:t0 + nq], start=True, stop=True)
                            return last
                        S.add('pe', fs, reads=KD_ALL + QA_ALL + ab['rk'], writes=[('ps', 5), ('ps', 6)])
                        for jj in range(2):
                            bank = 5 + jj
                            S.add('act', lambda jj=jj, bank=bank, nq=nq: A.activation(
                                out=Ebp[:, 2 * jj:2 * jj + 2, 0, 0:nq], in_=PS[bank][:, :].rearrange("p (h c t) -> p h c t", h=2, c=2)[:, :, 0, 0:nq],
                                func=AF.Exp, scale=0.125), reads=[('ps', bank)], writes=[('E', jj, 0, par)])
                            S.add('act', lambda jj=jj, bank=bank, nq=nq: A.activation(
                                out=Ebp[0:nq, 2 * jj:2 * jj + 2, 1, 0:nq], in_=PS[bank][:, :].rearrange("p (h c t) -> p h c t", h=2, c=2)[0:nq, :, 1, 0:nq],
                                func=AF.Exp, scale=0.125), reads=[('ps', bank)], writes=[('E', jj, 1, par)])
                        yield 'a'
                        emv = emg[:, :, :].rearrange("p h (c t) -> p h c t", c=2)
                        S.add('dve', lambda nq=nq, emv=emv: V.tensor_tensor(out=Pbp[:, :, 0, 0:nq], in0=Ebp[:, :, 0, 0:nq], in1=emv[:, :, 0, 0:nq], op=ALU.mult),
                              reads=[('E', 0, 0, par), ('E', 1, 0, par), ('emg',)], writes=[('P', 0, par)])
                        S.add('dve', lambda nq=nq, emv=emv: V.tensor_tensor(out=Pbp[0:nq, :, 1, 0:nq], in0=Ebp[0:nq, :, 1, 0:nq], in1=emv[0:nq, :, 1, 0:nq], op=ALU.mult),
                              reads=[('E', 0, 1, par), ('E', 1, 1, par), ('emg',)], writes=[('P', 1, par)])

                        yield 'b'
                        def fpv(ab=ab, nq=nq):
                            for j in range(4):
                                PE.matmul(PS[3][:, 128 * j:128 * j + nq], lhsT=ab['vp'], rhs=Pbp[:, j, 0, 0:nq], start=True, stop=False)
                                PE.matmul(PS[3][:, 128 * j:128 * j + nq], lhsT=ab['vc'], rhs=Pbp[0:nq, j, 1, 0:nq], start=False, stop=True)
                            for j in range(4):
                                PE.matmul(PS[7][:, 128 * j:128 * j + nq], lhsT=ab['op'][:, :], rhs=Pbp[:, j, 0, 0:nq], start=True, stop=False)
                                last = PE.matmul(PS[7][:, 128 * j:128 * j + nq], lhsT=onesB[0:nq, :], rhs=Pbp[0:nq, j, 1, 0:nq], start=False, stop=True)
                            return last
                        S.add('pe', fpv, reads=[('P', 0, par), ('P', 1, par)] + ab['rk'], writes=[('ps', 3), ('ps', 7)])
                        es = esink[:, l * 16 + 4 * g:l * 16 + 4 * g + 4]
                        S.add('dve', lambda nq=nq, es=es: V.tensor_tensor(out=den[:, :, 0:nq], in0=PS[7][:, :].rearrange("p (h t) -> p h t", h=4)[:, :, 0:nq],
                                                                          in1=es.rearrange("p (h o) -> p h o", o=1).to_broadcast([128, 4, nq]),
                                                                          op=ALU.add),
                              reads=[('ps', 7), ('esink',)], writes=[('den',)])
                        S.add('dve', lambda nq=nq: V.reciprocal(out=den[:, :, 0:nq], in_=den[:, :, 0:nq]), reads=[('den',)], writes=[('den',)])
                        S.add('dve', lambda nq=nq: V.tensor_tensor(out=onr[:, :, 0:nq], in0=PS[3][:, :].rearrange("p (h t) -> p h t", h=4)[:, :, 0:nq],
                                                                   in1=den[:, :, 0:nq], op=ALU.mult),
                              reads=[('ps', 3), ('den',)], writes=[('onr',)])
                        tcs = [i for i, (a, m) in enumerate(TCH) if a < t0 + nq and t0 < a + m]
                        for hf in range(2):
                            S.add('dve', lambda g=g, hf=hf, t0=t0, nq=nq: V.tensor_tensor(
                                out=mix[64 * hf:64 * hf + 64, 2 * g:2 * g + 2, t0:t0 + nq],
                                in0=onr[64 * hf:64 * hf + 64, 2 * hf:2 * hf + 2, 0:nq],
                                in1=sga[64 * hf:64 * hf + 64, :, t0:t0 + nq], op=ALU.mult),
                                reads=[('onr',)] + SGA_ALL, writes=[('mix', 2 * g + jj, i) for jj in range(2) for i in tcs])
                        yield 'c'

                    gens = [blk_gen(ab_, i_ % 2) for i_, ab_ in enumerate(ablocks)]
                    next(gens[0])
                    next(gens[0])
                    for i_ in range(len(ablocks)):
                        if i_ + 1 < len(ablocks):
                            next(gens[i_ + 1])
                        next(gens[i_])
                        if i_ + 1 < len(ablocks):
                            next(gens[i_ + 1])
                if not _on('out'):
                    break
                MIX_ALL = [('mix', c, i) for c in range(KC) for i in range(4)]
                for n_ in range(KC):
                    slot = w_next()
                    for i, (t0, n) in enumerate(TCH):
                        b = next_pb()

                        def fo2(slot=slot, t0=t0, n=n, b=b):
                            for c in range(KC):
                                last = PE.matmul(PS[b][:, 0:n], lhsT=wbuf[:, slot, c, :], rhs=mix[:, c, t0:t0 + n], start=(c == 0), stop=(c == KC - 1))
                            return last
                        S.add('pe', fo2, reads=[('w', slot)] + [('mix', c, i) for c in range(KC)], writes=[('ps', b)])
                        S.add('dve', lambda n_=n_, t0=t0, n=n, b=b: V.tensor_tensor(out=hT[:, n_, t0:t0 + n], in0=hT[:, n_, t0:t0 + n], in1=PS[b][:, 0:n], op=ALU.add),
                              reads=[('ps', b), ('h', n_, i)], writes=[('h', n_, i)])
                barrier(extra_writes=[('kd', 'halo'), ('vtk', 0), ('vtks',), ('ckst', 0), ('ckst', 1), ('emg',)])

            if _on('fin'):
                rtk = do_norm(fw, 0)
                barrier(extra_writes=[('sq', 0), ('sq', 1)])
                for ti, (t0, n) in enumerate(TT):
                    sl = ti % 2
                    for kq in range(4):
                        b = 5 + (ti * 4 + kq) % 3
                        for j in range(4):
                            k = kq * 4 + j
                            yp = (ti * 16 + k) % 2
                            S.add('dve', lambda k=k, t0=t0, n=n, yp=yp: V.scalar_tensor_tensor(out=ynt[:, yp, 0:n], in0=hT[:, k, t0:t0 + n], scalar=fw[:, k:k + 1],
                                                                                              in1=rt[:, t0:t0 + n], op0=ALU.mult, op1=ALU.mult),
                                  reads=hkeys(k, t0, n) + rtk + [('fw',)], writes=[('ynt', yp)])
                            S.add('pe', lambda j=j, n=n, yp=yp, b=b: PE.transpose(out=PS[b][0:n, j * 128:(j + 1) * 128], in_=ynt[:, yp, 0:n], identity=identF[:, :]),
                                  reads=[('ynt', yp), ('identF',)], writes=[('ps', b)])
                        if kq % 2 == 0:
                            S.add('act', lambda sl=sl, kq=kq, n=n, b=b: A.copy(out=xst[0:n, sl, kq * 512:(kq + 1) * 512], in_=PS[b][0:n, :]),
                                  reads=[('ps', b)], writes=[('xst', sl)])
                        else:
                            S.add('dve', lambda sl=sl, kq=kq, n=n, b=b: V.tensor_copy(out=xst[0:n, sl, kq * 512:(kq + 1) * 512], in_=PS[b][0:n, :]),
                                  reads=[('ps', b)], writes=[('xst', sl)])
                    S.add('sp', lambda sl=sl, t0=t0, n=n: nc.sync.dma_start(out=y_out[t0:t0 + n, :], in_=xst[0:n, sl, :]),
                          reads=[('xst', sl)], dma=('o_y', sl))

        for p_ in range(2):
            run_pass(p_, xin_p[p_], ck_p[p_], cv_p[p_], st_p[p_], y_p[p_], pk_p[p_], pv_p[p_], ps_p[p_], sk_p[p_], sv_p[p_], ss_p[p_], send_p[p_], send_p[0])

        S.emit([('o_y', 0), ('o_y', 1), ('o_pk',), ('o_sk',), ('o_ss', 0), ('o_ss', 1), ('snd',), ('emd',)])
    return nc


def _t5_bucket(dist):
    d = np.maximum(dist, 0)
    df = np.maximum(d, 1).astype(np.float32)
    large = 16 + (np.log(df / np.float32(16)) / np.float32(np.log(128 / 16)) * np.float32(16)).astype(np.int32)
    large = np.minimum(large, 31)
    return np.where(d < 16, d, large)


_NC_CACHE = {}


def kernel(x_prompt, x_sample, cache_k, cache_v, state_h, meta_tokens, w_in, w_out, norm_w,
           final_norm_w, attn_sinks, rel_bias_table, hgrn_lb_logits, hgrn_norm_w, _nl=DEPTH):
    f32 = np.float32
    x_prompt = np.asarray(x_prompt, f32)
    x_sample = np.asarray(x_sample, f32)
    cache_k = np.asarray(cache_k, f32).reshape(DEPTH, 32, 128, 256)
    cache_v = np.asarray(cache_v, f32).reshape(DEPTH, 32, 128, 256)
    state_h = np.asarray(state_h, f32)
    w_in = np.asarray(w_in, f32)
    w_out = np.asarray(w_out, f32)
    rel = np.asarray(rel_bias_table, f32)

    def pk(a, n):
        a = np.asarray(a, f32).reshape(-1, n, 128)
        return np.ascontiguousarray(a.transpose(2, 0, 1).reshape(128, -1))

    nw = pk(norm_w, KC)
    fw = pk(final_norm_w, KC)
    lbl = pk(hgrn_lb_logits, 8)
    hw = pk(hgrn_norm_w, 8)
    perm = np.array([4 * g_ + 2 * (jp_ % 2) + jp_ // 2 for g_ in range(4) for jp_ in range(4)])
    sink = np.ascontiguousarray(np.broadcast_to(np.asarray(attn_sinks, f32)[:, perm].reshape(1, -1), (128, DEPTH * 16)))
    s_ = np.arange(128)[:, None]
    t_ = np.arange(128)[None, :]
    dprev = np.clip(128 + t_ - s_, 0, 127)
    dcur = np.clip(t_ - s_, 0, 127)
    bidx = _t5_bucket(np.arange(128))
    bias_d = rel[bidx]
    biasT = np.empty((16, 128, 256), f32)
    biasT[:, :, 0:128] = bias_d[dprev].transpose(2, 0, 1)
    biasT[:, :, 128:256] = bias_d[dcur].transpose(2, 0, 1)
    mask01 = np.zeros((128, 256), f32)
    mask01[:, 0:128] = (s_ > t_)
    mask01[:, 128:256] = (s_ <= t_)
    tril = np.zeros((128, 64), f32)
    tril[0:64, :] = (np.arange(64)[:, None] <= np.arange(64)[None, :])
    seg01 = np.ones((128, T), f32)
    for s in range(NSEQ):
        seg01[:, TP + 8 * s] = 0.0
    identf = np.eye(128, dtype=f32)

    NCORE = int(os.environ.get('KNCORE', '4'))
    wi = w_in[:_nl] if _nl < DEPTH else w_in
    wo = w_out[:_nl] if _nl < DEPTH else w_out
    if os.environ.get('KSMALLW', '0') == '1':
        wi = np.ascontiguousarray(w_in[:1, :128, :128])
        wo = np.ascontiguousarray(w_out[:1, :128, :128])
    in_maps = []
    for c in range(NCORE):
        full = np.concatenate([np.asarray(meta_tokens, f32), x_prompt[c]], axis=0)
        m = dict(w_in=wi, w_out=wo, nw=nw, fw=fw, sink=sink, biasT=biasT, mask01=mask01, tril=tril,
                 seg01=seg01, lbl=lbl, hw=hw, identf=identf)
        for p in range(2):
            s0 = 8 * c + 4 * p
            m["xin%d" % p] = np.ascontiguousarray(np.concatenate([full[p * TP:(p + 1) * TP], x_sample[s0:s0 + 4].reshape(TS, D)], axis=0))
            m["ck%d" % p] = np.ascontiguousarray(cache_k[:, s0:s0 + 4])
            m["cv%d" % p] = np.ascontiguousarray(cache_v[:, s0:s0 + 4])
            m["st%d" % p] = np.ascontiguousarray(state_h[:, s0:s0 + 4])
        in_maps.append(m)
    if _nl not in _NC_CACHE:
        _NC_CACHE[_nl] = build_nc(_nl)
    nc = _NC_CACHE[_nl]
    res = run_bass_kernel_spmd(nc, in_maps, core_ids=list(range(NCORE)))
    R = res.results
    y_prompt = np.empty((4, 2048, D), f32)
    y_sample = np.empty((32, 8, D), f32)
    pkk = np.empty((DEPTH, 4, 128, 4, 64), f32)
    pvv = np.empty((DEPTH, 4, 128, 4, 64), f32)
    pss = np.empty((DEPTH, 4, 8, 128, 128), f32)
    skk = np.empty((DEPTH, 32, 128, 4, 64), f32)
    svv = np.empty((DEPTH, 32, 128, 4, 64), f32)
    sss = np.empty((DEPTH, 32, 8, 128, 128), f32)
    for c in range(NCORE):
        y0, y1 = R[c]["y0"], R[c]["y1"]
        y_prompt[c, 0:TP - 16] = y0[16:TP]
        y_prompt[c, TP - 16:] = y1[0:TP]
        pkk[:, c] = R[c]["pk1"].reshape(DEPTH, 128, 4, 64)
        pvv[:, c] = R[c]["pv1"].reshape(DEPTH, 128, 4, 64)
        pss[:, c] = R[c]["ps1"]
        for p in range(2):
            s0 = 8 * c + 4 * p
            y_sample[s0:s0 + 4] = R[c]["y%d" % p][TP:].reshape(4, 8, D)
            skk[:, s0:s0 + 4] = R[c]["sk%d" % p].reshape(DEPTH, 4, 128, 4, 64)
            svv[:, s0:s0 + 4] = R[c]["sv%d" % p].reshape(DEPTH, 4, 128, 4, 64)
            sss[:, s0:s0 + 4] = R[c]["ss%d" % p]
    return (y_prompt, y_sample, pkk, pvv, pss, skk, svv, sss)
```

```python
import contextlib
import numpy as np
import concourse.bass as bass
import concourse.mybir as mybir
from concourse.bass_utils import run_bass_kernel_spmd

F32 = mybir.dt.float32
BF16 = mybir.dt.bfloat16
AF = mybir.ActivationFunctionType
ALU = mybir.AluOpType

DEPTH = 4
D = 2048
KC = 16
TP = 1032
NSEQ = 4
TS = 32
T = TP + TS
DIN = 6656
TCH = [(0, 320), (320, 320), (640, 320), (960, 104)]
EPS = 1e-6
NS = 3
C_QA, C_KA, C_VA, C_GA, C_QR, C_FR, C_IR, C_GR = 0, 1024, 1280, 1536, 2560, 3584, 4608, 5632
SEND_ROWS = 1536
import os
USE_CC = os.environ.get('KNOCC', '0') != '1'
STOP = os.environ.get('KSTOP', '')
P1L = int(os.environ.get('KP1', '9'))
PH = ['const', 'p0', 'norm', 'p15', 'p1', 'ex', 'p2', 'p3', 'out', 'fin']


def _on(ph):
    return STOP == '' or PH.index(ph) <= PH.index(STOP)


class Sched:
    def __init__(self, nc, stack):
        self.nc = nc
        self.stack = stack
        self.ops = []
        self.lastw = {}
        self.rds = {}
        self.eng = {'pe': nc.tensor, 'act': nc.scalar, 'dve': nc.vector, 'pool': nc.gpsimd, 'sp': nc.sync}
        self.sem = {e: stack.enter_context(nc.semaphore("s_" + e)) for e in self.eng}
        self.dsem = {}

    def add(self, eng, fn, reads=(), writes=(), dma=None):
        idx = len(self.ops)
        psr = [r for r in reads if r[0] == 'ps']
        if psr:
            reads = [r for r in reads if r[0] != 'ps']
            writes = list(writes) + [r for r in psr if r not in writes]
        deps = set()
        for r in reads:
            w = self.lastw.get(r)
            if w is not None:
                deps.add(w)
        for r in writes:
            w = self.lastw.get(r)
            if w is not None:
                deps.add(w)
            for x in self.rds.get(r, ()):
                deps.add(x)
        deps.discard(idx)
        if dma is not None and dma not in self.dsem:
            self.dsem[dma] = self.stack.enter_context(self.nc.semaphore("d%d" % len(self.dsem)))
        self.ops.append(dict(eng=eng, fn=fn, deps=deps, dma=dma, signal=False, sidx=0))
        for r in reads:
            self.rds.setdefault(r, []).append(idx)
        for r in writes:
            self.lastw[r] = idx
            self.rds[r] = []
        return idx

    def emit(self, final_slots):
        ops = self.ops

        def skip(p, op):
            return p['dma'] is None and op['dma'] is None and p['eng'] == 'pe' and op['eng'] == 'pe'

        for op in ops:
            for d in op['deps']:
                if not skip(ops[d], op):
                    ops[d]['signal'] = True
        cnt = {e: 0 for e in self.eng}
        dcnt = {}
        waited = {}
        for op in ops:
            e = op['eng']
            E = self.eng[e]
            need = {}
            for d in op['deps']:
                p = ops[d]
                if skip(p, op):
                    continue
                if p['dma'] is not None:
                    key = ('d', p['dma'])
                    val = 16 * dcnt[p['dma']]
                else:
                    key = ('e', p['eng'])
                    val = p['sidx']
                if val > need.get(key, 0):
                    need[key] = val
            for key, val in need.items():
                if waited.get((e, key), 0) >= val:
                    continue
                sem = self.dsem[key[1]] if key[0] == 'd' else self.sem[key[1]]
                E.wait_ge(sem, val)
                waited[(e, key)] = val
            inst = op['fn']()
            if op['dma'] is not None:
                inst.then_inc(self.dsem[op['dma']], 16)
                dcnt[op['dma']] = dcnt.get(op['dma'], 0) + 1
            elif op['signal']:
                cnt[e] += 1
                inst.then_inc(self.sem[e], 1)
                op['sidx'] = cnt[e]
        sp = self.eng['sp']
        for slot in final_slots:
            if slot in dcnt:
                sp.wait_ge(self.dsem[slot], 16 * dcnt[slot])


def build_nc(nl=DEPTH):
    nc = bass.Bass("TRN2", target_bir_lowering=False)
    dt_in = lambda name, shape: nc.dram_tensor(name, list(shape), F32, kind="ExternalInput").ap()
    dt_out = lambda name, shape: nc.dram_tensor(name, list(shape), F32, kind="ExternalOutput").ap()
    xin_p = [dt_in("xin%d" % p, (T, D)) for p in range(2)]
    ck_p = [dt_in("ck%d" % p, (DEPTH, NSEQ, 128, 256)) for p in range(2)]
    cv_p = [dt_in("cv%d" % p, (DEPTH, NSEQ, 128, 256)) for p in range(2)]
    st_p = [dt_in("st%d" % p, (DEPTH, NSEQ, 8, 128, 128)) for p in range(2)]
    SMALLW = os.environ.get("KSMALLW", "0") == "1"
    w_in = dt_in("w_in", (1, 128, 128) if SMALLW else (nl, D, DIN))
    w_out = dt_in("w_out", (1, 128, 128) if SMALLW else (nl, D, D))
    nw_in = dt_in("nw", (128, DEPTH * KC))
    fw_in = dt_in("fw", (128, KC))
    sink_in = dt_in("sink", (128, DEPTH * 16))
    biasT_in = dt_in("biasT", (16, 128, 256))
    mask_in = dt_in("mask01", (128, 256))
    tril_in = dt_in("tril", (128, 64))
    seg_in = dt_in("seg01", (128, T))
    lbl_in = dt_in("lbl", (128, DEPTH * 8))
    hw_in = dt_in("hw", (128, DEPTH * 8))
    identf_in = dt_in("identf", (128, 128))

    y_p = [dt_out("y%d" % p, (T, D)) for p in range(2)]
    pk_p = [dt_out("pk%d" % p, (DEPTH, 128, 256)) for p in range(2)]
    pv_p = [dt_out("pv%d" % p, (DEPTH, 128, 256)) for p in range(2)]
    ps_p = [dt_out("ps%d" % p, (DEPTH, 8, 128, 128)) for p in range(2)]
    sk_p = [dt_out("sk%d" % p, (DEPTH, NSEQ, 128, 256)) for p in range(2)]
    sv_p = [dt_out("sv%d" % p, (DEPTH, NSEQ, 128, 256)) for p in range(2)]
    ss_p = [dt_out("ss%d" % p, (DEPTH, NSEQ, 8, 128, 128)) for p in range(2)]
    send_p = [[nc.dram_tensor("send%d_%d" % (p, l), [SEND_ROWS, 128], F32).ap() for l in range(DEPTH)] for p in range(2)]
    emd = nc.dram_tensor("emd", [16, 128, 256], BF16).ap()

    stack = contextlib.ExitStack()
    with stack:
        S = Sched(nc, stack)
        cur = [16512]
        TOP = 229344
        nid = [0]

        def alloc(shape, dtype, at=None):
            isz = 4 if dtype == F32 else 2
            n = 1
            for s in shape[1:]:
                n *= s
            size = (n * isz + 31) // 32 * 32
            if at is None:
                off = cur[0]
                cur[0] += size
                assert cur[0] <= TOP, ("SBUF overflow", cur[0] - TOP)
            else:
                off = at
            nid[0] += 1
            return nc.alloc_sbuf_tensor_at("t%d" % nid[0], list(shape), dtype, offset=off)

        hT = alloc([128, KC, T], F32)
        xn = alloc([128, KC, T], BF16)
        mix = alloc([128, KC, T], BF16)
        XB = cur[0]
        cur[0] += 34048
        wbuf = alloc([128, NS, KC, 128], BF16)
        emg = alloc([128, 4, 256], BF16)
        identB = alloc([128, 128], BF16)
        identF = alloc([128, 128], F32)
        onesB = alloc([128, 128], BF16)
        onesFl = alloc([128, 128], BF16)
        tril = alloc([128, 64], F32)
        seg01 = alloc([128, T], BF16)
        nw = alloc([128, DEPTH * KC], F32)
        fw = alloc([128, KC], F32)
        lb = alloc([128, DEPTH * 8], F32)
        oml = alloc([128, DEPTH * 8], F32)
        noml = alloc([128, DEPTH * 8], F32)
        hw = alloc([128, DEPTH * 8], F32)
        esink = alloc([128, DEPTH * 16], F32)
        flag = alloc([128, 1], F32)
        epsT = alloc([128, 1], F32)
        zcol = alloc([128, 1], F32)
        carry = alloc([128, 1], F32)
        Dsave = alloc([128, 8], F32)
        YB = cur[0]
        yf = [alloc([128, 321], F32) for _ in range(6)]
        qs_t, sg_t, la_t, gx_t, eg_t, en_t = yf
        vT_t, qt_t, kt_t, kh_t = [alloc([128, 320], BF16) for _ in range(4)]
        khT = alloc([128, 2, 128], BF16)
        vtk2 = alloc([128, 2, 128], BF16)
        att = alloc([128, 2, 64], BF16)
        Sst = alloc([128, 128], F32)
        Sbf = alloc([128, 128], BF16)
        S0 = alloc([128, 4, 128], F32)
        sgr = alloc([128, T], BF16)
        sfst = alloc([128, 2, 128], F32)
        YEND = cur[0]
        print("SBUF used", cur[0] - 16512, "free", TOP - cur[0])
        kv32 = alloc([128, 4, 160], F32, at=YB)
        tokp = alloc([128, 512], F32, at=YB + 2560)
        toks = alloc([128, 512], F32, at=YB + 2560 + 2048)
        assert YB + 2560 + 4096 <= YEND
        rt = alloc([128, T], F32, at=YB)
        den = alloc([128, 4, 128], F32, at=YB)
        onr = alloc([128, 4, 128], F32, at=YB + 2048)
        ynt = alloc([128, 2, 128], F32, at=YB + 4288)
        oloc = alloc([128, 8, T], BF16, at=XB)
        Qd = alloc([128, 8, T], BF16, at=XB + 17024)
        sqtmp = alloc([128, 2, T], BF16, at=XB)
        xst = alloc([128, 2, 2048], F32, at=XB)
        emst_f = alloc([128, 2, 256], F32, at=XB + 16384)
        emst_t = alloc([128, 256], F32, at=XB + 16384 + 2048)
        emst_b = alloc([128, 2, 256], BF16, at=XB + 16384 + 3072)
        mask01 = alloc([128, 256], F32, at=XB + 16384 + 4096)
        lbtmp = alloc([128, DEPTH * 8], F32, at=XB + 16384 + 5120)
        lbsum = alloc([128, 8], F32, at=XB + 16384 + 5120 + 128)
        xo = [XB]

        def xalloc(shape, dtype):
            t_ = alloc(shape, dtype, at=xo[0])
            n = 1
            for s in shape[1:]:
                n *= s
            xo[0] += (n * (4 if dtype == F32 else 2) + 31) // 32 * 32
            assert xo[0] <= XB + 34048
            return t_

        qa = xalloc([128, 2, T], BF16)
        kd = xalloc([128, 128 + T], BF16)
        vdT = xalloc([128, T], BF16)
        sga = xalloc([128, 2, T], BF16)
        vtk = xalloc([128, 10, 128], BF16)
        kds = xalloc([128, NSEQ, 128], BF16)
        vtks = xalloc([128, NSEQ, 128], BF16)
        vtkn = xalloc([128, NSEQ, 128], BF16)
        ckst = xalloc([128, 2, 128], BF16)
        Eb2 = [xalloc([128, 4, 2, 128], BF16) for _ in range(2)]
        Pb2 = [xalloc([128, 4, 2, 128], BF16) for _ in range(2)]

        PS = []
        for b in range(8):
            if b == 4:
                PS.append(stack.enter_context(nc.psum_tensor("psb%d" % b, [128, 1024], BF16)))
            else:
                PS.append(stack.enter_context(nc.psum_tensor("psb%d" % b, [128, 512], F32)))
        pbrot = [0]

        def next_pb():
            b = pbrot[0] % 3
            pbrot[0] += 1
            return b

        V = nc.vector
        A = nc.scalar
        PE = nc.tensor

        bscr = alloc([128, 4], F32)

        def barrier(extra_writes=()):
            if os.environ.get('KNOBAR', '0') == '1':
                return
            S.add('act', lambda: A.copy(out=bscr[:, 0:1], in_=zcol[:, 0:1]), reads=[('zc',)], writes=[('bar', 'act')] + list(extra_writes))
            S.add('dve', lambda: V.tensor_copy(out=bscr[:, 1:2], in_=zcol[:, 0:1]), reads=[('zc',)], writes=[('bar', 'dve')])
            S.add('pe', lambda: PE.matmul(PS[7][0:1, 0:1], lhsT=identB[:, 0:1], rhs=identB[:, 0:1], start=True, stop=True),
                  reads=[('identB',)], writes=[('bar', 'pe'), ('ps', 7)])
            allb = [('bar', 'act'), ('bar', 'dve'), ('bar', 'pe')]
            S.add('act', lambda: A.copy(out=bscr[:, 2:3], in_=zcol[:, 0:1]), reads=allb + [('zc',)], writes=[('barj', 'act')])
            S.add('dve', lambda: V.tensor_copy(out=bscr[:, 3:4], in_=zcol[:, 0:1]), reads=allb + [('zc',)], writes=[('barj', 'dve')])
            S.add('pe', lambda: PE.matmul(PS[7][0:1, 0:1], lhsT=identB[:, 0:1], rhs=identB[:, 0:1], start=True, stop=True),
                  reads=allb + [('identB',)], writes=[('barj', 'pe'), ('ps', 7)])

        def ld(dst, src, key, q='sp'):
            eng = nc.sync if q == 'sp' else nc.gpsimd
            S.add(q, lambda: eng.dma_start(out=dst, in_=src), writes=[key], dma=('c', key))

        ld(identF[:, :], identf_in[:, :], ('identF',))
        ld(identB[:, :], identf_in[:, :], ('identB',), q='pool')
        ld(tril[:, :], tril_in[:, :], ('tril',))
        ld(seg01[:, :], seg_in[:, :], ('seg01',), q='pool')
        ld(nw[:, :], nw_in[:, :], ('nw',))
        ld(fw[:, :], fw_in[:, :], ('fw',))
        ld(hw[:, :], hw_in[:, :], ('hw',))
        ld(esink[:, :], sink_in[:, :], ('esink',))
        ld(lbtmp[:, :], lbl_in[:, :], ('lbtmp',))
        ld(mask01[:, :], mask_in[:, :], ('mask01',))
        S.add('dve', lambda: V.memset(zcol[:, :], 0.0), writes=[('zc',)])
        S.add('dve', lambda: V.memset(epsT[:, :], EPS), writes=[('eps',)])
        S.add('dve', lambda: V.memset(onesB[:, :], 1.0), writes=[('onesB',)])
        S.add('act', lambda: A.activation(out=esink[:, :], in_=esink[:, :], func=AF.Exp), reads=[('esink',)], writes=[('esink',)])
        S.add('act', lambda: A.activation(out=lbtmp[:, :], in_=lbtmp[:, :], func=AF.Exp), reads=[('lbtmp',)], writes=[('lbtmp',)])

        lb_steps = []

        def lb_chain():
            steps = [
                (lambda: V.tensor_tensor(out=lbsum[:, :], in0=lbtmp[:, 0:8], in1=lbtmp[:, 8:16], op=ALU.add)),
                (lambda: V.tensor_tensor(out=lbsum[:, :], in0=lbsum[:, :], in1=lbtmp[:, 16:24], op=ALU.add)),
                (lambda: V.tensor_tensor(out=lbsum[:, :], in0=lbsum[:, :], in1=lbtmp[:, 24:32], op=ALU.add)),
                (lambda: V.reciprocal(out=lbsum[:, :], in_=lbsum[:, :])),
                (lambda: V.memset(lb[:, 0:8], 0.0)),
                (lambda: V.tensor_tensor(out=lb[:, 8:16], in0=lbtmp[:, 8:16], in1=lbsum[:, :], op=ALU.mult)),
                (lambda: V.tensor_tensor(out=lb[:, 16:24], in0=lbtmp[:, 16:24], in1=lbsum[:, :], op=ALU.mult)),
                (lambda: V.tensor_tensor(out=lb[:, 24:32], in0=lbtmp[:, 24:32], in1=lbsum[:, :], op=ALU.mult)),
                (lambda: V.tensor_tensor(out=lb[:, 16:24], in0=lb[:, 16:24], in1=lb[:, 8:16], op=ALU.add)),
                (lambda: V.tensor_tensor(out=lb[:, 24:32], in0=lb[:, 24:32], in1=lb[:, 16:24], op=ALU.add)),
                (lambda: V.tensor_scalar(out=noml[:, :], in0=lb[:, :], scalar1=-1.0, scalar2=None, op0=ALU.add)),
                (lambda: V.tensor_scalar(out=oml[:, :], in0=noml[:, :], scalar1=-1.0, scalar2=None, op0=ALU.mult)),
            ]
            for f in steps:
                S.add('dve', f, reads=[('lbtmp',), ('lbc',)], writes=[('lbc',)])

        lb_chain()

        for h in range(16 if os.environ.get('KNOEM', '0') != '1' else 0):
            sl = h % 2
            S.add('sp', lambda h=h, sl=sl: nc.sync.dma_start(out=emst_f[:, sl, :], in_=biasT_in[h, :, :]),
                  writes=[('emf', sl)], dma=('emf', sl))
            S.add('act', lambda sl=sl: A.activation(out=emst_t[:, :], in_=emst_f[:, sl, :], func=AF.Exp),
                  reads=[('emf', sl)], writes=[('emt',)])
            S.add('dve', lambda sl=sl: V.tensor_tensor(out=emst_b[:, sl, :], in0=emst_t[:, :], in1=mask01[:, :], op=ALU.mult),
                  reads=[('emt',), ('mask01',)], writes=[('emb', sl)])
            S.add('sp', lambda h=h, sl=sl: nc.sync.dma_start(out=emd[h, :, :], in_=emst_b[:, sl, :]),
                  reads=[('emb', sl)], writes=[('emd',)], dma=('emd',))
        barrier(extra_writes=[('emb', 0), ('emb', 1), ('emf', 0), ('emf', 1), ('mask01',), ('lbtmp',)])

        def hkeys(k, t0=0, n=T):
            return [('h', k, i) for i, (a, m) in enumerate(TCH) if a < t0 + n and t0 < a + m]

        TT = [(i * 128, 128) for i in range(8)] + [(1024, 40)]
        units = []
        for l in range(nl):
            for u in range(4):
                units.append(('in', l, C_KA + 128 * u))
            for h in range(8):
                units.append(('in', l, C_QR + 128 * h))
                units.append(('in', l, C_FR + 128 * h))
                units.append(('in', l, C_IR + 128 * h))
            for h in range(8):
                units.append(('in', l, C_GR + 128 * h))
            for g in range(4):
                units.append(('in', l, C_QA + 256 * g))
                units.append(('in', l, C_QA + 256 * g + 128))
                units.append(('dup', l, C_KA + 64 * g))
                units.append(('dup', l, C_VA + 64 * g))
                units.append(('in', l, C_GA + 256 * g))
                units.append(('in', l, C_GA + 256 * g + 128))
            for n_ in range(16):
                units.append(('out', l, 128 * n_))
        units = units + units
        wst = dict(next=0, issued=0)

        def w_issue(i):
            kind, l, c0 = units[i]
            slot = i % NS
            if kind == 'in':
                src = w_in[l].rearrange("(k p) c -> p k c", p=128)[:, :, c0:c0 + 128]
                S.add('pool', lambda slot=slot, src=src: nc.gpsimd.dma_start(out=wbuf[:, slot, :, :], in_=src),
                      writes=[('w', slot)], dma=('w', slot))
            elif kind == 'out':
                src = w_out[l].rearrange("(k p) c -> p k c", p=128)[:, :, c0:c0 + 128]
                S.add('pool', lambda slot=slot, src=src: nc.gpsimd.dma_start(out=wbuf[:, slot, :, :], in_=src),
                      writes=[('w', slot)], dma=('w', slot))
            else:
                src = w_in[l].rearrange("(k p) c -> p k c", p=128)[:, :, c0:c0 + 64]
                S.add('pool', lambda slot=slot, src=src: nc.gpsimd.dma_start(out=wbuf[:, slot, :, 0:64], in_=src),
                      writes=[('w', slot)], dma=('w', slot))
                S.add('pool', lambda slot=slot, src=src: nc.gpsimd.dma_start(out=wbuf[:, slot, :, 64:128], in_=src),
                      writes=[('w', slot)], dma=('w', slot))

        def w_release():
            done = wst['next']
            while wst['issued'] < min(done + NS, len(units)):
                w_issue(wst['issued'])
                wst['issued'] += 1

        def w_begin(k=1):
            w_release()
            first = wst['next']
            wst['next'] += k
            assert wst['issued'] >= wst['next'], (wst, k)
            return [(first + j) % NS for j in range(k)]

        def w_next():
            return w_begin(1)[0]

        XN_ALL = [('xn', k) for k in range(KC)]
        BARK = [('barj', 'act'), ('barj', 'dve'), ('barj', 'pe')]
        RECV_KEYS = [('send', x_) for x_ in list(range(8)) + ['k', 'v']]

        def proj(slot, t0, n, b):
            def f():
                for k in range(KC):
                    last = PE.matmul(PS[b][:, 0:n], lhsT=wbuf[:, slot, k, :], rhs=xn[:, k, t0:t0 + n],
                                     start=(k == 0), stop=(k == KC - 1))
                return last
            S.add('pe', f, reads=[('w', slot)] + XN_ALL, writes=[('ps', b)])

        def do_norm(wt, wcol0):
            banks = [0, 1, 2, 5]
            for k in range(KC):
                sl = k % 2
                S.add('act', lambda k=k, sl=sl: A.activation(out=sqtmp[:, sl, :], in_=hT[:, k, :], func=AF.Square),
                      reads=hkeys(k), writes=[('sq', sl)])
                for i, (t0, n) in enumerate(TCH):
                    S.add('pe', lambda k=k, sl=sl, i=i, t0=t0, n=n: PE.matmul(PS[banks[i]][:, 0:n], lhsT=onesB[:, :], rhs=sqtmp[:, sl, t0:t0 + n],
                                                                             start=(k == 0), stop=(k == KC - 1)),
                          reads=[('sq', sl), ('onesB',)], writes=[('ps', banks[i])])
            for i, (t0, n) in enumerate(TCH):
                S.add('act', lambda i=i, t0=t0, n=n: A.activation(out=rt[:, t0:t0 + n], in_=PS[banks[i]][:, 0:n], func=AF.Ln,
                                                                  bias=epsT[:, 0:1], scale=1.0 / D),
                      reads=[('ps', banks[i]), ('eps',)], writes=[('rt', i)])
                S.add('act', lambda t0=t0, n=n: A.activation(out=rt[:, t0:t0 + n], in_=rt[:, t0:t0 + n], func=AF.Exp, scale=-0.5),
                      reads=[('rt', i)], writes=[('rt', i)])
            return [('rt', i) for i in range(4)]

        def tc_of(t0):
            return [i for i, (a, m) in enumerate(TCH) if a == t0][0]

        def run_pass(p, xin, ck_in, cv_in, st_in, y_out, pk_out, pv_out, ps_out, sk_out, sv_out, ss_out, send, recv):
            barrier(extra_writes=[('xst', 0), ('xst', 1), ('flag',), ('onesFl',)])
            S.add('dve', lambda: V.memset(flag[:, :], float(p)), writes=[('flag',)])
            S.add('dve', lambda: V.tensor_scalar(out=onesFl[:, :], in0=onesB[:, :], scalar1=flag[:, 0:1], scalar2=None, op0=ALU.mult),
                  reads=[('onesB',), ('flag',)], writes=[('onesFl',)])
            if _on('p0'):
                for ti, (t0, n) in enumerate(TT):
                    sl = ti % 2
                    S.add('sp', lambda sl=sl, t0=t0, n=n: nc.sync.dma_start(out=xst[0:n, sl, :], in_=xin[t0:t0 + n, :]),
                          reads=BARK, writes=[('xst', sl)], dma=('xst', sl))
                    for kq in range(4):
                        b = 5 + (ti * 4 + kq) % 3

                        def f(sl=sl, n=n, kq=kq, b=b):
                            for j in range(4):
                                k = kq * 4 + j
                                last = PE.transpose(out=PS[b][:, j * 128:j * 128 + n], in_=xst[0:n, sl, k * 128:(k + 1) * 128],
                                                    identity=identF[0:n, 0:n])
                            return last
                        S.add('pe', f, reads=[('xst', sl), ('identF',)], writes=[('ps', b)])
                        hk = []
                        for j in range(4):
                            hk += hkeys(kq * 4 + j, t0, n)
                        cp = (lambda kq=kq, t0=t0, n=n, b=b: A.copy(out=hT[:, kq * 4:kq * 4 + 4, t0:t0 + n],
                                                                   in_=PS[b][:, :].rearrange("p (j c) -> p j c", j=4)[:, :, 0:n]))
                        cpv = (lambda kq=kq, t0=t0, n=n, b=b: V.tensor_copy(out=hT[:, kq * 4:kq * 4 + 4, t0:t0 + n],
                                                                             in_=PS[b][:, :].rearrange("p (j c) -> p j c", j=4)[:, :, 0:n]))
                        if kq % 2 == 0:
                            S.add('act', cp, reads=[('ps', b)], writes=hk)
                        else:
                            S.add('dve', cpv, reads=[('ps', b)], writes=hk)
            barrier(extra_writes=[('xst', 0), ('xst', 1)])

            out_slots = []
            for l in range(nl):
                if not _on('norm'):
                    break
                rtk = do_norm(nw, l * KC)
                for k in range(KC):
                    S.add('dve', lambda k=k, l=l: V.scalar_tensor_tensor(out=xn[:, k, :], in0=hT[:, k, :], scalar=nw[:, l * KC + k:l * KC + k + 1],
                                                                         in1=rt[:, :], op0=ALU.mult, op1=ALU.mult),
                          reads=hkeys(k) + rtk + [('nw',)], writes=[('xn', k)])
                barrier(extra_writes=rtk + [('sq', 0), ('sq', 1)])

                if not _on('p15'):
                    break
                T15 = 904
                for u in range(4):
                    slot = w_next()
                    b = next_pb()
                    proj(slot, T15, 160, b)
                    S.add('act', lambda u=u, b=b: A.copy(out=kv32[:, u, :], in_=PS[b][:, 0:160]), reads=[('ps', b)], writes=[('kv32', u)])
                S.add('sp', lambda l=l: nc.sync.dma_start(out=send[l][1024:1280, :].rearrange("(c p) t -> p c t", p=128), in_=kv32[:, 0:2, 0:128]),
                      reads=[('kv32', 0), ('kv32', 1)], writes=[('send', 'k')], dma=('snd',))

                def f15():
                    for u in range(4):
                        PE.transpose(out=PS[7][:, u * 128:(u + 1) * 128], in_=kv32[:, u, 0:128], identity=identF[:, :])
                    for u in range(4):
                        last = PE.transpose(out=PS[6][0:32, u * 128:(u + 1) * 128], in_=kv32[:, u, 128:160], identity=identF[:, :])
                    return last
                S.add('pe', f15, reads=[('kv32', u) for u in range(4)] + [('identF',)], writes=[('ps', 7), ('ps', 6)])
                S.add('act', lambda: A.copy(out=tokp[:, :], in_=PS[7][:, :]), reads=[('ps', 7)], writes=[('tokp',)])
                S.add('dve', lambda: V.tensor_copy(out=toks[0:32, :], in_=PS[6][0:32, :]), reads=[('ps', 6)], writes=[('toks',)])
                S.add('sp', lambda l=l: nc.sync.dma_start(out=pk_out[l, :, :], in_=tokp[:, 0:256]), reads=[('tokp',)], dma=('o_pk',))
                S.add('sp', lambda l=l: nc.sync.dma_start(out=pv_out[l, :, :], in_=tokp[:, 256:512]), reads=[('tokp',)], dma=('o_pk',))
                S.add('sp', lambda l=l: nc.sync.dma_start(out=send[l][1280:1536, :].rearrange("(c s) f -> s c f", s=128),
                                                          in_=tokp[:, 256:512].rearrange("s (c f) -> s c f", c=2)),
                      reads=[('tokp',)], writes=[('send', 'v')], dma=('snd',))
                for s in range(NSEQ):
                    S.add('sp', lambda l=l, s=s: nc.sync.dma_start(out=sk_out[l, s, 120:128, :], in_=toks[8 * s:8 * s + 8, 0:256]),
                          reads=[('toks',)], dma=('o_sk',))
                    S.add('sp', lambda l=l, s=s: nc.sync.dma_start(out=sv_out[l, s, 120:128, :], in_=toks[8 * s:8 * s + 8, 256:512]),
                          reads=[('toks',)], dma=('o_sk',))
                S.add('sp', lambda l=l: nc.sync.dma_start(out=sk_out[l, :, 0:120, :], in_=ck_in[l, :, 8:128, :]), dma=('o_sk',))
                S.add('sp', lambda l=l: nc.sync.dma_start(out=sv_out[l, :, 0:120, :], in_=cv_in[l, :, 8:128, :]), dma=('o_sk',))
                barrier(extra_writes=[('kv32', u) for u in range(4)] + [('tokp',), ('toks',)])

                if not _on('p1'):
                    break
                for h in range(8):
                    lh = l * 8 + h
                    s_q, s_f, s_i = None, None, None
                    S.add('sp', lambda l=l, h=h: nc.sync.dma_start(out=S0[:, :, :], in_=st_in[l, :, h, :, :].rearrange("s k v -> k s v")),
                          writes=[('S0',)], dma=('S0',))
                    S.add('dve', lambda: V.memset(carry[:, :], 0.0), writes=[('carry',)])
                    slots = w_begin(3)
                    def tc_gen(i, t0, n, h=h, lh=lh, l=l, slots=slots):
                        proj(slots[0], t0, n, 0)
                        proj(slots[1], t0, n, 1)
                        proj(slots[2], t0, n, 2)
                        if i == 3:
                            w_release()
                        yield 'proj'
                        S.add('act', lambda n=n: A.activation(out=qs_t[:, 0:n], in_=PS[0][:, 0:n], func=AF.Silu), reads=[('ps', 0)], writes=[('qs',)])
                        S.add('act', lambda n=n: A.activation(out=sg_t[:, 0:n], in_=PS[1][:, 0:n], func=AF.Sigmoid), reads=[('ps', 1)], writes=[('sg',)])
                        S.add('act', lambda n=n: A.copy(out=vT_t[:, 0:n], in_=PS[2][:, 0:n]), reads=[('ps', 2)], writes=[('vT',)])
                        S.add('act', lambda n=n, lh=lh: A.activation(out=la_t[:, 0:n], in_=sg_t[:, 0:n], func=AF.Ln,
                                                                     bias=lb[:, lh:lh + 1], scale=oml[:, lh:lh + 1]),
                              reads=[('sg',), ('lbc',)], writes=[('la',)])
                        S.add('dve', lambda n=n, lh=lh: V.tensor_scalar(out=sg_t[:, 0:n], in0=sg_t[:, 0:n], scalar1=noml[:, lh:lh + 1],
                                                                        scalar2=oml[:, lh:lh + 1], op0=ALU.mult, op1=ALU.add),
                              reads=[('sg',), ('la',), ('lbc',)], writes=[('sg',)])
                        S.add('dve', lambda: V.tensor_copy(out=gx_t[:, 0:1], in_=carry[:, 0:1]), reads=[('carry',)], writes=[('gx',)])
                        S.add('dve', lambda t0=t0, n=n: V.tensor_tensor_scan(out=gx_t[:, 1:1 + n], data0=seg01[:, t0:t0 + n], data1=la_t[:, 0:n],
                                                                             initial=carry[:, 0:1], op0=ALU.mult, op1=ALU.add),
                              reads=[('la',), ('carry',), ('seg01',), ('gx',)], writes=[('gx',)])
                        S.add('dve', lambda n=n: V.tensor_copy(out=carry[:, 0:1], in_=gx_t[:, n:n + 1]), reads=[('gx',)], writes=[('carry',)])
                        S.add('act', lambda n=n: A.activation(out=eg_t[:, 0:n], in_=gx_t[:, 1:1 + n], func=AF.Exp), reads=[('gx',)], writes=[('eg',)])
                        S.add('dve', lambda h=h, t0=t0, n=n: V.tensor_tensor(out=Qd[:, h, t0:t0 + n], in0=qs_t[:, 0:n], in1=eg_t[:, 0:n], op=ALU.mult),
                              reads=[('qs',), ('eg',)], writes=[('Qd', h)])
                        if i < 3:
                            chunks = [(64 * j, 64, 'p', j == 0 and i == 0, False, -1) for j in range(5)]
                        else:
                            chunks = [(0, 64, 'p', False, False, -1), (64, 8, 'p', False, True, -1)] + \
                                     [(72 + 8 * s, 8, 's', True, True, s) for s in range(NSEQ)]
                        for (o, C, kind, segstart, segend, sq) in chunks:
                            ref = zcol[:, 0:1] if kind == 's' else gx_t[:, o:o + 1]
                            S.add('dve', lambda o=o, C=C, ref=ref: V.tensor_scalar(out=la_t[:, o:o + C], in0=gx_t[:, 1 + o:1 + o + C], scalar1=ref,
                                                                                   scalar2=None, op0=ALU.subtract),
                                  reads=[('gx',), ('zc',), ('la',)], writes=[('la',)])
                        S.add('act', lambda n=n: A.activation(out=en_t[:, 0:n], in_=la_t[:, 0:n], func=AF.Exp, scale=-1.0), reads=[('la',)], writes=[('en',)])
                        S.add('act', lambda n=n: A.activation(out=la_t[:, 0:n], in_=la_t[:, 0:n], func=AF.Exp), reads=[('la',), ('en',)], writes=[('la',)])
                        S.add('dve', lambda n=n: V.tensor_tensor(out=qt_t[:, 0:n], in0=qs_t[:, 0:n], in1=la_t[:, 0:n], op=ALU.mult),
                              reads=[('qs',), ('la',)], writes=[('qt',)])
                        S.add('dve', lambda n=n: V.tensor_tensor(out=kt_t[:, 0:n], in0=sg_t[:, 0:n], in1=en_t[:, 0:n], op=ALU.mult),
                              reads=[('sg',), ('en',)], writes=[('kt',)])
                        for (o, C, kind, segstart, segend, sq) in chunks:
                            S.add('dve', lambda o=o, C=C: V.tensor_scalar(out=kh_t[:, o:o + C], in0=kt_t[:, o:o + C], scalar1=la_t[:, o + C - 1:o + C],
                                                                          scalar2=None, op0=ALU.mult),
                                  reads=[('kt',), ('la',)], writes=[('kh',)])
                        yield 'chain'
                        def stageA(ci, o, C, kind, segstart, segend, sq):
                            par = ci % 2

                            def ft(o=o, C=C):
                                PE.transpose(out=PS[4][0:C, 0:128], in_=kh_t[:, o:o + C], identity=identB[:, :])
                                return PE.transpose(out=PS[4][0:C, 128:256], in_=vT_t[:, o:o + C], identity=identB[:, :])
                            S.add('pe', ft, reads=[('kh',), ('vT',), ('identB',)], writes=[('ps', 4)])
                            S.add('act', lambda C=C, par=par: A.copy(out=khT[0:C, par, :], in_=PS[4][0:C, 0:128]),
                                  reads=[('ps', 4)], writes=[('khT', par)])
                            S.add('dve', lambda C=C, par=par: V.tensor_copy(out=vtk2[0:C, par, :], in_=PS[4][0:C, 128:256]),
                                  reads=[('ps', 4)], writes=[('vtk2', par)])
                            S.add('pe', lambda o=o, C=C: PE.matmul(PS[3][0:C, 0:C], lhsT=kt_t[:, o:o + C], rhs=qt_t[:, o:o + C], start=True, stop=True),
                                  reads=[('kt',), ('qt',)], writes=[('ps', 3)])
                            S.add('dve', lambda C=C, par=par: V.tensor_tensor(out=att[0:C, par, 0:C], in0=PS[3][0:C, 0:C], in1=tril[0:C, 0:C], op=ALU.mult),
                                  reads=[('ps', 3), ('tril',)], writes=[('att', par)])

                        def stageB(ci, o, C, kind, segstart, segend, sq, i=i, t0=t0, h=h, l=l):
                            par = ci % 2
                            sb = 6 + par
                            S.add('pe', lambda C=C, par=par, sb=sb: PE.matmul(PS[sb][:, 0:128], lhsT=khT[0:C, par, :], rhs=vtk2[0:C, par, :], start=True, stop=True),
                                  reads=[('khT', par), ('vtk2', par)], writes=[('ps', sb)])
                            if kind == 's':
                                S.add('act', lambda sq=sq: A.copy(out=Sbf[:, :], in_=S0[:, sq, :]), reads=[('S0',)], writes=[('Sbf',)])
                            use_s = (kind == 's') or (not segstart)

                            def fo(o=o, C=C, par=par, use_s=use_s):
                                if use_s:
                                    PE.matmul(PS[5][:, 0:C], lhsT=Sbf[:, :], rhs=qt_t[:, o:o + C], start=True, stop=False)
                                return PE.matmul(PS[5][:, 0:C], lhsT=vtk2[0:C, par, :], rhs=att[0:C, par, 0:C], start=(not use_s), stop=True)
                            S.add('pe', fo, reads=[('Sbf',), ('qt',), ('vtk2', par), ('att', par)], writes=[('ps', 5)])
                            S.add('act', lambda o=o, C=C: A.copy(out=oloc[:, h, t0 + o:t0 + o + C], in_=PS[5][:, 0:C]),
                                  reads=[('ps', 5)], writes=[('oloc', h)])
                            elast = la_t[:, o + C - 1:o + C]
                            if kind == 's':
                                S.add('dve', lambda sq=sq, sb=sb, par=par, elast=elast: V.scalar_tensor_tensor(
                                    out=sfst[:, par, :], in0=S0[:, sq, :], scalar=elast, in1=PS[sb][:, 0:128], op0=ALU.mult, op1=ALU.add),
                                    reads=[('S0',), ('la',), ('ps', sb)], writes=[('sfst', par)])
                                S.add('sp', lambda sq=sq, par=par: nc.sync.dma_start(out=ss_out[l, sq, h, :, :], in_=sfst[:, par, :]),
                                      reads=[('sfst', par)], dma=('o_ss', par))
                            else:
                                if segstart:
                                    S.add('dve', lambda sb=sb: V.tensor_copy(out=Sst[:, :], in_=PS[sb][:, 0:128]),
                                          reads=[('ps', sb)], writes=[('Sst',)])
                                else:
                                    S.add('dve', lambda sb=sb, elast=elast: V.scalar_tensor_tensor(
                                        out=Sst[:, :], in0=Sst[:, :], scalar=elast, in1=PS[sb][:, 0:128], op0=ALU.mult, op1=ALU.add),
                                        reads=[('Sst',), ('la',), ('ps', sb)], writes=[('Sst',)])
                                if segend:
                                    S.add('sp', lambda: nc.sync.dma_start(out=send[l][h * 128:(h + 1) * 128, :], in_=Sst[:, :]),
                                          reads=[('Sst',)], writes=[('send', h)], dma=('snd',))
                                    S.add('dve', lambda o=o, C=C: V.tensor_copy(out=Dsave[:, h:h + 1], in_=eg_t[:, o + C - 1:o + C]),
                                          reads=[('eg',)], writes=[('Dsave', h)])
                                else:
                                    S.add('act', lambda: A.copy(out=Sbf[:, :], in_=Sst[:, :]), reads=[('Sst',)], writes=[('Sbf',)])

                        stageA(0, *chunks[0])
                        for ci in range(len(chunks)):
                            if ci + 1 < len(chunks):
                                stageA(ci + 1, *chunks[ci + 1])
                            stageB(ci, *chunks[ci])
                        yield 'chunks'

                    gens = [tc_gen(i_, t0_, n_) for i_, (t0_, n_) in enumerate(TCH)]
                    next(gens[0])
                    for i_ in range(len(TCH)):
                        next(gens[i_])
                        if i_ + 1 < len(TCH):
                            next(gens[i_ + 1])
                        next(gens[i_])

                if not _on('ex'):
                    break
                for h in range(8):
                    lh = l * 8 + h
                    slot = w_next()
                    S.add('sp', lambda l=l, h=h: nc.sync.dma_start(out=S0[:, 0, :], in_=recv[l][h * 128:(h + 1) * 128, :]),
                          reads=RECV_KEYS, writes=[('S0',)], dma=('S0',))
                    S.add('sp', lambda l=l, h=h: nc.sync.dma_start(out=S0[:, 1, :], in_=send[l][h * 128:(h + 1) * 128, :]),
                          reads=[('send', h)], writes=[('S0',)], dma=('S0',))
                    S.add('dve', lambda: V.tensor_scalar(out=S0[:, 0, :], in0=S0[:, 0, :], scalar1=flag[:, 0:1], scalar2=None, op0=ALU.mult),
                          reads=[('S0',), ('flag',)], writes=[('S0',)])
                    S.add('act', lambda: A.copy(out=Sbf[:, :], in_=S0[:, 0, :]), reads=[('S0',)], writes=[('Sbf',)])
                    par = h % 2
                    S.add('dve', lambda h=h, par=par: V.scalar_tensor_tensor(out=sfst[:, par, :], in0=S0[:, 0, :], scalar=Dsave[:, h:h + 1], in1=S0[:, 1, :],
                                                                             op0=ALU.mult, op1=ALU.add),
                          reads=[('S0',), ('Dsave', h)], writes=[('sfst', par)])
                    S.add('sp', lambda l=l, h=h, par=par: nc.sync.dma_start(out=ps_out[l, h, :, :], in_=sfst[:, par, :]),
                          reads=[('sfst', par)], dma=('o_ss', par))
                    for i, (t0, n) in enumerate(TCH):
                        b = next_pb()
                        proj(slot, t0, n, b)
                        S.add('act', lambda t0=t0, n=n, b=b: A.activation(out=sgr[:, t0:t0 + n], in_=PS[b][:, 0:n], func=AF.Silu),
                              reads=[('ps', b)], writes=[('sgr', i)])
                    for i, (t0, n) in enumerate(TCH):
                        npr = min(n, TP - t0)
                        S.add('pe', lambda h=h, t0=t0, npr=npr: PE.matmul(PS[5][:, 0:npr], lhsT=Sbf[:, :], rhs=Qd[:, h, t0:t0 + npr], start=True, stop=True),
                              reads=[('Sbf',), ('Qd', h)], writes=[('ps', 5)])
                        S.add('dve', lambda h=h, t0=t0, npr=npr: V.tensor_tensor(out=qs_t[:, 0:npr], in0=PS[5][:, 0:npr], in1=oloc[:, h, t0:t0 + npr], op=ALU.add),
                              reads=[('ps', 5), ('oloc', h)], writes=[('of',)])
                        if npr < n:
                            S.add('dve', lambda h=h, t0=t0, n=n, npr=npr: V.tensor_copy(out=qs_t[:, npr:n], in_=oloc[:, h, t0 + npr:t0 + n]),
                                  reads=[('oloc', h), ('of',)], writes=[('of',)])
                        S.add('act', lambda n=n: A.activation(out=qt_t[:, 0:n], in_=qs_t[:, 0:n], func=AF.Square), reads=[('of',)], writes=[('osq',)])
                        S.add('pe', lambda n=n: PE.matmul(PS[6][:, 0:n], lhsT=onesB[:, :], rhs=qt_t[:, 0:n], start=True, stop=True),
                              reads=[('osq',), ('onesB',)], writes=[('ps', 6)])
                        S.add('act', lambda n=n: A.activation(out=sg_t[:, 0:n], in_=PS[6][:, 0:n], func=AF.Ln, bias=epsT[:, 0:1], scale=1.0 / 128),
                              reads=[('ps', 6), ('eps',)], writes=[('rr',)])
                        S.add('act', lambda n=n: A.activation(out=sg_t[:, 0:n], in_=sg_t[:, 0:n], func=AF.Exp, scale=-0.5), reads=[('rr',)], writes=[('rr',)])
                        S.add('dve', lambda n=n: V.tensor_tensor(out=qs_t[:, 0:n], in0=qs_t[:, 0:n], in1=sg_t[:, 0:n], op=ALU.mult),
                              reads=[('of',), ('rr',)], writes=[('of',)])
                        S.add('dve', lambda h=h, lh=lh, t0=t0, n=n: V.scalar_tensor_tensor(out=mix[:, 8 + h, t0:t0 + n], in0=qs_t[:, 0:n], scalar=hw[:, lh:lh + 1],
                                                                                          in1=sgr[:, t0:t0 + n], op0=ALU.mult, op1=ALU.mult),
                              reads=[('of',), ('hw',), ('sgr', i)], writes=[('mix', 8 + h, i)])
                barrier(extra_writes=[('oloc', h) for h in range(8)] + [('Qd', h) for h in range(8)])

                if not _on('p3'):
                    break
                for g in range(4):
                    for jp in range(4):
                        hsrc = 4 * g + 2 * (jp % 2) + jp // 2
                        S.add('sp', lambda jp=jp, hsrc=hsrc: nc.sync.dma_start(out=emg[:, jp, :], in_=emd[hsrc, :, :]),
                              reads=[('emd',)], writes=[('emg',)], dma=('emg',))
                    for hf in range(2):
                        S.add('pool', lambda l=l, g=g, hf=hf: nc.gpsimd.dma_start(out=kd[64 * hf:64 * hf + 64, 0:128], in_=recv[l][1024 + 64 * g:1024 + 64 * g + 64, :]),
                              reads=RECV_KEYS + BARK, writes=[('kd', 'halo')], dma=('halo',))
                        S.add('pool', lambda l=l, g=g, hf=hf: nc.gpsimd.dma_start(
                            out=vtk[:, 0, 64 * hf:64 * hf + 64], in_=recv[l][1280 + 128 * (g // 2):1280 + 128 * (g // 2) + 128, 64 * (g % 2):64 * (g % 2) + 64]),
                            reads=RECV_KEYS + BARK, writes=[('vtk', 0)], dma=('halo',))
                        S.add('pool', lambda l=l, g=g, hf=hf: nc.gpsimd.dma_start(
                            out=vtks[:, :, 64 * hf:64 * hf + 64], in_=cv_in[l, :, :, 64 * g:64 * g + 64].rearrange("s t f -> t s f")),
                            reads=BARK, writes=[('vtks',)], dma=('halo',))
                    S.add('dve', lambda: V.tensor_scalar(out=vtk[:, 0, :], in0=vtk[:, 0, :], scalar1=flag[:, 0:1], scalar2=None, op0=ALU.mult),
                          reads=[('vtk', 0), ('flag',)], writes=[('vtk', 0)])
                    for s in range(NSEQ):
                        sl = s % 2
                        for hf in range(2):
                            S.add('pool', lambda l=l, g=g, s=s, sl=sl, hf=hf: nc.gpsimd.dma_start(out=ckst[:, sl, 64 * hf:64 * hf + 64], in_=ck_in[l, s, :, 64 * g:64 * g + 64]),
                                  reads=BARK, writes=[('ckst', sl)], dma=('ckst', sl))
                        S.add('pe', lambda sl=sl: PE.transpose(out=PS[4][:, 256 * sl:256 * sl + 128], in_=ckst[:, sl, :], identity=identB[:, :]),
                              reads=[('ckst', sl), ('identB',)], writes=[('ps', 4)])
                        S.add('act', lambda s=s, sl=sl: A.copy(out=kds[:, s, :], in_=PS[4][:, 256 * sl:256 * sl + 128]), reads=[('ps', 4)], writes=[('kds', s)])
                    s_q0 = w_next()
                    for i, (t0, n) in enumerate(TCH):
                        b = next_pb()
                        proj(s_q0, t0, n, b)
                        S.add('act', lambda t0=t0, n=n, b=b: A.copy(out=qa[:, 0, t0:t0 + n], in_=PS[b][:, 0:n]), reads=[('ps', b)], writes=[('qa', 0, i)])
                    s_q1 = w_next()
                    for i, (t0, n) in enumerate(TCH):
                        b = next_pb()
                        proj(s_q1, t0, n, b)
                        S.add('dve', lambda t0=t0, n=n, b=b: V.tensor_copy(out=qa[:, 1, t0:t0 + n], in_=PS[b][:, 0:n]), reads=[('ps', b)], writes=[('qa', 1, i)])
                    s_k = w_next()
                    for i, (t0, n) in enumerate(TCH):
                        b = next_pb()
                        proj(s_k, t0, n, b)
                        S.add('act', lambda t0=t0, n=n, b=b: A.copy(out=kd[:, 128 + t0:128 + t0 + n], in_=PS[b][:, 0:n]), reads=[('ps', b)], writes=[('kd', i)])
                    s_v = w_next()
                    for i, (t0, n) in enumerate(TCH):
                        b = next_pb()
                        proj(s_v, t0, n, b)
                        S.add('dve', lambda t0=t0, n=n, b=b: V.tensor_copy(out=vdT[:, t0:t0 + n], in_=PS[b][:, 0:n]), reads=[('ps', b)], writes=[('vdT', i)])
                    for jj in range(2):
                        s_g = w_next()
                        for i, (t0, n) in enumerate(TCH):
                            b = next_pb()
                            proj(s_g, t0, n, b)
                            S.add('act', lambda jj=jj, t0=t0, n=n, b=b: A.activation(out=sga[:, jj, t0:t0 + n], in_=PS[b][:, 0:n], func=AF.Silu),
                                  reads=[('ps', b)], writes=[('sga', jj, i)])
                    VD_ALL = [('vdT', i) for i in range(4)]
                    KD_ALL = [('kd', i) for i in range(4)] + [('kd', 'halo')]
                    QA_ALL = [('qa', jj, i) for jj in range(2) for i in range(4)]
                    SGA_ALL = [('sga', jj, i) for jj in range(2) for i in range(4)]
                    blocks = [(128 * i, 128) for i in range(8)] + [(1024, 8)]
                    for bi, (t0, n) in enumerate(blocks):
                        sl = bi % 2
                        S.add('pe', lambda t0=t0, n=n, sl=sl: PE.transpose(out=PS[4][0:n, 256 * sl:256 * sl + 128], in_=vdT[:, t0:t0 + n], identity=identB[:, :]),
                              reads=VD_ALL + [('identB',)], writes=[('ps', 4)])
                        if bi % 2 == 0:
                            S.add('act', lambda bi=bi, n=n, sl=sl: A.copy(out=vtk[0:n, 1 + bi, :], in_=PS[4][0:n, 256 * sl:256 * sl + 128]),
                                  reads=[('ps', 4)], writes=[('vtk', 1 + bi)])
                        else:
                            S.add('dve', lambda bi=bi, n=n, sl=sl: V.tensor_copy(out=vtk[0:n, 1 + bi, :], in_=PS[4][0:n, 256 * sl:256 * sl + 128]),
                                  reads=[('ps', 4)], writes=[('vtk', 1 + bi)])
                    for s in range(NSEQ):
                        sl = s % 2
                        t0 = TP + 8 * s
                        S.add('pe', lambda t0=t0, sl=sl: PE.transpose(out=PS[4][0:8, 256 * sl:256 * sl + 128], in_=vdT[:, t0:t0 + 8], identity=identB[:, :]),
                              reads=VD_ALL + [('identB',)], writes=[('ps', 4)])
                        S.add('act', lambda s=s, sl=sl: A.copy(out=vtkn[0:8, s, :], in_=PS[4][0:8, 256 * sl:256 * sl + 128]),
                              reads=[('ps', 4)], writes=[('vtkn', s)])
                    ablocks = []
                    for bi, (t0, n) in enumerate(blocks):
                        ablocks.append(dict(t0=t0, nq=n, kp=kd[:, 128 * bi:128 * bi + 128], kc=kd[:, 128 + t0:128 + t0 + n],
                                            vp=vtk[:, bi, :], vc=vtk[0:n, 1 + bi, :], op=(onesFl if bi == 0 else onesB),
                                            rk=[('vtk', bi), ('vtk', 1 + bi), ('onesFl',), ('onesB',)]))
                    for s in range(NSEQ):
                        t0 = TP + 8 * s
                        ablocks.append(dict(t0=t0, nq=8, kp=kds[:, s, :], kc=kd[:, 128 + t0:128 + t0 + 8],
                                            vp=vtks[:, s, :], vc=vtkn[0:8, s, :], op=onesB,
                                            rk=[('kds', s), ('vtks',), ('vtkn', s), ('onesB',)]))
                    def blk_gen(ab, par, g=g, l=l):
                        t0, nq = ab['t0'], ab['nq']
                        Ebp, Pbp = Eb2[par], Pb2[par]

                        def fs(ab=ab, t0=t0, nq=nq):
                            for j in range(4):
                                jj, hf = j // 2, j % 2
                                bank = 5 + hf
                                base = jj * 256
                                PE.matmul(PS[bank][:, base:base + nq], lhsT=ab['kp'][64 * hf:64 * hf + 64, :], rhs=qa[64 * hf:64 * hf + 64, jj, t0:t0 + nq],
                                          start=True, stop=True)
                                last = PE.matmul(PS[bank][0:nq, base + 128:base + 128 + nq], lhsT=ab['kc'][64 * hf:64 * hf + 64, :],
                                                 rhs=qa[64 * hf:64 * hf + 64, jj, t0:t0 + nq], start=True, stop=True)
                            return last
                        S.add('pe', fs, reads=KD_ALL + QA_ALL + ab['rk'], writes=[('ps', 5), ('ps', 6)])
                        for jj in range(2):
                            bank = 5 + jj
                            S.add('act', lambda jj=jj, bank=bank, nq=nq: A.activation(
                                out=Ebp[:, 2 * jj:2 * jj + 2, 0, 0:nq], in_=PS[bank][:, :].rearrange("p (h c t) -> p h c t", h=2, c=2)[:, :, 0, 0:nq],
                                func=AF.Exp, scale=0.125), reads=[('ps', bank)], writes=[('E', jj, 0, par)])
                            S.add('act', lambda jj=jj, bank=bank, nq=nq: A.activation(
                                out=Ebp[0:nq, 2 * jj:2 * jj + 2, 1, 0:nq], in_=PS[bank][:, :].rearrange("p (h c t) -> p h c t", h=2, c=2)[0:nq, :, 1, 0:nq],
                                func=AF.Exp, scale=0.125), reads=[('ps', bank)], writes=[('E', jj, 1, par)])
                        yield 'a'
                        emv = emg[:, :, :].rearrange("p h (c t) -> p h c t", c=2)
                        S.add('dve', lambda nq=nq, emv=emv: V.tensor_tensor(out=Pbp[:, :, 0, 0:nq], in0=Ebp[:, :, 0, 0:nq], in1=emv[:, :, 0, 0:nq], op=ALU.mult),
                              reads=[('E', 0, 0, par), ('E', 1, 0, par), ('emg',)], writes=[('P', 0, par)])
                        S.add('dve', lambda nq=nq, emv=emv: V.tensor_tensor(out=Pbp[0:nq, :, 1, 0:nq], in0=Ebp[0:nq, :, 1, 0:nq], in1=emv[0:nq, :, 1, 0:nq], op=ALU.mult),
                              reads=[('E', 0, 1, par), ('E', 1, 1, par), ('emg',)], writes=[('P', 1, par)])

                        yield 'b'
                        def fpv(ab=ab, nq=nq):
                            for j in range(4):
                                PE.matmul(PS[3][:, 128 * j:128 * j + nq], lhsT=ab['vp'], rhs=Pbp[:, j, 0, 0:nq], start=True, stop=False)
                                PE.matmul(PS[3][:, 128 * j:128 * j + nq], lhsT=ab['vc'], rhs=Pbp[0:nq, j, 1, 0:nq], start=False, stop=True)
                            for j in range(4):
                                PE.matmul(PS[7][:, 128 * j:128 * j + nq], lhsT=ab['op'][:, :], rhs=Pbp[:, j, 0, 0:nq], start=True, stop=False)
                                last = PE.matmul(PS[7][:, 128 * j:128 * j + nq], lhsT=onesB[0:nq, :], rhs=Pbp[0:nq, j, 1, 0:nq], start=False, stop=True)
                            return last
                        S.add('pe', fpv, reads=[('P', 0, par), ('P', 1, par)] + ab['rk'], writes=[('ps', 3), ('ps', 7)])
                        es = esink[:, l * 16 + 4 * g:l * 16 + 4 * g + 4]
                        S.add('dve', lambda nq=nq, es=es: V.tensor_tensor(out=den[:, :, 0:nq], in0=PS[7][:, :].rearrange("p (h t) -> p h t", h=4)[:, :, 0:nq],
                                                                          in1=es.rearrange("p (h o) -> p h o", o=1).to_broadcast([128, 4, nq]),
                                                                          op=ALU.add),
                              reads=[('ps', 7), ('esink',)], writes=[('den',)])
                        S.add('act', lambda nq=nq: A.activation(out=den[:, :, 0:nq], in_=den[:, :, 0:nq], func=AF.Ln), reads=[('den',)], writes=[('den',)])
                        S.add('act', lambda nq=nq: A.activation(out=den[:, :, 0:nq], in_=den[:, :, 0:nq], func=AF.Exp, scale=-1.0), reads=[('den',)], writes=[('den',)])
                        S.add('dve', lambda nq=nq: V.tensor_tensor(out=onr[:, :, 0:nq], in0=PS[3][:, :].rearrange("p (h t) -> p h t", h=4)[:, :, 0:nq],
                                                                   in1=den[:, :, 0:nq], op=ALU.mult),
                              reads=[('ps', 3), ('den',)], writes=[('onr',)])
                        tcs = [i for i, (a, m) in enumerate(TCH) if a < t0 + nq and t0 < a + m]
                        for hf in range(2):
                            S.add('dve', lambda g=g, hf=hf, t0=t0, nq=nq: V.tensor_tensor(
                                out=mix[64 * hf:64 * hf + 64, 2 * g:2 * g + 2, t0:t0 + nq],
                                in0=onr[64 * hf:64 * hf + 64, 2 * hf:2 * hf + 2, 0:nq],
                                in1=sga[64 * hf:64 * hf + 64, :, t0:t0 + nq], op=ALU.mult),
                                reads=[('onr',)] + SGA_ALL, writes=[('mix', 2 * g + jj, i) for jj in range(2) for i in tcs])
                        yield 'c'

                    gens = [blk_gen(ab_, i_ % 2) for i_, ab_ in enumerate(ablocks)]
                    next(gens[0])
                    next(gens[0])
                    for i_ in range(len(ablocks)):
                        if i_ + 1 < len(ablocks):
                            next(gens[i_ + 1])
                        next(gens[i_])
                        if i_ + 1 < len(ablocks):
                            next(gens[i_ + 1])
                if not _on('out'):
                    break
                MIX_ALL = [('mix', c, i) for c in range(KC) for i in range(4)]
                for n_ in range(KC):
                    slot = w_next()
                    for i, (t0, n) in enumerate(TCH):
                        b = next_pb()

                        def fo2(slot=slot, t0=t0, n=n, b=b):
                            for c in range(KC):
                                last = PE.matmul(PS[b][:, 0:n], lhsT=wbuf[:, slot, c, :], rhs=mix[:, c, t0:t0 + n], start=(c == 0), stop=(c == KC - 1))
                            return last
                        S.add('pe', fo2, reads=[('w', slot)] + [('mix', c, i) for c in range(KC)], writes=[('ps', b)])
                        S.add('dve', lambda n_=n_, t0=t0, n=n, b=b: V.tensor_tensor(out=hT[:, n_, t0:t0 + n], in0=hT[:, n_, t0:t0 + n], in1=PS[b][:, 0:n], op=ALU.add),
                              reads=[('ps', b), ('h', n_, i)], writes=[('h', n_, i)])
                barrier(extra_writes=[('kd', 'halo'), ('vtk', 0), ('vtks',), ('ckst', 0), ('ckst', 1), ('emg',)])

            if _on('fin'):
                rtk = do_norm(fw, 0)
                barrier(extra_writes=[('sq', 0), ('sq', 1)])
                for ti, (t0, n) in enumerate(TT):
                    sl = ti % 2
                    for kq in range(4):
                        b = 5 + (ti * 4 + kq) % 3
                        for j in range(4):
                            k = kq * 4 + j
                            yp = (ti * 16 + k) % 2
                            S.add('dve', lambda k=k, t0=t0, n=n, yp=yp: V.scalar_tensor_tensor(out=ynt[:, yp, 0:n], in0=hT[:, k, t0:t0 + n], scalar=fw[:, k:k + 1],
                                                                                              in1=rt[:, t0:t0 + n], op0=ALU.mult, op1=ALU.mult),
                                  reads=hkeys(k, t0, n) + rtk + [('fw',)], writes=[('ynt', yp)])
                            S.add('pe', lambda j=j, n=n, yp=yp, b=b: PE.transpose(out=PS[b][0:n, j * 128:(j + 1) * 128], in_=ynt[:, yp, 0:n], identity=identF[:, :]),
                                  reads=[('ynt', yp), ('identF',)], writes=[('ps', b)])
                        if kq % 2 == 0:
                            S.add('act', lambda sl=sl, kq=kq, n=n, b=b: A.copy(out=xst[0:n, sl, kq * 512:(kq + 1) * 512], in_=PS[b][0:n, :]),
                                  reads=[('ps', b)], writes=[('xst', sl)])
                        else:
                            S.add('dve', lambda sl=sl, kq=kq, n=n, b=b: V.tensor_copy(out=xst[0:n, sl, kq * 512:(kq + 1) * 512], in_=PS[b][0:n, :]),
                                  reads=[('ps', b)], writes=[('xst', sl)])
                    S.add('sp', lambda sl=sl, t0=t0, n=n: nc.sync.dma_start(out=y_out[t0:t0 + n, :], in_=xst[0:n, sl, :]),
                          reads=[('xst', sl)], dma=('o_y', sl))

        for p_ in range(2):
            run_pass(p_, xin_p[p_], ck_p[p_], cv_p[p_], st_p[p_], y_p[p_], pk_p[p_], pv_p[p_], ps_p[p_], sk_p[p_], sv_p[p_], ss_p[p_], send_p[p_], send_p[0])

        S.emit([('o_y', 0), ('o_y', 1), ('o_pk',), ('o_sk',), ('o_ss', 0), ('o_ss', 1), ('snd',), ('emd',)])
    return nc


def _t5_bucket(dist):
    d = np.maximum(dist, 0)
    df = np.maximum(d, 1).astype(np.float32)
    large = 16 + (np.log(df / np.float32(16)) / np.float32(np.log(128 / 16)) * np.float32(16)).astype(np.int32)
    large = np.minimum(large, 31)
    return np.where(d < 16, d, large)


_NC_CACHE = {}


def kernel(x_prompt, x_sample, cache_k, cache_v, state_h, meta_tokens, w_in, w_out, norm_w,
           final_norm_w, attn_sinks, rel_bias_table, hgrn_lb_logits, hgrn_norm_w, _nl=DEPTH):
    f32 = np.float32
    x_prompt = np.asarray(x_prompt, f32)
    x_sample = np.asarray(x_sample, f32)
    cache_k = np.asarray(cache_k, f32).reshape(DEPTH, 32, 128, 256)
    cache_v = np.asarray(cache_v, f32).reshape(DEPTH, 32, 128, 256)
    state_h = np.asarray(state_h, f32)
    w_in = np.asarray(w_in, f32)
    w_out = np.asarray(w_out, f32)
    rel = np.asarray(rel_bias_table, f32)

    def pk(a, n):
        a = np.asarray(a, f32).reshape(-1, n, 128)
        return np.ascontiguousarray(a.transpose(2, 0, 1).reshape(128, -1))

    nw = pk(norm_w, KC)
    fw = pk(final_norm_w, KC)
    lbl = pk(hgrn_lb_logits, 8)
    hw = pk(hgrn_norm_w, 8)
    perm = np.array([4 * g_ + 2 * (jp_ % 2) + jp_ // 2 for g_ in range(4) for jp_ in range(4)])
    sink = np.ascontiguousarray(np.broadcast_to(np.asarray(attn_sinks, f32)[:, perm].reshape(1, -1), (128, DEPTH * 16)))
    s_ = np.arange(128)[:, None]
    t_ = np.arange(128)[None, :]
    dprev = np.clip(128 + t_ - s_, 0, 127)
    dcur = np.clip(t_ - s_, 0, 127)
    bidx = _t5_bucket(np.arange(128))
    bias_d = rel[bidx]
    biasT = np.empty((16, 128, 256), f32)
    biasT[:, :, 0:128] = bias_d[dprev].transpose(2, 0, 1)
    biasT[:, :, 128:256] = bias_d[dcur].transpose(2, 0, 1)
    mask01 = np.zeros((128, 256), f32)
    mask01[:, 0:128] = (s_ > t_)
    mask01[:, 128:256] = (s_ <= t_)
    tril = np.zeros((128, 64), f32)
    tril[0:64, :] = (np.arange(64)[:, None] <= np.arange(64)[None, :])
    seg01 = np.ones((128, T), f32)
    for s in range(NSEQ):
        seg01[:, TP + 8 * s] = 0.0
    identf = np.eye(128, dtype=f32)

    NCORE = int(os.environ.get('KNCORE', '4'))
    wi = w_in[:_nl] if _nl < DEPTH else w_in
    wo = w_out[:_nl] if _nl < DEPTH else w_out
    if os.environ.get('KSMALLW', '0') == '1':
        wi = np.ascontiguousarray(w_in[:1, :128, :128])
        wo = np.ascontiguousarray(w_out[:1, :128, :128])
    in_maps = []
    for c in range(NCORE):
        full = np.concatenate([np.asarray(meta_tokens, f32), x_prompt[c]], axis=0)
        m = dict(w_in=wi, w_out=wo, nw=nw, fw=fw, sink=sink, biasT=biasT, mask01=mask01, tril=tril,
                 seg01=seg01, lbl=lbl, hw=hw, identf=identf)
        for p in range(2):
            s0 = 8 * c + 4 * p
            m["xin%d" % p] = np.ascontiguousarray(np.concatenate([full[p * TP:(p + 1) * TP], x_sample[s0:s0 + 4].reshape(TS, D)], axis=0))
            m["ck%d" % p] = np.ascontiguousarray(cache_k[:, s0:s0 + 4])
            m["cv%d" % p] = np.ascontiguousarray(cache_v[:, s0:s0 + 4])
            m["st%d" % p] = np.ascontiguousarray(state_h[:, s0:s0 + 4])
        in_maps.append(m)
    if _nl not in _NC_CACHE:
        _NC_CACHE[_nl] = build_nc(_nl)
    nc = _NC_CACHE[_nl]
    res = run_bass_kernel_spmd(nc, in_maps, core_ids=list(range(NCORE)))
    R = res.results
    y_prompt = np.empty((4, 2048, D), f32)
    y_sample = np.empty((32, 8, D), f32)
    pkk = np.empty((DEPTH, 4, 128, 4, 64), f32)
    pvv = np.empty((DEPTH, 4, 128, 4, 64), f32)
    pss = np.empty((DEPTH, 4, 8, 128, 128), f32)
    skk = np.empty((DEPTH, 32, 128, 4, 64), f32)
    svv = np.empty((DEPTH, 32, 128, 4, 64), f32)
    sss = np.empty((DEPTH, 32, 8, 128, 128), f32)
    for c in range(NCORE):
        y0, y1 = R[c]["y0"], R[c]["y1"]
        y_prompt[c, 0:TP - 16] = y0[16:TP]
        y_prompt[c, TP - 16:] = y1[0:TP]
        pkk[:, c] = R[c]["pk1"].reshape(DEPTH, 128, 4, 64)
        pvv[:, c] = R[c]["pv1"].reshape(DEPTH, 128, 4, 64)
        pss[:, c] = R[c]["ps1"]
        for p in range(2):
            s0 = 8 * c + 4 * p
            y_sample[s0:s0 + 4] = R[c]["y%d" % p][TP:].reshape(4, 8, D)
            skk[:, s0:s0 + 4] = R[c]["sk%d" % p].reshape(DEPTH, 4, 128, 4, 64)
            svv[:, s0:s0 + 4] = R[c]["sv%d" % p].reshape(DEPTH, 4, 128, 4, 64)
            sss[:, s0:s0 + 4] = R[c]["ss%d" % p]
    return (y_prompt, y_sample, pkk, pvv, pss, skk, svv, sss)
```
